# Optimizing a Trainium2 kernel written in Bass

```python
import jax, jax.numpy as jnp
from jax import lax
import numpy as np

D_MODEL = 2048
BATCH = 4
SEQ = 2048
DEPTH = 4

N_MIXERS = 3
HEAD_DIM = 128
DIL_GROUPS = ((128, 1), (512, 4), (2048, 16))
N_DIL = len(DIL_GROUPS)
HEADS_PER_GROUP = D_MODEL // HEAD_DIM
ATTN_OUT = HEADS_PER_GROUP * HEAD_DIM
ATTN_QKV = N_DIL * ATTN_OUT
ROPE_DIM = HEAD_DIM // 4
ROPE_THETA = 500000.0
BLOCK = 128
EXPAND = 2
SGU_WIDTH = EXPAND * D_MODEL
SGU_CHUNK = 128
SGU_GROUPS = 16
CONV_WIDTH = EXPAND * D_MODEL
CONV_K = 31
NORM_EPS = 1e-6
N_LAYERS_A = (DEPTH + 2) // 3
N_LAYERS_B = (DEPTH + 1) // 3
N_LAYERS_C = DEPTH // 3

kernel_name = "hybrid_dilated_sgu_conformer_trunk"


def _rms(x):
    xf = x.astype(jnp.float32)
    return xf * lax.rsqrt(jnp.mean(xf * xf, axis=-1, keepdims=True) + NORM_EPS)


def _layernorm(x, g, b):
    xf = x.astype(jnp.float32)
    mu = jnp.mean(xf, axis=-1, keepdims=True)
    xc = xf - mu
    var = jnp.mean(xc * xc, axis=-1, keepdims=True)
    return (xc * lax.rsqrt(var + NORM_EPS) * g + b).astype(x.dtype)


def _rope(x, cos, sin):
    half = ROPE_DIM // 2
    x1 = x[..., :half]
    x2 = x[..., half:ROPE_DIM]
    return jnp.concatenate([x1 * cos - x2 * sin, x2 * cos + x1 * sin, x[..., ROPE_DIM:]], axis=-1)


def _banded_attention(q, k, v, n_back):
    L = q.shape[-2]
    nb = -(-L // BLOCK)
    pad = nb * BLOCK - L
    padcfg = [(0, 0)] * (q.ndim - 2) + [(0, pad), (0, 0)]
    q, k, v = (jnp.pad(t, padcfg) for t in (q, k, v))
    lead = q.shape[:-2]
    qb = q.reshape(*lead, nb, BLOCK, HEAD_DIM)

    def with_prev(t):
        tb = t.reshape(*lead, nb, BLOCK, HEAD_DIM)
        prev = jnp.concatenate([jnp.zeros_like(tb[..., :1, :, :]), tb[..., :-1, :, :]], axis=-3)
        return jnp.concatenate([prev, tb], axis=-2)

    kk, vv = with_prev(k), with_prev(v)
    s = jnp.einsum('...nqd,...nkd->...nqk', qb, kk,
                   preferred_element_type=jnp.float32) * (HEAD_DIM ** -0.5)
    qi = jnp.arange(BLOCK)[:, None] + BLOCK
    ki = jnp.arange(2 * BLOCK)[None, :]
    dist = qi - ki
    band = (dist >= 0) & (dist <= n_back)
    first = (jnp.arange(nb) == 0)[:, None, None] & (ki < BLOCK)[None]
    valid = band[None] & ~first
    s = jnp.where(valid, s, -jnp.inf)
    lse = jax.nn.logsumexp(s, axis=-1)
    p = jnp.exp(s - lse[..., None])
    o = jnp.einsum('...nqk,...nkd->...nqd', p.astype(v.dtype), vv)
    o = o.reshape(*lead, nb * BLOCK, HEAD_DIM)[..., :L, :]
    lse = lse.reshape(*lead, nb * BLOCK)[..., :L]
    return o, lse


def _dilated_attention_mixer(h, w_in, q_gain, k_gain, w_out, cos, sin):
    B, S, _ = h.shape
    proj = h @ w_in
    shp = (B, S, N_DIL, HEADS_PER_GROUP, HEAD_DIM)
    q = proj[..., :ATTN_QKV].reshape(shp)
    k = proj[..., ATTN_QKV:2 * ATTN_QKV].reshape(shp)
    v = proj[..., 2 * ATTN_QKV:3 * ATTN_QKV].reshape(shp)
    z = proj[..., 3 * ATTN_QKV:]
    q = _rope(_rms(q) * q_gain, cos, sin).astype(h.dtype)
    k = _rope(_rms(k) * k_gain, cos, sin).astype(h.dtype)
    outs, lses = [], []
    for g, (window, dil) in enumerate(DIL_GROUPS):
        L = S // dil

        def to_residue(t):
            return t[:, :, g].reshape(B, L, dil, HEADS_PER_GROUP, HEAD_DIM).transpose(0, 2, 3, 1, 4)

        o, lse = _banded_attention(to_residue(q), to_residue(k), to_residue(v), window // dil)
        outs.append(o.transpose(0, 3, 1, 2, 4).reshape(B, S, HEADS_PER_GROUP, HEAD_DIM))
        lses.append(lse.transpose(0, 3, 1, 2).reshape(B, S, HEADS_PER_GROUP))
    wts = jax.nn.softmax(jnp.stack(lses, axis=0), axis=0)
    o = jnp.sum(wts[..., None] * jnp.stack(outs, axis=0).astype(jnp.float32), axis=0)
    y = o.reshape(B, S, ATTN_OUT).astype(h.dtype) * jax.nn.silu(z)
    return y @ w_out


def _sgu_mixer(h, w_in, ln_g, ln_b, ws, bs, w_out):
    B, S, _ = h.shape
    E = SGU_WIDTH
    proj = h @ w_in
    u = jax.nn.gelu(proj[..., :E])
    v = _layernorm(jax.nn.gelu(proj[..., E:2 * E]), ln_g, ln_b)
    z = proj[..., 2 * E:]
    nc = S // SGU_CHUNK
    vc = v.reshape(B, nc, SGU_CHUNK, SGU_GROUPS, E // SGU_GROUPS)
    causal = jnp.tril(jnp.ones((SGU_CHUNK, SGU_CHUNK), dtype=bool))
    wm = jnp.where(causal, ws, jnp.zeros_like(ws))
    sv = jnp.einsum('gts,bnsgc->bntgc', wm, vc) + bs.T[:, :, None]
    y = u * sv.reshape(B, S, E) * jax.nn.silu(z)
    return y @ w_out


def _conv_mixer(h, w_in, dw_w, dw_b, ln_g, ln_b, w_out):
    E = CONV_WIDTH
    proj = h @ w_in
    g = proj[..., :E] * jax.nn.sigmoid(proj[..., E:2 * E])
    z = proj[..., 2 * E:]
    g = lax.conv_general_dilated(g, dw_w[:, None, :], window_strides=(1,),
                                 padding=((CONV_K - 1, 0),),
                                 dimension_numbers=('NWC', 'WIO', 'NWC'),
                                 feature_group_count=E) + dw_b
    g = jax.nn.silu(_layernorm(g, ln_g, ln_b))
    return (g * jax.nn.silu(z)) @ w_out


def _normal(k, shape, scale):
    return jax.random.normal(k, shape, jnp.float32) * scale


def setup_inputs(seed: int = 0) -> dict:
    key = jax.random.key(seed)
    ks = jax.random.split(key, 24)
    D = D_MODEL
    x = _normal(ks[0], (BATCH, SEQ, D), 1.0)
    c = _normal(ks[1], (BATCH, D), 1.0)
    offs = jax.random.randint(ks[2], (BATCH, 1), 0, 4096, dtype=jnp.int32)
    positions = offs + jnp.arange(SEQ, dtype=jnp.int32)[None, :]
    ada_w = _normal(ks[3], (DEPTH, D, 3 * D), D ** -0.5)
    ada_b = _normal(ks[4], (DEPTH, 3 * D), 0.01)
    norm_g = 1.0 + _normal(ks[5], (DEPTH, D), 0.02)
    attn_w_in = _normal(ks[6], (N_LAYERS_A, D, 3 * ATTN_QKV + ATTN_OUT), D ** -0.5)
    attn_q_gain = 1.0 + _normal(ks[7], (N_LAYERS_A, HEAD_DIM), 0.02)
    attn_k_gain = 1.0 + _normal(ks[8], (N_LAYERS_A, HEAD_DIM), 0.02)
    attn_w_out = _normal(ks[9], (N_LAYERS_A, ATTN_OUT, D), ATTN_OUT ** -0.5)
    sgu_w_in = _normal(ks[10], (N_LAYERS_B, D, 3 * SGU_WIDTH), D ** -0.5)
    sgu_ln_g = 1.0 + _normal(ks[11], (N_LAYERS_B, SGU_WIDTH), 0.02)
    sgu_ln_b = _normal(ks[12], (N_LAYERS_B, SGU_WIDTH), 0.02)
    sgu_ws = _normal(ks[13], (N_LAYERS_B, SGU_GROUPS, SGU_CHUNK, SGU_CHUNK), SGU_CHUNK ** -0.5)
    sgu_bs = 1.0 + _normal(ks[14], (N_LAYERS_B, SGU_GROUPS, SGU_CHUNK), 0.1)
    sgu_w_out = _normal(ks[15], (N_LAYERS_B, SGU_WIDTH, D), SGU_WIDTH ** -0.5)
    conv_w_in = _normal(ks[16], (N_LAYERS_C, D, 3 * CONV_WIDTH), D ** -0.5)
    conv_dw_w = _normal(ks[17], (N_LAYERS_C, CONV_K, CONV_WIDTH), CONV_K ** -0.5)
    conv_dw_b = _normal(ks[18], (N_LAYERS_C, CONV_WIDTH), 0.02)
    conv_ln_g = 1.0 + _normal(ks[19], (N_LAYERS_C, CONV_WIDTH), 0.02)
    conv_ln_b = _normal(ks[20], (N_LAYERS_C, CONV_WIDTH), 0.02)
    conv_w_out = _normal(ks[21], (N_LAYERS_C, CONV_WIDTH, D), CONV_WIDTH ** -0.5)
    return {"x": x, "c": c, "positions": positions,
            "ada_w": ada_w, "ada_b": ada_b, "norm_g": norm_g,
            "attn_w_in": attn_w_in, "attn_q_gain": attn_q_gain, "attn_k_gain": attn_k_gain,
            "attn_w_out": attn_w_out,
            "sgu_w_in": sgu_w_in, "sgu_ln_g": sgu_ln_g, "sgu_ln_b": sgu_ln_b,
            "sgu_ws": sgu_ws, "sgu_bs": sgu_bs, "sgu_w_out": sgu_w_out,
            "conv_w_in": conv_w_in, "conv_dw_w": conv_dw_w, "conv_dw_b": conv_dw_b,
            "conv_ln_g": conv_ln_g, "conv_ln_b": conv_ln_b, "conv_w_out": conv_w_out}


def reference(x, c, positions, ada_w, ada_b, norm_g,
              attn_w_in, attn_q_gain, attn_k_gain, attn_w_out,
              sgu_w_in, sgu_ln_g, sgu_ln_b, sgu_ws, sgu_bs, sgu_w_out,
              conv_w_in, conv_dw_w, conv_dw_b, conv_ln_g, conv_ln_b, conv_w_out):
    inv_freq = jnp.power(ROPE_THETA, -jnp.arange(0, ROPE_DIM, 2, dtype=jnp.float32) / ROPE_DIM)
    ang = positions.astype(jnp.float32)[..., None] * inv_freq
    cos = jnp.cos(ang)[:, :, None, None, :]
    sin = jnp.sin(ang)[:, :, None, None, :]
    c_act = jax.nn.silu(c)
    for i in range(DEPTH):
        mod = c_act @ ada_w[i] + ada_b[i]
        shift, scale, gate = jnp.split(mod, 3, axis=-1)
        h = (_rms(x) * norm_g[i] * (1.0 + scale[:, None, :]) + shift[:, None, :]).astype(x.dtype)
        kind, j = i % N_MIXERS, i // N_MIXERS
        if kind == 0:
            y = _dilated_attention_mixer(h, attn_w_in[j], attn_q_gain[j], attn_k_gain[j],
                                         attn_w_out[j], cos, sin)
        elif kind == 1:
            y = _sgu_mixer(h, sgu_w_in[j], sgu_ln_g[j], sgu_ln_b[j], sgu_ws[j], sgu_bs[j],
                           sgu_w_out[j])
        else:
            y = _conv_mixer(h, conv_w_in[j], conv_dw_w[j], conv_dw_b[j], conv_ln_g[j],
                            conv_ln_b[j], conv_w_out[j])
        x = x + gate[:, None, :] * y
    return x
```

```python
import numpy as np
import concourse.bass as bass
import concourse.mybir as mybir
from concourse.bass_utils import run_bass_kernel_spmd

F32 = mybir.dt.float32
BF16 = mybir.dt.bfloat16
AF = mybir.ActivationFunctionType
ALU = mybir.AluOpType


class Res:
    __slots__ = ("name", "t", "w", "r")

    def __init__(self, name, t=None):
        self.name = name
        self.t = t
        self.w = None
        self.r = {}


class Sched:
    COMPUTE = ("pe", "act", "dve", "pool")

    def __init__(self, nc, ndma=6):
        self.nc = nc
        self.eng = {"pe": nc.tensor, "act": nc.scalar, "dve": nc.vector,
                    "pool": nc.gpsimd, "sp": nc.sync}
        self.sems = {}
        for k in self.COMPUTE:
            self.sems[("c", k)] = nc.alloc_semaphore(name=f"c_{k}")
        self.ccnt = {k: 0 for k in self.COMPUTE}
        self.ndma = ndma
        self.dval = {}
        self.dnext = {}
        for q in ("sp", "pool", "act"):
            self.dnext[q] = 0
            for i in range(ndma):
                self.sems[("d", q, i)] = nc.alloc_semaphore(name=f"d_{q}_{i}")
                self.dval[("d", q, i)] = 0
        self.seen = {e: {} for e in self.eng}
        self.nwait = 0
        self.nops = 0
        self.trace = {e: [] for e in self.eng}
        self.pending = {e: [] for e in self.eng}

    def sb(self, name, shape, dtype):
        return Res(name, self.nc.alloc_sbuf_tensor(name, shape, dtype))

    def ps(self, name, shape, dtype):
        return Res(name, self.nc.alloc_psum_tensor(name, shape, dtype))

    def res(self, name):
        return Res(name)

    dram_res = res

    def _wait(self, e, ev):
        if ev is None:
            return
        key, val = ev
        if val <= 0:
            return
        if self.seen[e].get(key, 0) >= val:
            return
        if e == "pe" and key == ("c", "pe"):
            return
        self.eng[e].wait_ge(self.sems[key], val)
        self.seen[e][key] = val
        self.nwait += 1
        self.pending[e].append((key, val))

    def _deps(self, e, reads, writes):
        for r in reads:
            self._wait(e, r.w)
        for w in writes:
            self._wait(e, w.w)
            for key, val in w.r.items():
                self._wait(e, (key, val))

    def _mark(self, ev, reads, writes):
        key, val = ev
        for r in reads:
            if r.r.get(key, 0) < val:
                r.r[key] = val
        for w in writes:
            w.w = ev
            w.r = {}

    def op(self, e, fn, reads=(), writes=()):
        self._deps(e, reads, writes)
        inst = fn()
        self.ccnt[e] += 1
        key = ("c", e)
        inst.then_inc(self.sems[key], 1)
        ev = (key, self.ccnt[e])
        self._mark(ev, reads, writes)
        self.nops += 1
        self.trace[e].append((self.pending[e], (key, 1)))
        self.pending[e] = []
        return ev

    def _dma_like(self, q, fn, reads, writes):
        slot = self.dnext[q]
        self.dnext[q] = (slot + 1) % self.ndma
        key = ("d", q, slot)
        self._wait(q, (key, self.dval[key]))
        self._deps(q, reads, writes)
        inst = fn()
        inst.then_inc(self.sems[key], 16)
        self.dval[key] += 16
        ev = (key, self.dval[key])
        self._mark(ev, reads, writes)
        self.nops += 1
        self.trace[q].append((self.pending[q], (key, 16)))
        self.pending[q] = []
        return ev

    def dma(self, q, out, in_, reads=(), writes=()):
        return self._dma_like(q, lambda: self.eng[q].dma_start(out=out, in_=in_), reads, writes)

    def collective(self, fn, reads=(), writes=()):
        return self._dma_like("pool", fn, reads, writes)

    def barrier(self):
        for e in ("pe", "act", "dve", "sp"):
            for k in ("pe", "act", "dve"):
                self._wait(e, (("c", k), self.ccnt[k]))
            for q in ("sp", "act"):
                for i in range(self.ndma):
                    key = ("d", q, i)
                    self._wait(e, (key, self.dval[key]))

    def simulate(self):
        for e in self.eng:
            if self.pending[e]:
                self.trace[e].append((self.pending[e], None))
                self.pending[e] = []
        cnt = {k: 0 for k in self.sems}
        ptr = {e: 0 for e in self.eng}
        progress = True
        while progress:
            progress = False
            for e in self.eng:
                tr = self.trace[e]
                while ptr[e] < len(tr):
                    waits, inc = tr[ptr[e]]
                    if any(cnt[k] < v for k, v in waits):
                        break
                    if inc is not None:
                        cnt[inc[0]] += inc[1]
                    ptr[e] += 1
                    progress = True
        stuck = {e: (ptr[e], len(self.trace[e]), self.trace[e][ptr[e]][0]) for e in self.eng if ptr[e] < len(self.trace[e])}
        return stuck, cnt

    def finish(self):
        for key, val in self.dval.items():
            self._wait("sp", (key, val))
        for k in self.COMPUTE:
            self._wait("sp", (("c", k), self.ccnt[k]))


D = 2048
T = 2048
NB = 4
NCORES = 4
REAL_CORES = (0, 1, 2, 3)
KC = 16
TC = 4
E = 4096
NEG = -30000.0
EPS = 1e-6
SCALE = 128.0 ** -0.5
TWO_PI_HI = 6.28125
TWO_PI_LO = 2.0 * np.pi - 6.28125
DILS = (1, 4, 16)


def _consts():
    ident = np.eye(128, dtype=np.float32)
    prot = np.zeros((128, 128), np.float32)
    for p in range(16):
        prot[p + 16, p] = -1.0
        prot[p, p + 16] = 1.0
    j = np.arange(128)[:, None]
    i = np.arange(128)[None, :]
    maskb = np.concatenate([np.where(j <= i, 0.0, NEG), np.where(j >= i, 0.0, NEG)], 1).astype(np.float32)
    inv_freq = (500000.0 ** (-np.arange(0, 32, 2, dtype=np.float32) / 32.0)).astype(np.float32)
    fcol = np.zeros((128, 1), np.float32)
    fcol[:32, 0] = np.concatenate([inv_freq, inv_freq])
    tril_st = (j <= i).astype(np.float32)
    cmat = np.concatenate([ident, prot, maskb, tril_st], 1)
    return cmat, fcol


class Builder:
    def __init__(self, nlayers=4):
        self.nlayers = nlayers
        nc = self.nc = bass.Bass("TRN2", target_bir_lowering=False)
        S = self.S = Sched(nc)
        dt = nc.dram_tensor
        def inp(name, shape, dtype=F32):
            return dt(name, list(shape), dtype, kind="ExternalInput").ap()
        self.xT = inp("xT", [D, T])
        self.cT = inp("cT", [128, KC])
        self.pos = inp("pos", [128, T], mybir.dt.int32)
        self.ada_w = inp("ada_w", [4, D, 3 * D])
        self.ada_bT = inp("ada_bT", [4, 128, 48])
        self.norm_gT = inp("norm_gT", [4, 128, KC])
        self.attn_w_in = inp("attn_w_in", [2, D, 20480])
        self.attn_qg = inp("attn_qg", [2, 128, 1])
        self.attn_kg = inp("attn_kg", [2, 128, 1])
        self.attn_w_out = inp("attn_w_out", [2, D, D])
        self.sgu_w_in = inp("sgu_w_in", [1, D, 3 * E])
        self.sgu_ln_g = inp("sgu_ln_g", [128, E])
        self.sgu_ln_b = inp("sgu_ln_b", [1, E])
        self.sgu_wsT = inp("sgu_wsT", [128, 16, 128])
        self.sgu_bs2 = inp("sgu_bs2", [2, 16 * 128])
        self.sgu_w_out = inp("sgu_w_out", [1, E, D])
        self.conv_w_in = inp("conv_w_in", [1, D, 3 * E])
        self.conv_w_out = inp("conv_w_out", [1, E, D])
        self.conv_dwT = inp("conv_dwT", [128, 32, 31])
        self.conv_vecT = inp("conv_vecT", [128, 3, 32])
        self.cmat = inp("cmat", [128, 640])
        self.fcol_in = inp("fcol", [128, 1])
        self.outT = dt("outT", [D, T], F32, kind="ExternalOutput").ap()
        self.xs = dt("xs", [D, T], F32, kind="Internal").ap()
        self.ysc = dt("ysc", [E, T], BF16, kind="Internal").ap()
        self.gsc = dt("gsc", [T, E], BF16, kind="Internal").ap()
        self.g2sc = dt("g2sc", [E, T], BF16, kind="Internal").ap()
        self.xs_res = [[S.res(f"xs{c}_{t}") for t in range(TC)] for c in range(KC)]
        self.ysc_res = [S.res(f"ysc{c}") for c in range(32)]
        self.gsc_res = [S.res(f"gsc{c}") for c in range(16)]
        self.g2_res = [S.res(f"g2sc{c}") for c in range(32)]
        self.alloc()

    def alloc(self):
        S = self.S
        self.hT = [S.sb(f"hT{kc}", [128, T], BF16) for kc in range(KC)]
        self.NW = 8
        self.wt = [S.sb(f"wt{i}", [128, KC, 128], BF16) for i in range(self.NW)]
        self.wplan = []
        self.wissued = 0
        self.wused = 0
        self.dry = False
        self.pp = [S.ps(f"pp{i}", [128, 512], F32) for i in range(3)]
        self.pa = S.ps("pa", [128, 512], F32)
        self.pb = S.ps("pb", [128, 512], F32)
        self.psx = [S.ps(f"psx{i}", [128, 512], F32) for i in range(2)]
        self.pol = S.ps("pol", [128, 512], F32)
        self.ppi = 0
        self.cb = S.sb("cb", [128, 640], BF16)
        self.ident = self.cb.t[:, 0:128]
        self.prot = self.cb.t[:, 128:256]
        self.maskb = self.cb.t[:, 256:512]
        self.tril = self.cb.t[:, 512:640]
        self.onesD = S.sb("onesD", [128, 128], BF16)
        self.onesH = S.sb("onesH", [128, 128], BF16)
        self.ones1 = S.sb("ones1", [128, 128], BF16)
        self.onesE = S.sb("onesE", [128, 128], BF16)
        self.fcol = S.sb("fcolsb", [128, 1], F32)
        self.cosT = S.sb("cosT", [128, T], F32)
        self.sinT = S.sb("sinT", [128, T], F32)
        self.cact = S.sb("cact", [128, KC], BF16)
        self.mod = S.sb("mod", [128, 48], F32)
        self.modn = S.sb("modn", [128, 48], F32)
        self.avec = S.sb("avec", [128, KC], F32)
        self.adab = S.sb("adab", [128, 48], F32)
        self.ng = S.sb("ng", [128, KC], F32)
        self.gain = S.sb("gain", [128, 2], F32)
        self.epsb = S.sb("epsb", [128, 1], F32)
        self.rstdn = [S.sb(f"rstdn{i}", [128, 512], F32) for i in range(2)]
        self.f32s = [S.sb(f"f32s{i}", [128, 512], F32) for i in range(6)]
        self.f32i = 0
        self.bf16s = [S.sb(f"bf16s{i}", [128, 512], BF16) for i in range(4)]
        self.bf16i = 0
        self.ARENA_F32 = 16896
        self.arena = S.sb("arena", [128, self.ARENA_F32], F32)
        self.aoff = 0

    def a_reset(self):
        self.aoff = 0

    def a_f32(self, name, n):
        ap = self.arena.t[:, self.aoff:self.aoff + n]
        self.aoff += n
        assert self.aoff <= self.ARENA_F32, (name, self.aoff)
        return Res(name, ap)

    def a_bf16(self, name, n):
        assert n % 2 == 0
        ap = self.arena.t[:, self.aoff:self.aoff + n // 2].bitcast(BF16)
        self.aoff += n // 2
        assert self.aoff <= self.ARENA_F32, (name, self.aoff)
        return Res(name, ap)

    def a_i32(self, name, n):
        ap = self.arena.t[:, self.aoff:self.aoff + n].bitcast(mybir.dt.int32)
        self.aoff += n
        assert self.aoff <= self.ARENA_F32, (name, self.aoff)
        return Res(name, ap)

    def f32(self):
        r = self.f32s[self.f32i % len(self.f32s)]
        self.f32i += 1
        return r

    def b16(self):
        r = self.bf16s[self.bf16i % len(self.bf16s)]
        self.bf16i += 1
        return r

    def nextpp(self):
        r = self.pp[self.ppi % 3]
        self.ppi += 1
        return r

    def load_w(self, wap):
        if self.dry:
            self.wplan.append(wap)
            return self.wt[0]
        u = self.wused
        self.wused += 1
        while self.wissued < len(self.wplan) and self.wissued < u + self.NW - 1:
            i = self.wissued
            t = self.wt[i % self.NW]
            self.S.dma("pool", t.t[:], self.wplan[i].rearrange("(kc p) n -> p kc n", p=128), writes=[t])
            self.wissued += 1
        return self.wt[u % self.NW]

    def proj_fm(self, wt, c0, src, tc, n=512, pp=None):
        nc, S = self.nc, self.S
        if pp is None:
            pp = self.nextpp()

        def fn():
            ins = None
            for kc in range(KC):
                ins = nc.tensor.matmul(pp.t[:, 0:n], wt.t[:, kc, c0:c0 + 128],
                                       src[kc].t[:, tc * n:(tc + 1) * n],
                                       start=(kc == 0), stop=(kc == KC - 1))
            return ins
        S.op("pe", fn, reads=[wt] + list(src), writes=[pp])
        return pp

    def setup(self):
        nc, S = self.nc, self.S
        S.dma("pool", self.cb.t[:], self.cmat, writes=[self.cb])
        S.dma("sp", self.fcol.t[:], self.fcol_in, writes=[self.fcol])
        for t, v in ((self.onesD, 1.0 / D), (self.onesH, 1.0 / 128), (self.ones1, 1.0), (self.onesE, 1.0 / E)):
            S.op("dve", lambda t=t, v=v: nc.vector.memset(t.t[:], v), writes=[t])
        S.op("dve", lambda: nc.vector.memset(self.epsb.t[:], EPS), writes=[self.epsb])
        c32 = self.f32()
        S.dma("sp", c32.t[:, 0:KC], self.cT, writes=[c32])
        S.op("act", lambda: nc.scalar.activation(self.cact.t[:], c32.t[:, 0:KC], AF.Silu), reads=[c32], writes=[self.cact])
        self.a_reset()
        posi = self.a_i32("posi", T)
        S.dma("sp", posi.t[:], self.pos, writes=[posi])
        ang, ki, kf, r = self.a_f32("ang", T), self.a_i32("ki", T), self.a_f32("kf", T), self.a_f32("r", T)
        S.op("dve", lambda: nc.vector.tensor_copy(ang.t[:], posi.t[:]), reads=[posi], writes=[ang])
        S.op("dve", lambda: nc.vector.tensor_scalar(ang.t[:], ang.t[:], self.fcol.t[:, 0:1], None, ALU.mult), reads=[ang, self.fcol], writes=[ang])
        for tab, shift in ((self.sinT, 0.0), (self.cosT, np.pi / 2)):
            S.op("dve", lambda: nc.vector.tensor_scalar(r.t[:], ang.t[:], float(shift), None, ALU.add), reads=[ang], writes=[r])
            S.op("dve", lambda: nc.vector.tensor_scalar(ki.t[:], r.t[:], float(1.0 / (2 * np.pi)), None, ALU.mult), reads=[r], writes=[ki])
            S.op("dve", lambda: nc.vector.tensor_copy(kf.t[:], ki.t[:]), reads=[ki], writes=[kf])
            S.op("dve", lambda: nc.vector.scalar_tensor_tensor(r.t[:], kf.t[:], -TWO_PI_HI, r.t[:], ALU.mult, ALU.add), reads=[kf, r], writes=[r])
            S.op("dve", lambda: nc.vector.scalar_tensor_tensor(r.t[:], kf.t[:], -float(TWO_PI_LO), r.t[:], ALU.mult, ALU.add), reads=[kf, r], writes=[r])
            S.op("dve", lambda: nc.vector.tensor_scalar(kf.t[:], r.t[:], float(np.pi), None, ALU.is_gt), reads=[r], writes=[kf])
            S.op("dve", lambda: nc.vector.scalar_tensor_tensor(r.t[:], kf.t[:], -float(2 * np.pi), r.t[:], ALU.mult, ALU.add), reads=[kf, r], writes=[r])
            S.op("dve", lambda: nc.vector.tensor_scalar(kf.t[:], r.t[:], -float(np.pi), None, ALU.is_lt), reads=[r], writes=[kf])
            S.op("dve", lambda: nc.vector.scalar_tensor_tensor(r.t[:], kf.t[:], float(2 * np.pi), r.t[:], ALU.mult, ALU.add), reads=[kf, r], writes=[r])
            S.op("dve", lambda: nc.vector.tensor_scalar(r.t[:], r.t[:], -3.1415925, 3.1415925, ALU.max, ALU.min), reads=[r], writes=[r])
            S.op("act", lambda tab=tab: nc.scalar.activation(tab.t[:], r.t[:], AF.Sin), reads=[r], writes=[tab])

    def ada_begin(self, li):
        self.ada_li = li
        self.ada_j = 0

    def ada_step(self, n=1):
        nc, S = self.nc, self.S
        li = self.ada_li
        for _ in range(n):
            if li is None or li >= self.nlayers or self.ada_j >= 48:
                return
            j = self.ada_j
            self.ada_j += 1
            wt = self.load_w(self.ada_w[li, :, j * 128:(j + 1) * 128])
            pm = self.pa

            def fn():
                ins = None
                for kc in range(KC):
                    ins = nc.tensor.matmul(pm.t[:, 0:1], wt.t[:, kc, :], self.cact.t[:, kc:kc + 1], start=(kc == 0), stop=(kc == KC - 1))
                return ins
            S.op("pe", fn, reads=[wt, self.cact], writes=[pm])
            S.op("dve", lambda: nc.vector.tensor_copy(self.modn.t[:, j:j + 1], pm.t[:, 0:1]), reads=[pm], writes=[self.modn])

    def ada_finish(self, li):
        nc, S = self.nc, self.S
        assert self.ada_li == li
        self.ada_step(48)
        S.dma("sp", self.adab.t[:], self.ada_bT[li], writes=[self.adab])
        S.dma("sp", self.ng.t[:], self.norm_gT[li], writes=[self.ng])
        S.op("dve", lambda: nc.vector.tensor_tensor(self.mod.t[:], self.modn.t[:], self.adab.t[:], ALU.add),
             reads=[self.modn, self.adab], writes=[self.mod])
        S.op("dve", lambda: nc.vector.scalar_tensor_tensor(self.avec.t[:], self.mod.t[:, 16:32], 1.0, self.ng.t[:], ALU.add, ALU.mult),
             reads=[self.mod, self.ng], writes=[self.avec])
        self.ada_begin(li + 1)

    def norm(self, li):
        nc, S = self.nc, self.S
        src = self.xT if li == 0 else self.xs
        self.a_reset()
        xch = [[self.a_f32(f"xch{i}_{kc}", 512) for kc in range(KC)] for i in range(2)]
        for tc in range(TC):
            pm = self.pa
            xc = xch[tc % 2]
            for kc in range(KC):
                xb = xc[kc]
                S.dma("sp", xb.t[:], src[kc * 128:(kc + 1) * 128, tc * 512:(tc + 1) * 512],
                      reads=[self.xs_res[kc][tc]], writes=[xb])
                sq = self.b16()
                S.op("act", lambda xb=xb, sq=sq: nc.scalar.activation(sq.t[:], xb.t[:], AF.Square), reads=[xb], writes=[sq])
                S.op("pe", lambda sq=sq, kc=kc: nc.tensor.matmul(pm.t[:], self.onesD.t[:], sq.t[:], start=(kc == 0), stop=(kc == KC - 1)),
                     reads=[sq, self.onesD], writes=[pm])
            rstd = self.rstd_from(pm, out=self.rstdn[tc % 2])
            for kc in range(KC):
                xb = xc[kc]
                S.op("dve", lambda xb=xb, kc=kc: nc.vector.scalar_tensor_tensor(xb.t[:], xb.t[:], self.avec.t[:, kc:kc + 1], rstd.t[:], ALU.mult, ALU.mult),
                     reads=[xb, self.avec, rstd], writes=[xb])
                S.op("act", lambda xb=xb, kc=kc, tc=tc: nc.scalar.activation(self.hT[kc].t[:, tc * 512:(tc + 1) * 512], xb.t[:], AF.Identity,
                                                                       bias=self.mod.t[:, kc:kc + 1], scale=1.0),
                     reads=[xb, self.mod], writes=[self.hT[kc]])
        self.S.barrier()

    def rstd_from(self, pm, n=512, out=None):
        nc, S = self.nc, self.S
        r = out if out is not None else self.f32()
        S.op("act", lambda: nc.scalar.activation(r.t[:, 0:n], pm.t[:, 0:n], AF.Ln, bias=self.epsb.t[:, 0:1], scale=1.0), reads=[pm, self.epsb], writes=[r])
        S.op("act", lambda: nc.scalar.activation(r.t[:, 0:n], r.t[:, 0:n], AF.Exp, scale=-0.5), reads=[r], writes=[r])
        return r

    def perm_out(self, buf, dil, tc):
        if dil == 1:
            return buf.t[:, tc * 512:(tc + 1) * 512]
        L = T // dil
        n = 512 // dil
        return buf.t[:, :].rearrange("p (r m) -> p r m", r=dil)[:, :, tc * n:(tc + 1) * n]

    def perm_in(self, ap512, dil):
        if dil == 1:
            return ap512
        return ap512.rearrange("p (m r) -> p r m", r=dil)

    def nat_ap(self, buf, dil, b4):
        if dil == 1:
            return buf.t[:, b4 * 512:(b4 + 1) * 512]
        if dil == 4:
            return buf.t[:, :].rearrange("p (m r) -> p r m", r=4)[:, b4, :]
        return buf.t[:, :].rearrange("p (m r) -> p r m", r=16)[:, 4 * b4:4 * b4 + 4, :]

    def blk_view(self, ps, dil):
        if dil == 16:
            return ps.t[:, :].rearrange("p (r m) -> p r m", r=4)
        return ps.t[:, :]

    def nat_ap2(self, buf, dil, b0):
        if dil == 1:
            return buf.t[:, b0 * 128:b0 * 128 + 256]
        if dil == 4:
            r, m0 = b0 // 4, (b0 % 4) * 128
            return buf.t[:, :].rearrange("p (m r) -> p r m", r=4)[:, r, m0:m0 + 256]
        return buf.t[:, :].rearrange("p (m r) -> p r m", r=16)[:, b0:b0 + 2, :]

    def blk_view2(self, ap256, dil):
        if dil == 16:
            return ap256.rearrange("p (r m) -> p r m", r=2)
        return ap256

    def attn_layer(self, li, j):
        nc, S = self.nc, self.S
        w_in = self.attn_w_in[j]
        S.dma("sp", self.gain.t[:, 0:1], self.attn_qg[j], writes=[self.gain])
        S.dma("sp", self.gain.t[:, 1:2], self.attn_kg[j], writes=[self.gain])
        self.a_reset()
        qf = [self.a_bf16(f"qf{i}", T) for i in range(2)]
        kf = [self.a_bf16(f"kf{i}", T) for i in range(2)]
        vtm = [self.a_bf16(f"vtm{i}", T) for i in range(2)]
        zs = [self.a_bf16(f"zs{i}", T) for i in range(2)]
        vT = self.a_bf16("vT", T)
        ybuf = self.a_bf16("ybuf", T)
        oacc, lacc = self.a_f32("oacc", T), self.a_f32("lacc", T)
        pT = [self.a_bf16(f"pT{i}", 256) for i in range(6)]
        kgp = [self.a_bf16(f"kg{i}", 512) for i in range(3)]
        pipe = Pipe()
        units = [(h, g) for h in range(16) for g in range(3)]
        jobc = [0]

        def a_jobs(u):
            h, g = units[u]
            dil = DILS[g]
            par = u % 2
            jobs = []

            def mk(name, base, tc, shared):
                jid = jobc[0]
                jobc[0] += 1
                pp = self.pp[jid % 3]
                kg = kgp[jid % 3]
                c = base + (g * 2048 if name != "z" else 0) + h * 128

                def s0():
                    if tc == 0:
                        shared["wt"] = self.load_w(w_in[:, c:c + 128])
                    self.proj_fm(shared["wt"], 0, self.hT, tc, pp=pp)
                if name in ("k", "q"):
                    gcol = 1 if name == "k" else 0
                    dst = kf[par] if name == "k" else qf[par]

                    def s1():
                        sq = self.b16()
                        S.op("act", lambda: nc.scalar.activation(sq.t[:], pp.t[:], AF.Square), reads=[pp], writes=[sq])
                        S.op("pe", lambda: nc.tensor.matmul(self.pa.t[:], self.onesH.t[:], sq.t[:], start=True, stop=True),
                             reads=[sq, self.onesH], writes=[self.pa])
                        rstd = self.rstd_from(self.pa)
                        S.op("dve", lambda: nc.vector.scalar_tensor_tensor(kg.t[:], pp.t[:], self.gain.t[:, gcol:gcol + 1], rstd.t[:], ALU.mult, ALU.mult),
                             reads=[pp, self.gain, rstd], writes=[kg])

                    def s2():
                        S.op("pe", lambda: nc.tensor.matmul(self.pb.t[:], self.prot, kg.t[:], start=True, stop=True),
                             reads=[kg, self.cb], writes=[self.pb])
                        t1 = self.f32()
                        t2 = self.f32()
                        S.op("dve", lambda: nc.vector.tensor_tensor(t1.t[:], kg.t[:], self.cosT.t[:, tc * 512:(tc + 1) * 512], ALU.mult),
                             reads=[kg, self.cosT], writes=[t1])
                        S.op("dve", lambda: nc.vector.tensor_tensor(t2.t[:], self.pb.t[:], self.sinT.t[:, tc * 512:(tc + 1) * 512], ALU.mult),
                             reads=[self.pb, self.sinT], writes=[t2])
                        S.op("dve", lambda: nc.vector.tensor_tensor(self.perm_out(dst, dil, tc), self.perm_in(t1.t[:, :], dil), self.perm_in(t2.t[:, :], dil), ALU.add),
                             reads=[t1, t2], writes=[dst])
                    return [s0, s1, s2]
                if name == "v":
                    def s1():
                        S.op("act", lambda: nc.scalar.activation(self.perm_out(vT, dil, tc), self.perm_in(pp.t[:, :], dil), AF.Copy),
                             reads=[pp], writes=[vT])

                    def s2():
                        for b4 in range(4):
                            ps = self.pa if b4 % 2 == 0 else self.pb

                            def fn():
                                ins = None
                                for q in range(4):
                                    b = b4 * 4 + q
                                    ins = nc.tensor.matmul(ps.t[:, q * 128:(q + 1) * 128], vT.t[:, b * 128:(b + 1) * 128], self.ident, start=True, stop=True)
                                return ins
                            S.op("pe", fn, reads=[vT, self.cb], writes=[ps])
                            S.op("act", lambda: nc.scalar.activation(vtm[par].t[:, b4 * 512:(b4 + 1) * 512], ps.t[:, :], AF.Copy), reads=[ps], writes=[vtm[par]])
                    return [s0, s1, s2 if tc == 3 else None]
                def s1z():
                    S.op("act", lambda: nc.scalar.activation(zs[h % 2].t[:, tc * 512:(tc + 1) * 512], pp.t[:, :], AF.Silu), reads=[pp], writes=[zs[h % 2]])
                return [s0, s1z]

            for name, base in (("k", 6144), ("q", 0), ("v", 12288)) + ((("z", 18432),) if g == 2 else ()):
                shared = {}
                for tc in range(TC):
                    jobs.append(mk(name, base, tc, shared))
            return jobs

        def b_steps(u):
            h, g = units[u]
            dil = DILS[g]
            nb = 16 // dil
            par = u % 2
            kfin, qfin, vt = kf[par], qf[par], vtm[par]

            def score(b):
                jj = b % nb
                n = 256 if jj + 1 < nb else 128
                ps = self.psx[b % 2]
                p = pT[b % 6]

                def fn():
                    nc.tensor.matmul(ps.t[:, 0:n], self.ident, self.maskb[:, 0:n], start=True, stop=False)
                    return nc.tensor.matmul(ps.t[:, 0:n], kfin.t[:, b * 128:(b + 1) * 128], qfin.t[:, b * 128:b * 128 + n], start=False, stop=True)
                S.op("pe", fn, reads=[self.cb, kfin, qfin], writes=[ps])
                S.op("act", lambda: nc.scalar.activation(p.t[:, 0:n], ps.t[:, 0:n], AF.Exp, scale=SCALE), reads=[ps], writes=[p])

            def pv(b):
                jj = b % nb
                q2 = b % 2
                p = pT[b % 6]
                pprev = pT[(b - 1) % 6]
                for which in (0, 1):
                    col = which * 256 + q2 * 128

                    def fn2():
                        first = True
                        if jj > 0:
                            l = vt.t[:, (b - 1) * 128:b * 128] if which == 0 else self.ones1.t[:, :]
                            nc.tensor.matmul(self.pol.t[:, col:col + 128], l, pprev.t[:, 128:256], start=True, stop=False)
                            first = False
                        l = vt.t[:, b * 128:(b + 1) * 128] if which == 0 else self.ones1.t[:, :]
                        return nc.tensor.matmul(self.pol.t[:, col:col + 128], l, p.t[:, 0:128], start=first, stop=True)
                    S.op("pe", fn2, reads=[vt, self.ones1, p] + ([pprev] if jj > 0 else []), writes=[self.pol])
                if q2 == 1:
                    b0 = b - 1
                    for which, acc in ((0, oacc), (1, lacc)):
                        dst_ap = self.nat_ap2(acc, dil, b0)
                        src_ap = self.blk_view2(self.pol.t[:, which * 256:(which + 1) * 256], dil)
                        if g == 0:
                            S.op("act", lambda: nc.scalar.activation(dst_ap, src_ap, AF.Copy), reads=[self.pol], writes=[acc])
                        else:
                            S.op("dve", lambda: nc.vector.tensor_tensor(dst_ap, dst_ap, src_ap, ALU.add), reads=[self.pol, acc], writes=[acc])

            def mkstep(k):
                def st():
                    for b in (2 * k, 2 * k + 1):
                        if b < 16:
                            score(b)
                    for b in (2 * k - 2, 2 * k - 1):
                        if 0 <= b < 16:
                            pv(b)
                return st
            steps = [mkstep(k) for k in range(9)]
            if g == 2:
                def comb():
                    S.op("dve", lambda: nc.vector.reciprocal(lacc.t[:, :], lacc.t[:, :]), reads=[lacc], writes=[lacc])
                    S.op("dve", lambda: nc.vector.tensor_tensor(oacc.t[:, :], oacc.t[:, :], lacc.t[:, :], ALU.mult), reads=[oacc, lacc], writes=[oacc])
                    S.op("dve", lambda: nc.vector.tensor_tensor(ybuf.t[:, :], oacc.t[:, :], zs[h % 2].t[:, :], ALU.mult), reads=[oacc, zs[h % 2]], writes=[ybuf])
                    S.dma("sp", self.ysc[h * 128:(h + 1) * 128, :], ybuf.t[:, :], reads=[ybuf], writes=[self.ysc_res[h]])
                steps.append(comb)
            return steps

        NU = len(units)
        LAG = 3
        for w in range(NU + 1):
            aj = a_jobs(w) if w < NU else []
            bs = b_steps(w - 1) if w >= 1 else []
            n = max(len(aj), (LAG + len(bs)) if bs else 0)
            self.ada_step(1)
            for i in range(n):
                pipe.push(aj[i] if i < len(aj) else [])
                if bs and LAG <= i < LAG + len(bs):
                    bs[i - LAG]()
        pipe.flush()
        self.out_proj(li, self.attn_w_out[j], 1)

    def gelu_from(self, pp, out_ap, n=512):
        nc, S = self.nc, self.S
        a = self.f32()
        S.op("act", lambda: nc.scalar.activation(a.t[:, 0:n], pp.t[:, 0:n], AF.Square), reads=[pp], writes=[a])
        S.op("dve", lambda: nc.vector.tensor_scalar(a.t[:, 0:n], a.t[:, 0:n], 0.044715, 1.0, ALU.mult, ALU.add), reads=[a], writes=[a])
        S.op("dve", lambda: nc.vector.tensor_tensor(a.t[:, 0:n], a.t[:, 0:n], pp.t[:, 0:n], ALU.mult), reads=[a, pp], writes=[a])
        S.op("act", lambda: nc.scalar.activation(a.t[:, 0:n], a.t[:, 0:n], AF.Sigmoid, scale=1.5957691216057308), reads=[a], writes=[a])
        return a

    def sgu_layer(self, li, j):
        nc, S = self.nc, self.S
        w_in = self.sgu_w_in[j]
        self.a_reset()
        gtiles = [self.a_bf16(f"gtile{i}", 16 * 512) for i in range(2)]
        lngs = [self.a_f32(f"lng{i}", 512) for i in range(2)]
        wmT = self.a_bf16("wmT", 16 * 128)
        L2 = self.a_bf16("L2", E)
        RB = self.a_bf16("RB", 16 * 128)
        ybuf = self.a_bf16("ybuf", T)
        ssum = self.a_f32("ssum", 512)
        ssq = self.a_f32("ssq", 512)
        st = self.a_f32("st", 64)
        gstage = [self.a_bf16(f"gstage{i}", 512) for i in range(2)]
        junk = self.a_bf16("junk", 128)
        for q in range(4):
            w32 = self.f32()
            S.dma("sp", w32.t[:, :], self.sgu_wsT[:, q * 4:(q + 1) * 4, :].rearrange("p g t -> p (g t)"), writes=[w32])
            for gg in range(4):
                g = q * 4 + gg
                S.op("dve", lambda: nc.vector.tensor_tensor(wmT.t[:, g * 128:(g + 1) * 128], w32.t[:, gg * 128:(gg + 1) * 128], self.tril, ALU.mult),
                     reads=[w32, self.cb], writes=[wmT])
        S.op("dve", lambda: nc.vector.memset(L2.t[0:2, :], 1.0), writes=[L2])
        for q in range(8):
            st32 = self.f32()
            S.dma("sp", st32.t[0:1, :], self.sgu_ln_b[:, q * 512:(q + 1) * 512], writes=[st32])
            S.op("dve", lambda: nc.vector.tensor_copy(L2.t[0:1, q * 512:(q + 1) * 512], st32.t[0:1, :]), reads=[st32], writes=[L2])
        for q in range(4):
            st32 = self.f32()
            S.dma("sp", st32.t[0:2, :], self.sgu_bs2[:, q * 512:(q + 1) * 512], writes=[st32])
            S.op("pe", lambda: nc.tensor.matmul(self.pa.t[0:1, :], self.ones1.t[:, 0:1], wmT.t[:, q * 512:(q + 1) * 512], start=True, stop=True),
                 reads=[wmT, self.ones1], writes=[self.pa])
            S.op("act", lambda: nc.scalar.activation(st32.t[0:1, :], self.pa.t[0:1, :], AF.Copy), reads=[self.pa], writes=[st32])
            S.op("dve", lambda: nc.vector.tensor_copy(RB.t[0:2, q * 512:(q + 1) * 512], st32.t[0:2, :]), reads=[st32], writes=[RB])
        pipe = Pipe()
        gvp = [self.a_bf16(f"gvp{i}", 512) for i in range(3)]
        jobc = [0]

        def mkb1(cbk, tc, shared):
            jid = jobc[0]
            jobc[0] += 1
            pp = self.pp[jid % 3]
            gv = gvp[jid % 3]
            ps = self.psx[jid % 2]
            gs = gstage[jid % 2]

            def s0():
                if tc == 0:
                    self.ada_step(1)
                    shared["wt"] = self.load_w(w_in[:, E + cbk * 128:E + (cbk + 1) * 128])
                self.proj_fm(shared["wt"], 0, self.hT, tc, pp=pp)

            def s1():
                a = self.gelu_from(pp, None)
                S.op("dve", lambda: nc.vector.tensor_tensor(gv.t[:, :], a.t[:, :], pp.t[:, :], ALU.mult), reads=[a, pp], writes=[gv])

            def s2():
                def fn():
                    ins = None
                    for q in range(4):
                        ins = nc.tensor.matmul(ps.t[:, q * 128:(q + 1) * 128], gv.t[:, q * 128:(q + 1) * 128], self.ident, start=True, stop=True)
                    return ins
                S.op("pe", fn, reads=[gv, self.cb], writes=[ps])
                for q in range(4):
                    n = tc * 4 + q
                    col = n * 32 + cbk
                    S.op("act", lambda: nc.scalar.activation(gs.t[:, q * 128:(q + 1) * 128], ps.t[:, q * 128:(q + 1) * 128], AF.Copy, accum_out=ssum.t[:, col:col + 1]),
                         reads=[ps], writes=[gs, ssum])
                    S.op("act", lambda: nc.scalar.activation(junk.t[:, :], ps.t[:, q * 128:(q + 1) * 128], AF.Square, accum_out=ssq.t[:, col:col + 1]),
                         reads=[ps], writes=[junk, ssq])
                dst = self.gsc[tc * 512:(tc + 1) * 512, cbk * 128:(cbk + 1) * 128].rearrange("(q p) c -> p q c", p=128)
                S.dma("sp", dst, gs.t.rearrange("p (q c) -> p q c", q=4), reads=[gs], writes=[self.gsc_res[tc * 4 + q2] for q2 in range(4)])
            return [s0, s1, s2]
        for cbk in range(32):
            shared = {}
            for tc in range(TC):
                pipe.push(mkb1(cbk, tc, shared))
        pipe.flush()
        mean, ex2, rstd, nmr = (st.t[:, k * 16:(k + 1) * 16] for k in range(4))
        S.op("dve", lambda: nc.vector.tensor_reduce(mean, ssum.t.rearrange("p (n c) -> p n c", n=16), mybir.AxisListType.X, ALU.add), reads=[ssum], writes=[st])
        S.op("dve", lambda: nc.vector.tensor_reduce(ex2, ssq.t.rearrange("p (n c) -> p n c", n=16), mybir.AxisListType.X, ALU.add), reads=[ssq], writes=[st])
        S.op("dve", lambda: nc.vector.tensor_scalar(mean, mean, 1.0 / E, None, ALU.mult), reads=[st], writes=[st])
        S.op("dve", lambda: nc.vector.tensor_scalar(ex2, ex2, 1.0 / E, None, ALU.mult), reads=[st], writes=[st])
        S.op("dve", lambda: nc.vector.tensor_tensor(nmr, mean, mean, ALU.mult), reads=[st], writes=[st])
        S.op("dve", lambda: nc.vector.tensor_tensor(ex2, ex2, nmr, ALU.subtract), reads=[st], writes=[st])
        S.op("act", lambda: nc.scalar.activation(rstd, ex2, AF.Ln, bias=self.epsb.t[:, 0:1], scale=1.0), reads=[st, self.epsb], writes=[st])
        S.op("act", lambda: nc.scalar.activation(rstd, rstd, AF.Exp, scale=-0.5), reads=[st], writes=[st])
        S.op("dve", lambda: nc.vector.scalar_tensor_tensor(nmr, mean, -1.0, rstd, ALU.mult, ALU.mult), reads=[st], writes=[st])
        def prep(cg):
            gtile, lng = gtiles[cg % 2], lngs[cg % 2]
            gt3 = gtile.t.rearrange("p (n c) -> p n c", n=16)
            S.dma("sp", gt3, self.gsc[:, cg * 512:(cg + 1) * 512].rearrange("(n p) c -> p n c", p=128), reads=self.gsc_res, writes=[gtile])
            S.dma("sp", lng.t[:, :], self.sgu_ln_g[:, cg * 512:(cg + 1) * 512], writes=[lng])
            for n in range(16):
                S.op("dve", lambda: nc.vector.tensor_scalar(gt3[:, n, :], gt3[:, n, :], st.t[:, 32 + n:33 + n], st.t[:, 48 + n:49 + n], ALU.mult, ALU.add),
                     reads=[gtile, st], writes=[gtile])
                S.op("dve", lambda: nc.vector.tensor_tensor(gt3[:, n, :], gt3[:, n, :], lng.t[:, :], ALU.mult), reads=[gtile, lng], writes=[gtile])
        prep(0)
        for cg in range(8):
            gtile = gtiles[cg % 2]
            gt3 = gtile.t.rearrange("p (n c) -> p n c", n=16)
            for cbl in range(4):
                cbk = cg * 4 + cbl
                g = cbk // 2
                if cbl == 1 and cg + 1 < 8:
                    prep(cg + 1)
                if cbk % 2 == 0:
                    self.ada_step(1)
                wu = self.load_w(w_in[:, cbk * 128:(cbk + 1) * 128])
                wz = self.load_w(w_in[:, 2 * E + cbk * 128:2 * E + (cbk + 1) * 128])
                for tc in range(TC):
                    ppu = self.proj_fm(wu, 0, self.hT, tc)
                    a = self.gelu_from(ppu, None)
                    gu = self.f32()
                    S.op("dve", lambda: nc.vector.tensor_tensor(gu.t[:, :], a.t[:, :], ppu.t[:, :], ALU.mult), reads=[a, ppu], writes=[gu])
                    ppz = self.proj_fm(wz, 0, self.hT, tc)
                    zs = self.f32()
                    S.op("act", lambda: nc.scalar.activation(zs.t[:, :], ppz.t[:, :], AF.Silu), reads=[ppz], writes=[zs])

                    def fn():
                        ins = None
                        for q in range(4):
                            n = tc * 4 + q
                            nc.tensor.matmul(self.pol.t[:, q * 128:(q + 1) * 128], gt3[:, n, cbl * 128:(cbl + 1) * 128], wmT.t[:, g * 128:(g + 1) * 128], start=True, stop=False)
                            ins = nc.tensor.matmul(self.pol.t[:, q * 128:(q + 1) * 128], L2.t[0:2, cbk * 128:(cbk + 1) * 128], RB.t[0:2, g * 128:(g + 1) * 128], start=False, stop=True)
                        return ins
                    S.op("pe", fn, reads=[gtile, wmT, L2, RB], writes=[self.pol])
                    S.op("dve", lambda: nc.vector.tensor_tensor(gu.t[:, :], gu.t[:, :], self.pol.t[:, :], ALU.mult), reads=[gu, self.pol], writes=[gu])
                    S.op("dve", lambda: nc.vector.tensor_tensor(ybuf.t[:, tc * 512:(tc + 1) * 512], gu.t[:, :], zs.t[:, :], ALU.mult), reads=[gu, zs], writes=[ybuf])
                S.dma("sp", self.ysc[cbk * 128:(cbk + 1) * 128, :], ybuf.t[:, :], reads=[ybuf], writes=[self.ysc_res[cbk]])
        self.out_proj(li, self.sgu_w_out[j], 2)

    def conv_layer(self, li, j):
        nc, S = self.nc, self.S
        w_in = self.conv_w_in[j]
        self.a_reset()
        PADW = 32
        gpad = [self.a_bf16(f"gpad{i}", PADW + T) for i in range(2)]
        diag = [self.a_bf16(f"diag{i}", 31 * 128) for i in range(2)]
        dwt = self.a_f32("dwt", 32 * 31)
        vec = self.a_f32("vec", 96)
        ssum = self.a_f32("csum", T)
        ssq = self.a_f32("csq", T)
        g2b = [self.a_bf16(f"g2b{i}", T) for i in range(2)]
        ybuf = self.a_bf16("cybuf", T)
        tmpf = self.a_f32("tmpf", T)
        S.dma("sp", dwt.t[:, :], self.conv_dwT.rearrange("p c k -> p (c k)"), writes=[dwt])
        S.dma("sp", vec.t[:, :], self.conv_vecT.rearrange("p w c -> p (w c)"), writes=[vec])
        for i in range(2):
            S.op("dve", lambda: nc.vector.memset(gpad[i].t[:, 0:PADW], 0.0), writes=[gpad[i]])
        for cbk in range(32):
            self.ada_step(1)
            wa = self.load_w(w_in[:, cbk * 128:(cbk + 1) * 128])
            wb = self.load_w(w_in[:, E + cbk * 128:E + (cbk + 1) * 128])
            gp = gpad[cbk % 2]
            dg = diag[cbk % 2]
            for k in range(31):
                S.op("dve", lambda: nc.vector.tensor_scalar(dg.t[:, k * 128:(k + 1) * 128], self.ident, dwt.t[:, cbk * 31 + k:cbk * 31 + k + 1], None, ALU.mult),
                     reads=[self.cb, dwt], writes=[dg])
            for tc in range(TC):
                ppa = self.proj_fm(wa, 0, self.hT, tc)
                ppb = self.proj_fm(wb, 0, self.hT, tc)
                sg = self.f32()
                S.op("act", lambda: nc.scalar.activation(sg.t[:, :], ppb.t[:, :], AF.Sigmoid), reads=[ppb], writes=[sg])
                S.op("dve", lambda: nc.vector.tensor_tensor(gp.t[:, PADW + tc * 512:PADW + (tc + 1) * 512], sg.t[:, :], ppa.t[:, :], ALU.mult), reads=[sg, ppa], writes=[gp])
            g2 = g2b[cbk % 2]
            for tc in range(TC):
                pc = self.psx[tc % 2]

                def fn():
                    ins = None
                    for k in range(31):
                        o = PADW + tc * 512 + k - 30
                        ins = nc.tensor.matmul(pc.t[:, :], dg.t[:, k * 128:(k + 1) * 128], gp.t[:, o:o + 512], start=(k == 0), stop=(k == 30))
                    return ins
                S.op("pe", fn, reads=[dg, gp], writes=[pc])
                S.op("act", lambda: nc.scalar.activation(g2.t[:, tc * 512:(tc + 1) * 512], pc.t[:, :], AF.Identity, bias=vec.t[:, cbk:cbk + 1], scale=1.0), reads=[pc, vec], writes=[g2])
                sq = self.b16()
                S.op("act", lambda: nc.scalar.activation(sq.t[:, :], g2.t[:, tc * 512:(tc + 1) * 512], AF.Square), reads=[g2], writes=[sq])
                S.op("pe", lambda: nc.tensor.matmul(self.pa.t[:, :], self.onesE.t[:, :], g2.t[:, tc * 512:(tc + 1) * 512], start=True, stop=True), reads=[g2, self.onesE], writes=[self.pa])
                S.op("pe", lambda: nc.tensor.matmul(self.pb.t[:, :], self.onesE.t[:, :], sq.t[:, :], start=True, stop=True), reads=[sq, self.onesE], writes=[self.pb])
                for pacc, acc in ((self.pa, ssum), (self.pb, ssq)):
                    if cbk == 0:
                        S.op("act", lambda: nc.scalar.activation(acc.t[:, tc * 512:(tc + 1) * 512], pacc.t[:, :], AF.Copy), reads=[pacc], writes=[acc])
                    else:
                        S.op("dve", lambda: nc.vector.tensor_tensor(acc.t[:, tc * 512:(tc + 1) * 512], acc.t[:, tc * 512:(tc + 1) * 512], pacc.t[:, :], ALU.add), reads=[pacc, acc], writes=[acc])
            S.dma("sp", self.g2sc[cbk * 128:(cbk + 1) * 128, :], g2.t[:, :], reads=[g2], writes=[self.g2_res[cbk]])
        S.op("dve", lambda: nc.vector.tensor_tensor(tmpf.t[:, :], ssum.t[:, :], ssum.t[:, :], ALU.mult), reads=[ssum], writes=[tmpf])
        S.op("dve", lambda: nc.vector.tensor_tensor(ssq.t[:, :], ssq.t[:, :], tmpf.t[:, :], ALU.subtract), reads=[ssq, tmpf], writes=[ssq])
        S.op("act", lambda: nc.scalar.activation(ssq.t[:, :], ssq.t[:, :], AF.Ln, bias=self.epsb.t[:, 0:1], scale=1.0), reads=[ssq, self.epsb], writes=[ssq])
        S.op("act", lambda: nc.scalar.activation(ssq.t[:, :], ssq.t[:, :], AF.Exp, scale=-0.5), reads=[ssq], writes=[ssq])
        S.op("dve", lambda: nc.vector.scalar_tensor_tensor(ssum.t[:, :], ssum.t[:, :], -1.0, ssq.t[:, :], ALU.mult, ALU.mult), reads=[ssum, ssq], writes=[ssum])
        for cbk in range(32):
            if cbk < 16:
                self.ada_step(1)
            wz = self.load_w(w_in[:, 2 * E + cbk * 128:2 * E + (cbk + 1) * 128])
            g2 = g2b[cbk % 2]
            S.dma("sp", g2.t[:, :], self.g2sc[cbk * 128:(cbk + 1) * 128, :], reads=[self.g2_res[cbk]], writes=[g2])
            for tc in range(TC):
                sl = slice(tc * 512, (tc + 1) * 512)
                t1 = self.f32()
                S.op("dve", lambda: nc.vector.tensor_tensor(t1.t[:, :], g2.t[:, sl], ssq.t[:, sl], ALU.mult), reads=[g2, ssq], writes=[t1])
                S.op("dve", lambda: nc.vector.tensor_tensor(t1.t[:, :], t1.t[:, :], ssum.t[:, sl], ALU.add), reads=[t1, ssum], writes=[t1])
                S.op("act", lambda: nc.scalar.activation(t1.t[:, :], t1.t[:, :], AF.Silu, bias=vec.t[:, 64 + cbk:65 + cbk], scale=vec.t[:, 32 + cbk:33 + cbk]), reads=[t1, vec], writes=[t1])
                ppz = self.proj_fm(wz, 0, self.hT, tc)
                zs = self.f32()
                S.op("act", lambda: nc.scalar.activation(zs.t[:, :], ppz.t[:, :], AF.Silu), reads=[ppz], writes=[zs])
                S.op("dve", lambda: nc.vector.tensor_tensor(ybuf.t[:, sl], t1.t[:, :], zs.t[:, :], ALU.mult), reads=[t1, zs], writes=[ybuf])
            S.dma("sp", self.ysc[cbk * 128:(cbk + 1) * 128, :], ybuf.t[:, :], reads=[ybuf], writes=[self.ysc_res[cbk]])
        self.out_proj(li, self.conv_w_out[j], 2)

    def out_proj(self, li, w_out, nhalf):
        nc, S = self.nc, self.S
        last = (li == self.nlayers - 1)
        src = self.xT if li == 0 else self.xs
        dst = self.outT if last else self.xs
        S.barrier()
        self.a_reset()
        ysrc = [list(self.hT)]
        if nhalf == 2:
            ysrc.append([self.a_bf16(f"yh{kc}", T) for kc in range(KC)])
        for half in range(nhalf):
            for kc in range(KC):
                r = half * KC + kc
                S.dma("sp", ysrc[half][kc].t[:, :], self.ysc[r * 128:(r + 1) * 128, :], reads=[self.ysc_res[r]], writes=[ysrc[half][kc]])
        iters = [(cb, tc) for cb in range(16) for tc in range(TC)]
        xbs = {}
        PRE = 3

        def load(i):
            if i < len(iters):
                cb, tc = iters[i]
                xb = self.f32()
                S.dma("sp", xb.t[:], src[cb * 128:(cb + 1) * 128, tc * 512:(tc + 1) * 512], reads=[self.xs_res[cb][tc]], writes=[xb])
                xbs[i] = xb
        for i in range(PRE):
            load(i)
        wts = None
        for i, (cb, tc) in enumerate(iters):
            if tc == 0:
                self.ada_step(1)
                wts = [self.load_w(w_out[half * 2048:(half + 1) * 2048, cb * 128:(cb + 1) * 128]) for half in range(nhalf)]
            load(i + PRE)
            pp = self.nextpp()

            def fn():
                ins = None
                n = nhalf * KC
                k = 0
                for half in range(nhalf):
                    for kc in range(KC):
                        ins = nc.tensor.matmul(pp.t[:, :], wts[half].t[:, kc, :], ysrc[half][kc].t[:, tc * 512:(tc + 1) * 512],
                                               start=(k == 0), stop=(k == n - 1))
                        k += 1
                return ins
            S.op("pe", fn, reads=list(wts) + [t for h in ysrc for t in h], writes=[pp])
            xb = xbs.pop(i)
            S.op("dve", lambda: nc.vector.scalar_tensor_tensor(xb.t[:], pp.t[:], self.mod.t[:, 32 + cb:33 + cb], xb.t[:], ALU.mult, ALU.add),
                 reads=[pp, self.mod, xb], writes=[xb])
            S.dma("act", dst[cb * 128:(cb + 1) * 128, tc * 512:(tc + 1) * 512], xb.t[:], reads=[xb], writes=[self.xs_res[cb][tc]])

    def layers(self):
        self.ada_begin(0)
        for li in range(self.nlayers):
            self.S.barrier()
            self.ada_finish(li)
            self.norm(li)
            kind, j = li % 3, li // 3
            if kind == 0:
                self.attn_layer(li, j)
            elif kind == 1:
                self.sgu_layer(li, j)
            else:
                self.conv_layer(li, j)

    def build(self):
        real = self.S
        self.S = DrySched()
        self.dry = True
        self.layers()
        self.S = real
        self.dry = False
        self.f32i = self.bf16i = self.ppi = 0
        self.setup()
        self.S.barrier()
        self.layers()
        self.S.finish()
        return self.nc


class Pipe:
    def __init__(self, depth=4):
        self.q = []
        self.depth = depth

    def push(self, stages):
        self.q.insert(0, stages)
        for age, st in enumerate(self.q):
            if age < len(st) and st[age] is not None:
                st[age]()
        del self.q[self.depth:]

    def flush(self):
        for _ in range(self.depth):
            self.push([])


class DrySched:
    def op(self, *a, **k):
        return None

    def barrier(self):
        return None

    def dma(self, *a, **k):
        return None


def make_in_maps(inputs):
    f = lambda a: np.ascontiguousarray(np.asarray(a))
    x, c, positions = f(inputs["x"]), f(inputs["c"]), f(inputs["positions"])
    cmat, fcol = _consts()
    shared = {
        "ada_w": f(inputs["ada_w"]),
        "ada_bT": f(inputs["ada_b"]).reshape(4, 48, 128).transpose(0, 2, 1).copy(),
        "norm_gT": f(inputs["norm_g"]).reshape(4, KC, 128).transpose(0, 2, 1).copy(),
        "attn_w_in": f(inputs["attn_w_in"]),
        "attn_qg": f(inputs["attn_q_gain"]).reshape(2, 128, 1),
        "attn_kg": f(inputs["attn_k_gain"]).reshape(2, 128, 1),
        "attn_w_out": f(inputs["attn_w_out"]),
        "sgu_w_in": f(inputs["sgu_w_in"]),
        "sgu_ln_g": np.ascontiguousarray(np.broadcast_to(f(inputs["sgu_ln_g"]).reshape(1, E), (128, E))),
        "sgu_ln_b": f(inputs["sgu_ln_b"]).reshape(1, E),
        "sgu_wsT": f(inputs["sgu_ws"])[0].transpose(2, 0, 1).copy(),
        "sgu_bs2": np.concatenate([np.zeros((1, 2048), np.float32), f(inputs["sgu_bs"]).reshape(1, 16 * 128)], 0),
        "sgu_w_out": f(inputs["sgu_w_out"]),
        "conv_w_in": f(inputs["conv_w_in"]),
        "conv_w_out": f(inputs["conv_w_out"]),
        "conv_dwT": f(inputs["conv_dw_w"])[0].reshape(31, 32, 128).transpose(2, 1, 0).copy(),
        "conv_vecT": np.stack([f(inputs["conv_dw_b"])[0], f(inputs["conv_ln_g"])[0], f(inputs["conv_ln_b"])[0]], 0)
                        .reshape(3, 32, 128).transpose(2, 0, 1).copy(),
        "cmat": cmat, "fcol": fcol,
    }
    maps = [None] * NCORES
    for b in range(NB):
        m = dict(shared)
        m["xT"] = np.ascontiguousarray(x[b].T)
        m["cT"] = np.ascontiguousarray(c[b].reshape(KC, 128).T)
        m["pos"] = np.ascontiguousarray(np.broadcast_to(positions[b].astype(np.int32)[None, :], (128, T)))
        maps[REAL_CORES[b]] = m
    zero = {k: np.zeros_like(v) for k, v in maps[REAL_CORES[0]].items()}
    for i in range(NCORES):
        if maps[i] is None:
            maps[i] = zero
    return maps


_NC_CACHE = {}


def kernel(**inputs):
    nl = 4
    if nl not in _NC_CACHE:
        _NC_CACHE[nl] = Builder(nl).build()
    nc = _NC_CACHE[nl]
    maps = make_in_maps(inputs)
    res = run_bass_kernel_spmd(nc, maps, core_ids=list(range(NCORES)))
    out = np.stack([np.ascontiguousarray(res.results[REAL_CORES[b]]["outT"].T) for b in range(NB)], 0)
    return out.astype(np.float32)
```

```python
import numpy as np
import concourse.bass as bass
import concourse.mybir as mybir
from concourse.bass_utils import run_bass_kernel_spmd

F32 = mybir.dt.float32
BF16 = mybir.dt.bfloat16
AF = mybir.ActivationFunctionType
ALU = mybir.AluOpType


class Res:
    __slots__ = ("name", "t", "w", "r")

    def __init__(self, name, t=None):
        self.name = name
        self.t = t
        self.w = None
        self.r = {}


class Sched:
    COMPUTE = ("pe", "act", "dve", "pool")

    def __init__(self, nc, ndma=6):
        self.nc = nc
        self.eng = {"pe": nc.tensor, "act": nc.scalar, "dve": nc.vector,
                    "pool": nc.gpsimd, "sp": nc.sync}
        self.sems = {}
        for k in self.COMPUTE:
            self.sems[("c", k)] = nc.alloc_semaphore(name=f"c_{k}")
        self.ccnt = {k: 0 for k in self.COMPUTE}
        self.ndma = ndma
        self.dval = {}
        self.dnext = {}
        for q in ("sp", "pool", "act"):
            self.dnext[q] = 0
            for i in range(ndma):
                self.sems[("d", q, i)] = nc.alloc_semaphore(name=f"d_{q}_{i}")
                self.dval[("d", q, i)] = 0
        self.seen = {e: {} for e in self.eng}
        self.nwait = 0
        self.nops = 0
        self.trace = {e: [] for e in self.eng}
        self.pending = {e: [] for e in self.eng}

    def sb(self, name, shape, dtype):
        return Res(name, self.nc.alloc_sbuf_tensor(name, shape, dtype))

    def ps(self, name, shape, dtype):
        return Res(name, self.nc.alloc_psum_tensor(name, shape, dtype))

    def res(self, name):
        return Res(name)

    dram_res = res

    def _wait(self, e, ev):
        if ev is None:
            return
        key, val = ev
        if val <= 0:
            return
        if self.seen[e].get(key, 0) >= val:
            return
        if e == "pe" and key == ("c", "pe"):
            return
        self.eng[e].wait_ge(self.sems[key], val)
        self.seen[e][key] = val
        self.nwait += 1
        self.pending[e].append((key, val))

    def _deps(self, e, reads, writes):
        for r in reads:
            self._wait(e, r.w)
        for w in writes:
            self._wait(e, w.w)
            for key, val in w.r.items():
                self._wait(e, (key, val))

    def _mark(self, ev, reads, writes):
        key, val = ev
        for r in reads:
            if r.r.get(key, 0) < val:
                r.r[key] = val
        for w in writes:
            w.w = ev
            w.r = {}

    def op(self, e, fn, reads=(), writes=()):
        self._deps(e, reads, writes)
        inst = fn()
        self.ccnt[e] += 1
        key = ("c", e)
        inst.then_inc(self.sems[key], 1)
        ev = (key, self.ccnt[e])
        self._mark(ev, reads, writes)
        self.nops += 1
        self.trace[e].append((self.pending[e], (key, 1)))
        self.pending[e] = []
        return ev

    def _dma_like(self, q, fn, reads, writes):
        slot = self.dnext[q]
        self.dnext[q] = (slot + 1) % self.ndma
        key = ("d", q, slot)
        self._wait(q, (key, self.dval[key]))
        self._deps(q, reads, writes)
        inst = fn()
        inst.then_inc(self.sems[key], 16)
        self.dval[key] += 16
        ev = (key, self.dval[key])
        self._mark(ev, reads, writes)
        self.nops += 1
        self.trace[q].append((self.pending[q], (key, 16)))
        self.pending[q] = []
        return ev

    def dma(self, q, out, in_, reads=(), writes=()):
        return self._dma_like(q, lambda: self.eng[q].dma_start(out=out, in_=in_), reads, writes)

    def collective(self, fn, reads=(), writes=()):
        return self._dma_like("pool", fn, reads, writes)

    def barrier(self):
        for e in ("pe", "act", "dve", "sp"):
            for k in ("pe", "act", "dve"):
                self._wait(e, (("c", k), self.ccnt[k]))
            for q in ("sp", "act"):
                for i in range(self.ndma):
                    key = ("d", q, i)
                    self._wait(e, (key, self.dval[key]))

    def simulate(self):
        for e in self.eng:
            if self.pending[e]:
                self.trace[e].append((self.pending[e], None))
                self.pending[e] = []
        cnt = {k: 0 for k in self.sems}
        ptr = {e: 0 for e in self.eng}
        progress = True
        while progress:
            progress = False
            for e in self.eng:
                tr = self.trace[e]
                while ptr[e] < len(tr):
                    waits, inc = tr[ptr[e]]
                    if any(cnt[k] < v for k, v in waits):
                        break
                    if inc is not None:
                        cnt[inc[0]] += inc[1]
                    ptr[e] += 1
                    progress = True
        stuck = {e: (ptr[e], len(self.trace[e]), self.trace[e][ptr[e]][0]) for e in self.eng if ptr[e] < len(self.trace[e])}
        return stuck, cnt

    def finish(self):
        for key, val in self.dval.items():
            self._wait("sp", (key, val))
        for k in self.COMPUTE:
            self._wait("sp", (("c", k), self.ccnt[k]))


D = 2048
T = 2048
NB = 4
NCORES = 4
REAL_CORES = (0, 1, 2, 3)
KC = 16
TC = 4
E = 4096
NEG = -30000.0
EPS = 1e-6
SCALE = 128.0 ** -0.5
TWO_PI_HI = 6.28125
TWO_PI_LO = 2.0 * np.pi - 6.28125
DILS = (1, 4, 16)


def _consts():
    ident = np.eye(128, dtype=np.float32)
    prot = np.zeros((128, 128), np.float32)
    for p in range(16):
        prot[p + 16, p] = -1.0
        prot[p, p + 16] = 1.0
    j = np.arange(128)[:, None]
    i = np.arange(128)[None, :]
    maskb = np.concatenate([np.where(j <= i, 0.0, NEG), np.where(j >= i, 0.0, NEG)], 1).astype(np.float32)
    inv_freq = (500000.0 ** (-np.arange(0, 32, 2, dtype=np.float32) / 32.0)).astype(np.float32)
    fcol = np.zeros((128, 1), np.float32)
    fcol[:32, 0] = np.concatenate([inv_freq, inv_freq])
    tril_st = (j <= i).astype(np.float32)
    cmat = np.concatenate([ident, prot, maskb, tril_st], 1)
    return cmat, fcol


class Builder:
    def __init__(self, nlayers=4):
        self.nlayers = nlayers
        nc = self.nc = bass.Bass("TRN2", target_bir_lowering=False)
        S = self.S = Sched(nc)
        dt = nc.dram_tensor
        def inp(name, shape, dtype=F32):
            return dt(name, list(shape), dtype, kind="ExternalInput").ap()
        self.xT = inp("xT", [D, T])
        self.cT = inp("cT", [128, KC])
        self.pos = inp("pos", [128, T], mybir.dt.int32)
        self.ada_w = inp("ada_w", [4, D, 3 * D])
        self.ada_bT = inp("ada_bT", [4, 128, 48])
        self.norm_gT = inp("norm_gT", [4, 128, KC])
        self.attn_w_in = inp("attn_w_in", [2, D, 20480])
        self.attn_qg = inp("attn_qg", [2, 128, 1])
        self.attn_kg = inp("attn_kg", [2, 128, 1])
        self.attn_w_out = inp("attn_w_out", [2, D, D])
        self.sgu_w_in = inp("sgu_w_in", [1, D, 3 * E])
        self.sgu_ln_g = inp("sgu_ln_g", [128, E])
        self.sgu_ln_b = inp("sgu_ln_b", [1, E])
        self.sgu_wsT = inp("sgu_wsT", [128, 16, 128])
        self.sgu_bs2 = inp("sgu_bs2", [2, 16 * 128])
        self.sgu_w_out = inp("sgu_w_out", [1, E, D])
        self.conv_w_in = inp("conv_w_in", [1, D, 3 * E])
        self.conv_w_out = inp("conv_w_out", [1, E, D])
        self.conv_dwT = inp("conv_dwT", [128, 32, 31])
        self.conv_vecT = inp("conv_vecT", [128, 3, 32])
        self.cmat = inp("cmat", [128, 640])
        self.fcol_in = inp("fcol", [128, 1])
        self.outT = dt("outT", [D, T], F32, kind="ExternalOutput").ap()
        self.xs = dt("xs", [D, T], F32, kind="Internal").ap()
        self.ysc = dt("ysc", [E, T], BF16, kind="Internal").ap()
        self.gsc = dt("gsc", [T, E], BF16, kind="Internal").ap()
        self.g2sc = dt("g2sc", [E, T], BF16, kind="Internal").ap()
        self.xs_res = [[S.res(f"xs{c}_{t}") for t in range(TC)] for c in range(KC)]
        self.ysc_res = [S.res(f"ysc{c}") for c in range(32)]
        self.gsc_res = [S.res(f"gsc{c}") for c in range(16)]
        self.g2_res = [S.res(f"g2sc{c}") for c in range(32)]
        self.alloc()

    def alloc(self):
        S = self.S
        self.hT = [S.sb(f"hT{kc}", [128, T], BF16) for kc in range(KC)]
        self.NW = 8
        self.wt = [S.sb(f"wt{i}", [128, KC, 128], BF16) for i in range(self.NW)]
        self.wplan = []
        self.wissued = 0
        self.wused = 0
        self.dry = False
        self.pp = [S.ps(f"pp{i}", [128, 512], F32) for i in range(3)]
        self.pa = S.ps("pa", [128, 512], F32)
        self.pb = S.ps("pb", [128, 512], F32)
        self.psx = [S.ps(f"psx{i}", [128, 512], F32) for i in range(2)]
        self.pol = S.ps("pol", [128, 512], F32)
        self.ppi = 0
        self.cb = S.sb("cb", [128, 640], BF16)
        self.ident = self.cb.t[:, 0:128]
        self.prot = self.cb.t[:, 128:256]
        self.maskb = self.cb.t[:, 256:512]
        self.tril = self.cb.t[:, 512:640]
        self.onesD = S.sb("onesD", [128, 128], BF16)
        self.onesH = S.sb("onesH", [128, 128], BF16)
        self.ones1 = S.sb("ones1", [128, 128], BF16)
        self.onesE = S.sb("onesE", [128, 128], BF16)
        self.fcol = S.sb("fcolsb", [128, 1], F32)
        self.cosT = S.sb("cosT", [128, T], F32)
        self.sinT = S.sb("sinT", [128, T], F32)
        self.cact = S.sb("cact", [128, KC], BF16)
        self.mod = S.sb("mod", [128, 48], F32)
        self.modn = S.sb("modn", [128, 48], F32)
        self.avec = S.sb("avec", [128, KC], F32)
        self.adab = S.sb("adab", [128, 48], F32)
        self.ng = S.sb("ng", [128, KC], F32)
        self.gain = S.sb("gain", [128, 2], F32)
        self.epsb = S.sb("epsb", [128, 1], F32)
        self.rstdn = [S.sb(f"rstdn{i}", [128, 512], F32) for i in range(2)]
        self.f32s = [S.sb(f"f32s{i}", [128, 512], F32) for i in range(6)]
        self.f32i = 0
        self.bf16s = [S.sb(f"bf16s{i}", [128, 512], BF16) for i in range(4)]
        self.bf16i = 0
        self.ARENA_F32 = 17152
        self.arena = S.sb("arena", [128, self.ARENA_F32], F32)
        self.aoff = 0

    def a_reset(self):
        self.aoff = 0

    def a_f32(self, name, n):
        ap = self.arena.t[:, self.aoff:self.aoff + n]
        self.aoff += n
        assert self.aoff <= self.ARENA_F32, (name, self.aoff)
        return Res(name, ap)

    def a_bf16(self, name, n):
        assert n % 2 == 0
        ap = self.arena.t[:, self.aoff:self.aoff + n // 2].bitcast(BF16)
        self.aoff += n // 2
        assert self.aoff <= self.ARENA_F32, (name, self.aoff)
        return Res(name, ap)

    def a_i32(self, name, n):
        ap = self.arena.t[:, self.aoff:self.aoff + n].bitcast(mybir.dt.int32)
        self.aoff += n
        assert self.aoff <= self.ARENA_F32, (name, self.aoff)
        return Res(name, ap)

    def f32(self):
        r = self.f32s[self.f32i % len(self.f32s)]
        self.f32i += 1
        return r

    def b16(self):
        r = self.bf16s[self.bf16i % len(self.bf16s)]
        self.bf16i += 1
        return r

    def nextpp(self):
        r = self.pp[self.ppi % 3]
        self.ppi += 1
        return r

    def load_w(self, wap):
        if self.dry:
            self.wplan.append(wap)
            return self.wt[0]
        u = self.wused
        self.wused += 1
        while self.wissued < len(self.wplan) and self.wissued < u + self.NW - 1:
            i = self.wissued
            t = self.wt[i % self.NW]
            self.S.dma("pool", t.t[:], self.wplan[i].rearrange("(kc p) n -> p kc n", p=128), writes=[t])
            self.wissued += 1
        return self.wt[u % self.NW]

    def proj_fm(self, wt, c0, src, tc, n=512, pp=None):
        nc, S = self.nc, self.S
        if pp is None:
            pp = self.nextpp()

        def fn():
            ins = None
            for kc in range(KC):
                ins = nc.tensor.matmul(pp.t[:, 0:n], wt.t[:, kc, c0:c0 + 128],
                                       src[kc].t[:, tc * n:(tc + 1) * n],
                                       start=(kc == 0), stop=(kc == KC - 1))
            return ins
        S.op("pe", fn, reads=[wt] + list(src), writes=[pp])
        return pp

    def setup(self):
        nc, S = self.nc, self.S
        S.dma("pool", self.cb.t[:], self.cmat, writes=[self.cb])
        S.dma("sp", self.fcol.t[:], self.fcol_in, writes=[self.fcol])
        for t, v in ((self.onesD, 1.0 / D), (self.onesH, 1.0 / 128), (self.ones1, 1.0), (self.onesE, 1.0 / E)):
            S.op("dve", lambda t=t, v=v: nc.vector.memset(t.t[:], v), writes=[t])
        S.op("dve", lambda: nc.vector.memset(self.epsb.t[:], EPS), writes=[self.epsb])
        c32 = self.f32()
        S.dma("sp", c32.t[:, 0:KC], self.cT, writes=[c32])
        S.op("act", lambda: nc.scalar.activation(self.cact.t[:], c32.t[:, 0:KC], AF.Silu), reads=[c32], writes=[self.cact])
        self.a_reset()
        posi = self.a_i32("posi", T)
        S.dma("sp", posi.t[:], self.pos, writes=[posi])
        ang, ki, kf, r = self.a_f32("ang", T), self.a_i32("ki", T), self.a_f32("kf", T), self.a_f32("r", T)
        S.op("dve", lambda: nc.vector.tensor_copy(ang.t[:], posi.t[:]), reads=[posi], writes=[ang])
        S.op("dve", lambda: nc.vector.tensor_scalar(ang.t[:], ang.t[:], self.fcol.t[:, 0:1], None, ALU.mult), reads=[ang, self.fcol], writes=[ang])
        for tab, shift in ((self.sinT, 0.0), (self.cosT, np.pi / 2)):
            S.op("dve", lambda: nc.vector.tensor_scalar(r.t[:], ang.t[:], float(shift), None, ALU.add), reads=[ang], writes=[r])
            S.op("dve", lambda: nc.vector.tensor_scalar(ki.t[:], r.t[:], float(1.0 / (2 * np.pi)), None, ALU.mult), reads=[r], writes=[ki])
            S.op("dve", lambda: nc.vector.tensor_copy(kf.t[:], ki.t[:]), reads=[ki], writes=[kf])
            S.op("dve", lambda: nc.vector.scalar_tensor_tensor(r.t[:], kf.t[:], -TWO_PI_HI, r.t[:], ALU.mult, ALU.add), reads=[kf, r], writes=[r])
            S.op("dve", lambda: nc.vector.scalar_tensor_tensor(r.t[:], kf.t[:], -float(TWO_PI_LO), r.t[:], ALU.mult, ALU.add), reads=[kf, r], writes=[r])
            S.op("dve", lambda: nc.vector.tensor_scalar(kf.t[:], r.t[:], float(np.pi), None, ALU.is_gt), reads=[r], writes=[kf])
            S.op("dve", lambda: nc.vector.scalar_tensor_tensor(r.t[:], kf.t[:], -float(2 * np.pi), r.t[:], ALU.mult, ALU.add), reads=[kf, r], writes=[r])
            S.op("dve", lambda: nc.vector.tensor_scalar(kf.t[:], r.t[:], -float(np.pi), None, ALU.is_lt), reads=[r], writes=[kf])
            S.op("dve", lambda: nc.vector.scalar_tensor_tensor(r.t[:], kf.t[:], float(2 * np.pi), r.t[:], ALU.mult, ALU.add), reads=[kf, r], writes=[r])
            S.op("dve", lambda: nc.vector.tensor_scalar(r.t[:], r.t[:], -3.1415925, 3.1415925, ALU.max, ALU.min), reads=[r], writes=[r])
            S.op("act", lambda tab=tab: nc.scalar.activation(tab.t[:], r.t[:], AF.Sin), reads=[r], writes=[tab])

    def ada_begin(self, li):
        self.ada_li = li
        self.ada_j = 0

    def ada_step(self, n=1):
        nc, S = self.nc, self.S
        li = self.ada_li
        for _ in range(n):
            if li is None or li >= self.nlayers or self.ada_j >= 48:
                return
            j = self.ada_j
            self.ada_j += 1
            wt = self.load_w(self.ada_w[li, :, j * 128:(j + 1) * 128])
            pm = self.pa

            def fn():
                ins = None
                for kc in range(KC):
                    ins = nc.tensor.matmul(pm.t[:, 0:1], wt.t[:, kc, :], self.cact.t[:, kc:kc + 1], start=(kc == 0), stop=(kc == KC - 1))
                return ins
            S.op("pe", fn, reads=[wt, self.cact], writes=[pm])
            S.op("dve", lambda: nc.vector.tensor_copy(self.modn.t[:, j:j + 1], pm.t[:, 0:1]), reads=[pm], writes=[self.modn])

    def ada_finish(self, li):
        nc, S = self.nc, self.S
        assert self.ada_li == li
        self.ada_step(48)
        S.dma("sp", self.adab.t[:], self.ada_bT[li], writes=[self.adab])
        S.dma("sp", self.ng.t[:], self.norm_gT[li], writes=[self.ng])
        S.op("dve", lambda: nc.vector.tensor_tensor(self.mod.t[:], self.modn.t[:], self.adab.t[:], ALU.add),
             reads=[self.modn, self.adab], writes=[self.mod])
        S.op("dve", lambda: nc.vector.scalar_tensor_tensor(self.avec.t[:], self.mod.t[:, 16:32], 1.0, self.ng.t[:], ALU.add, ALU.mult),
             reads=[self.mod, self.ng], writes=[self.avec])
        self.ada_begin(li + 1)

    def norm(self, li):
        nc, S = self.nc, self.S
        src = self.xT if li == 0 else self.xs
        self.a_reset()
        xch = [[self.a_f32(f"xch{i}_{kc}", 512) for kc in range(KC)] for i in range(2)]
        for tc in range(TC):
            pm = self.pa
            xc = xch[tc % 2]
            for kc in range(KC):
                xb = xc[kc]
                S.dma("sp", xb.t[:], src[kc * 128:(kc + 1) * 128, tc * 512:(tc + 1) * 512],
                      reads=[self.xs_res[kc][tc]], writes=[xb])
                sq = self.b16()
                S.op("act", lambda xb=xb, sq=sq: nc.scalar.activation(sq.t[:], xb.t[:], AF.Square), reads=[xb], writes=[sq])
                S.op("pe", lambda sq=sq, kc=kc: nc.tensor.matmul(pm.t[:], self.onesD.t[:], sq.t[:], start=(kc == 0), stop=(kc == KC - 1)),
                     reads=[sq, self.onesD], writes=[pm])
            rstd = self.rstd_from(pm, out=self.rstdn[tc % 2])
            for kc in range(KC):
                xb = xc[kc]
                S.op("dve", lambda xb=xb, kc=kc: nc.vector.scalar_tensor_tensor(xb.t[:], xb.t[:], self.avec.t[:, kc:kc + 1], rstd.t[:], ALU.mult, ALU.mult),
                     reads=[xb, self.avec, rstd], writes=[xb])
                S.op("act", lambda xb=xb, kc=kc, tc=tc: nc.scalar.activation(self.hT[kc].t[:, tc * 512:(tc + 1) * 512], xb.t[:], AF.Identity,
                                                                       bias=self.mod.t[:, kc:kc + 1], scale=1.0),
                     reads=[xb, self.mod], writes=[self.hT[kc]])
        self.S.barrier()

    def rstd_from(self, pm, n=512, out=None):
        nc, S = self.nc, self.S
        r = out if out is not None else self.f32()
        S.op("act", lambda: nc.scalar.activation(r.t[:, 0:n], pm.t[:, 0:n], AF.Ln, bias=self.epsb.t[:, 0:1], scale=1.0), reads=[pm, self.epsb], writes=[r])
        S.op("act", lambda: nc.scalar.activation(r.t[:, 0:n], r.t[:, 0:n], AF.Exp, scale=-0.5), reads=[r], writes=[r])
        return r

    def perm_out(self, buf, dil, tc):
        if dil == 1:
            return buf.t[:, tc * 512:(tc + 1) * 512]
        L = T // dil
        n = 512 // dil
        return buf.t[:, :].rearrange("p (r m) -> p r m", r=dil)[:, :, tc * n:(tc + 1) * n]

    def perm_in(self, ap512, dil):
        if dil == 1:
            return ap512
        return ap512.rearrange("p (m r) -> p r m", r=dil)

    def nat_ap(self, buf, dil, b4):
        if dil == 1:
            return buf.t[:, b4 * 512:(b4 + 1) * 512]
        if dil == 4:
            return buf.t[:, :].rearrange("p (m r) -> p r m", r=4)[:, b4, :]
        return buf.t[:, :].rearrange("p (m r) -> p r m", r=16)[:, 4 * b4:4 * b4 + 4, :]

    def blk_view(self, ps, dil):
        if dil == 16:
            return ps.t[:, :].rearrange("p (r m) -> p r m", r=4)
        return ps.t[:, :]

    def nat_ap2(self, buf, dil, b0):
        if dil == 1:
            return buf.t[:, b0 * 128:b0 * 128 + 256]
        if dil == 4:
            r, m0 = b0 // 4, (b0 % 4) * 128
            return buf.t[:, :].rearrange("p (m r) -> p r m", r=4)[:, r, m0:m0 + 256]
        return buf.t[:, :].rearrange("p (m r) -> p r m", r=16)[:, b0:b0 + 2, :]

    def blk_view2(self, ap256, dil):
        if dil == 16:
            return ap256.rearrange("p (r m) -> p r m", r=2)
        return ap256

    def attn_layer(self, li, j):
        nc, S = self.nc, self.S
        w_in = self.attn_w_in[j]
        S.dma("sp", self.gain.t[:, 0:1], self.attn_qg[j], writes=[self.gain])
        S.dma("sp", self.gain.t[:, 1:2], self.attn_kg[j], writes=[self.gain])
        self.a_reset()
        qf = [self.a_bf16(f"qf{i}", T) for i in range(2)]
        kf = [self.a_bf16(f"kf{i}", T) for i in range(2)]
        vtm = [self.a_bf16(f"vtm{i}", T) for i in range(2)]
        zs = [self.a_bf16(f"zs{i}", T) for i in range(2)]
        vT = self.a_bf16("vT", T)
        ybuf = self.a_bf16("ybuf", T)
        oacc, lacc = self.a_f32("oacc", T), self.a_f32("lacc", T)
        pT = [self.a_bf16(f"pT{i}", 256) for i in range(6)]
        kgp = [self.a_bf16(f"kg{i}", 512) for i in range(3)]
        pipe = Pipe()
        units = [(h, g) for h in range(16) for g in range(3)]
        jobc = [0]

        def a_jobs(u):
            h, g = units[u]
            dil = DILS[g]
            par = u % 2
            jobs = []

            def mk(name, base, tc, shared):
                jid = jobc[0]
                jobc[0] += 1
                pp = self.pp[jid % 3]
                kg = kgp[jid % 3]
                c = base + (g * 2048 if name != "z" else 0) + h * 128

                def s0():
                    if tc == 0:
                        shared["wt"] = self.load_w(w_in[:, c:c + 128])
                    self.proj_fm(shared["wt"], 0, self.hT, tc, pp=pp)
                if name in ("k", "q"):
                    gcol = 1 if name == "k" else 0
                    dst = kf[par] if name == "k" else qf[par]

                    def s1():
                        sq = self.b16()
                        S.op("act", lambda: nc.scalar.activation(sq.t[:], pp.t[:], AF.Square), reads=[pp], writes=[sq])
                        S.op("pe", lambda: nc.tensor.matmul(self.pa.t[:], self.onesH.t[:], sq.t[:], start=True, stop=True),
                             reads=[sq, self.onesH], writes=[self.pa])
                        rstd = self.rstd_from(self.pa)
                        S.op("dve", lambda: nc.vector.scalar_tensor_tensor(kg.t[:], pp.t[:], self.gain.t[:, gcol:gcol + 1], rstd.t[:], ALU.mult, ALU.mult),
                             reads=[pp, self.gain, rstd], writes=[kg])

                    def s2():
                        S.op("pe", lambda: nc.tensor.matmul(self.pb.t[:], self.prot, kg.t[:], start=True, stop=True),
                             reads=[kg, self.cb], writes=[self.pb])
                        t1 = self.f32()
                        t2 = self.f32()
                        S.op("dve", lambda: nc.vector.tensor_tensor(t1.t[:], kg.t[:], self.cosT.t[:, tc * 512:(tc + 1) * 512], ALU.mult),
                             reads=[kg, self.cosT], writes=[t1])
                        S.op("dve", lambda: nc.vector.tensor_tensor(t2.t[:], self.pb.t[:], self.sinT.t[:, tc * 512:(tc + 1) * 512], ALU.mult),
                             reads=[self.pb, self.sinT], writes=[t2])
                        S.op("dve", lambda: nc.vector.tensor_tensor(self.perm_out(dst, dil, tc), self.perm_in(t1.t[:, :], dil), self.perm_in(t2.t[:, :], dil), ALU.add),
                             reads=[t1, t2], writes=[dst])
                    return [s0, s1, s2]
                if name == "v":
                    def s1():
                        S.op("act", lambda: nc.scalar.activation(self.perm_out(vT, dil, tc), self.perm_in(pp.t[:, :], dil), AF.Copy),
                             reads=[pp], writes=[vT])

                    def s2():
                        for b4 in range(4):
                            ps = self.pa if b4 % 2 == 0 else self.pb

                            def fn():
                                ins = None
                                for q in range(4):
                                    b = b4 * 4 + q
                                    ins = nc.tensor.matmul(ps.t[:, q * 128:(q + 1) * 128], vT.t[:, b * 128:(b + 1) * 128], self.ident, start=True, stop=True)
                                return ins
                            S.op("pe", fn, reads=[vT, self.cb], writes=[ps])
                            S.op("act", lambda: nc.scalar.activation(vtm[par].t[:, b4 * 512:(b4 + 1) * 512], ps.t[:, :], AF.Copy), reads=[ps], writes=[vtm[par]])
                    return [s0, s1, s2 if tc == 3 else None]
                def s1z():
                    S.op("act", lambda: nc.scalar.activation(zs[h % 2].t[:, tc * 512:(tc + 1) * 512], pp.t[:, :], AF.Silu), reads=[pp], writes=[zs[h % 2]])
                return [s0, s1z]

            for name, base in (("k", 6144), ("q", 0), ("v", 12288)) + ((("z", 18432),) if g == 2 else ()):
                shared = {}
                for tc in range(TC):
                    jobs.append(mk(name, base, tc, shared))
            return jobs

        def b_steps(u):
            h, g = units[u]
            dil = DILS[g]
            nb = 16 // dil
            par = u % 2
            kfin, qfin, vt = kf[par], qf[par], vtm[par]

            def score(b):
                jj = b % nb
                n = 256 if jj + 1 < nb else 128
                ps = self.psx[b % 2]
                p = pT[b % 6]

                def fn():
                    nc.tensor.matmul(ps.t[:, 0:n], self.ident, self.maskb[:, 0:n], start=True, stop=False)
                    return nc.tensor.matmul(ps.t[:, 0:n], kfin.t[:, b * 128:(b + 1) * 128], qfin.t[:, b * 128:b * 128 + n], start=False, stop=True)
                S.op("pe", fn, reads=[self.cb, kfin, qfin], writes=[ps])
                S.op("act", lambda: nc.scalar.activation(p.t[:, 0:n], ps.t[:, 0:n], AF.Exp, scale=SCALE), reads=[ps], writes=[p])

            def pv(b):
                jj = b % nb
                q2 = b % 2
                p = pT[b % 6]
                pprev = pT[(b - 1) % 6]
                for which in (0, 1):
                    col = which * 256 + q2 * 128

                    def fn2():
                        first = True
                        if jj > 0:
                            l = vt.t[:, (b - 1) * 128:b * 128] if which == 0 else self.ones1.t[:, :]
                            nc.tensor.matmul(self.pol.t[:, col:col + 128], l, pprev.t[:, 128:256], start=True, stop=False)
                            first = False
                        l = vt.t[:, b * 128:(b + 1) * 128] if which == 0 else self.ones1.t[:, :]
                        return nc.tensor.matmul(self.pol.t[:, col:col + 128], l, p.t[:, 0:128], start=first, stop=True)
                    S.op("pe", fn2, reads=[vt, self.ones1, p] + ([pprev] if jj > 0 else []), writes=[self.pol])
                if q2 == 1:
                    b0 = b - 1
                    for which, acc in ((0, oacc), (1, lacc)):
                        dst_ap = self.nat_ap2(acc, dil, b0)
                        src_ap = self.blk_view2(self.pol.t[:, which * 256:(which + 1) * 256], dil)
                        if g == 0:
                            S.op("act", lambda: nc.scalar.activation(dst_ap, src_ap, AF.Copy), reads=[self.pol], writes=[acc])
                        else:
                            S.op("dve", lambda: nc.vector.tensor_tensor(dst_ap, dst_ap, src_ap, ALU.add), reads=[self.pol, acc], writes=[acc])

            def mkstep(k):
                def st():
                    for b in (2 * k, 2 * k + 1):
                        if b < 16:
                            score(b)
                    for b in (2 * k - 2, 2 * k - 1):
                        if 0 <= b < 16:
                            pv(b)
                return st
            steps = [mkstep(k) for k in range(9)]
            if g == 2:
                def comb():
                    S.op("dve", lambda: nc.vector.reciprocal(lacc.t[:, :], lacc.t[:, :]), reads=[lacc], writes=[lacc])
                    S.op("dve", lambda: nc.vector.tensor_tensor(oacc.t[:, :], oacc.t[:, :], lacc.t[:, :], ALU.mult), reads=[oacc, lacc], writes=[oacc])
                    S.op("dve", lambda: nc.vector.tensor_tensor(ybuf.t[:, :], oacc.t[:, :], zs[h % 2].t[:, :], ALU.mult), reads=[oacc, zs[h % 2]], writes=[ybuf])
                    S.dma("sp", self.ysc[h * 128:(h + 1) * 128, :], ybuf.t[:, :], reads=[ybuf], writes=[self.ysc_res[h]])
                steps.append(comb)
            return steps

        NU = len(units)
        LAG = 3
        for w in range(NU + 1):
            aj = a_jobs(w) if w < NU else []
            bs = b_steps(w - 1) if w >= 1 else []
            n = max(len(aj), (LAG + len(bs)) if bs else 0)
            self.ada_step(1)
            for i in range(n):
                pipe.push(aj[i] if i < len(aj) else [])
                if bs and LAG <= i < LAG + len(bs):
                    bs[i - LAG]()
        pipe.flush()
        self.out_proj(li, self.attn_w_out[j], 1)

    def gelu_from(self, pp, out_ap, n=512):
        nc, S = self.nc, self.S
        a = self.f32()
        S.op("act", lambda: nc.scalar.activation(a.t[:, 0:n], pp.t[:, 0:n], AF.Square), reads=[pp], writes=[a])
        S.op("dve", lambda: nc.vector.tensor_scalar(a.t[:, 0:n], a.t[:, 0:n], 0.044715, 1.0, ALU.mult, ALU.add), reads=[a], writes=[a])
        S.op("dve", lambda: nc.vector.tensor_tensor(a.t[:, 0:n], a.t[:, 0:n], pp.t[:, 0:n], ALU.mult), reads=[a, pp], writes=[a])
        S.op("act", lambda: nc.scalar.activation(a.t[:, 0:n], a.t[:, 0:n], AF.Sigmoid, scale=1.5957691216057308), reads=[a], writes=[a])
        return a

    def sgu_layer(self, li, j):
        nc, S = self.nc, self.S
        w_in = self.sgu_w_in[j]
        self.a_reset()
        gtiles = [self.a_bf16(f"gtile{i}", 16 * 512) for i in range(2)]
        lngs = [self.a_f32(f"lng{i}", 512) for i in range(2)]
        wmT = self.a_bf16("wmT", 16 * 128)
        L2 = self.a_bf16("L2", E)
        RB = self.a_bf16("RB", 16 * 128)
        ybuf = self.a_bf16("ybuf", T)
        ssum = self.a_f32("ssum", 512)
        ssq = self.a_f32("ssq", 512)
        st = self.a_f32("st", 64)
        gstage = [self.a_bf16(f"gstage{i}", 512) for i in range(2)]
        junk = self.a_bf16("junk", 128)
        for q in range(4):
            w32 = self.f32()
            S.dma("sp", w32.t[:, :], self.sgu_wsT[:, q * 4:(q + 1) * 4, :].rearrange("p g t -> p (g t)"), writes=[w32])
            for gg in range(4):
                g = q * 4 + gg
                S.op("dve", lambda: nc.vector.tensor_tensor(wmT.t[:, g * 128:(g + 1) * 128], w32.t[:, gg * 128:(gg + 1) * 128], self.tril, ALU.mult),
                     reads=[w32, self.cb], writes=[wmT])
        S.op("dve", lambda: nc.vector.memset(L2.t[0:2, :], 1.0), writes=[L2])
        for q in range(8):
            st32 = self.f32()
            S.dma("sp", st32.t[0:1, :], self.sgu_ln_b[:, q * 512:(q + 1) * 512], writes=[st32])
            S.op("dve", lambda: nc.vector.tensor_copy(L2.t[0:1, q * 512:(q + 1) * 512], st32.t[0:1, :]), reads=[st32], writes=[L2])
        for q in range(4):
            st32 = self.f32()
            S.dma("sp", st32.t[0:2, :], self.sgu_bs2[:, q * 512:(q + 1) * 512], writes=[st32])
            S.op("pe", lambda: nc.tensor.matmul(self.pa.t[0:1, :], self.ones1.t[:, 0:1], wmT.t[:, q * 512:(q + 1) * 512], start=True, stop=True),
                 reads=[wmT, self.ones1], writes=[self.pa])
            S.op("act", lambda: nc.scalar.activation(st32.t[0:1, :], self.pa.t[0:1, :], AF.Copy), reads=[self.pa], writes=[st32])
            S.op("dve", lambda: nc.vector.tensor_copy(RB.t[0:2, q * 512:(q + 1) * 512], st32.t[0:2, :]), reads=[st32], writes=[RB])
        pipe = Pipe()
        gvp = [self.a_bf16(f"gvp{i}", 512) for i in range(4)]
        jobc = [0]

        def mkb1(cbk, tc, shared):
            jid = jobc[0]
            jobc[0] += 1
            pp = self.pp[jid % 3]
            gv = gvp[jid % 4]
            ps = self.psx[jid % 2]
            gs = gstage[jid % 2]

            def s0():
                if tc == 0:
                    self.ada_step(1)
                    shared["wt"] = self.load_w(w_in[:, E + cbk * 128:E + (cbk + 1) * 128])
                self.proj_fm(shared["wt"], 0, self.hT, tc, pp=pp)

            def s1():
                a = self.gelu_from(pp, None)
                S.op("dve", lambda: nc.vector.tensor_tensor(gv.t[:, :], a.t[:, :], pp.t[:, :], ALU.mult), reads=[a, pp], writes=[gv])

            def s2():
                def fn():
                    ins = None
                    for q in range(4):
                        ins = nc.tensor.matmul(ps.t[:, q * 128:(q + 1) * 128], gv.t[:, q * 128:(q + 1) * 128], self.ident, start=True, stop=True)
                    return ins
                S.op("pe", fn, reads=[gv, self.cb], writes=[ps])
                for q in range(4):
                    n = tc * 4 + q
                    col = n * 32 + cbk
                    S.op("act", lambda: nc.scalar.activation(gs.t[:, q * 128:(q + 1) * 128], ps.t[:, q * 128:(q + 1) * 128], AF.Copy, accum_out=ssum.t[:, col:col + 1]),
                         reads=[ps], writes=[gs, ssum])
                    S.op("dve", lambda: nc.vector.scalar_tensor_tensor(junk.t[:, :], gs.t[:, q * 128:(q + 1) * 128], 1.0, gs.t[:, q * 128:(q + 1) * 128], ALU.mult, ALU.mult,
                                                                       accum_out=ssq.t[:, col:col + 1]),
                         reads=[gs], writes=[junk, ssq])
                dst = self.gsc[tc * 512:(tc + 1) * 512, cbk * 128:(cbk + 1) * 128].rearrange("(q p) c -> p q c", p=128)
                S.dma("sp", dst, gs.t.rearrange("p (q c) -> p q c", q=4), reads=[gs], writes=[self.gsc_res[tc * 4 + q2] for q2 in range(4)])
            return [s0, s1, None, s2]
        for cbk in range(32):
            shared = {}
            for tc in range(TC):
                pipe.push(mkb1(cbk, tc, shared))
        pipe.flush()
        mean, ex2, rstd, nmr = (st.t[:, k * 16:(k + 1) * 16] for k in range(4))
        S.op("dve", lambda: nc.vector.tensor_reduce(mean, ssum.t.rearrange("p (n c) -> p n c", n=16), mybir.AxisListType.X, ALU.add), reads=[ssum], writes=[st])
        S.op("dve", lambda: nc.vector.tensor_reduce(ex2, ssq.t.rearrange("p (n c) -> p n c", n=16), mybir.AxisListType.X, ALU.add), reads=[ssq], writes=[st])
        S.op("dve", lambda: nc.vector.tensor_scalar(mean, mean, 1.0 / E, None, ALU.mult), reads=[st], writes=[st])
        S.op("dve", lambda: nc.vector.tensor_scalar(ex2, ex2, 1.0 / E, None, ALU.mult), reads=[st], writes=[st])
        S.op("dve", lambda: nc.vector.tensor_tensor(nmr, mean, mean, ALU.mult), reads=[st], writes=[st])
        S.op("dve", lambda: nc.vector.tensor_tensor(ex2, ex2, nmr, ALU.subtract), reads=[st], writes=[st])
        S.op("act", lambda: nc.scalar.activation(rstd, ex2, AF.Ln, bias=self.epsb.t[:, 0:1], scale=1.0), reads=[st, self.epsb], writes=[st])
        S.op("act", lambda: nc.scalar.activation(rstd, rstd, AF.Exp, scale=-0.5), reads=[st], writes=[st])
        S.op("dve", lambda: nc.vector.scalar_tensor_tensor(nmr, mean, -1.0, rstd, ALU.mult, ALU.mult), reads=[st], writes=[st])
        def prep(cg):
            gtile, lng = gtiles[cg % 2], lngs[cg % 2]
            gt3 = gtile.t.rearrange("p (n c) -> p n c", n=16)
            ops = []

            def ld():
                S.dma("sp", gt3, self.gsc[:, cg * 512:(cg + 1) * 512].rearrange("(n p) c -> p n c", p=128), reads=self.gsc_res, writes=[gtile])
                S.dma("sp", lng.t[:, :], self.sgu_ln_g[:, cg * 512:(cg + 1) * 512], writes=[lng])
            ops.append(ld)
            for n in range(16):
                def nrm(n=n):
                    S.op("dve", lambda: nc.vector.tensor_scalar(gt3[:, n, :], gt3[:, n, :], st.t[:, 32 + n:33 + n], st.t[:, 48 + n:49 + n], ALU.mult, ALU.add),
                         reads=[gtile, st], writes=[gtile])
                    S.op("dve", lambda: nc.vector.tensor_tensor(gt3[:, n, :], gt3[:, n, :], lng.t[:, :], ALU.mult), reads=[gtile, lng], writes=[gtile])
                ops.append(nrm)
            return ops
        for o in prep(0):
            o()
        pend = []
        for cg in range(8):
            gtile = gtiles[cg % 2]
            gt3 = gtile.t.rearrange("p (n c) -> p n c", n=16)
            for o in pend:
                o()
            pend = prep(cg + 1) if cg + 1 < 8 else []
            for cbl in range(4):
                cbk = cg * 4 + cbl
                g = cbk // 2
                if cbk % 2 == 0:
                    self.ada_step(1)
                wu = self.load_w(w_in[:, cbk * 128:(cbk + 1) * 128])
                wz = self.load_w(w_in[:, 2 * E + cbk * 128:2 * E + (cbk + 1) * 128])
                for tc in range(TC):
                    for _ in range(2):
                        if pend:
                            pend.pop(0)()
                    ppu = self.proj_fm(wu, 0, self.hT, tc)
                    a = self.gelu_from(ppu, None)
                    gu = self.f32()
                    S.op("dve", lambda: nc.vector.tensor_tensor(gu.t[:, :], a.t[:, :], ppu.t[:, :], ALU.mult), reads=[a, ppu], writes=[gu])
                    ppz = self.proj_fm(wz, 0, self.hT, tc)
                    zs = self.f32()
                    S.op("act", lambda: nc.scalar.activation(zs.t[:, :], ppz.t[:, :], AF.Silu), reads=[ppz], writes=[zs])

                    def fn():
                        ins = None
                        for q in range(4):
                            n = tc * 4 + q
                            nc.tensor.matmul(self.pol.t[:, q * 128:(q + 1) * 128], gt3[:, n, cbl * 128:(cbl + 1) * 128], wmT.t[:, g * 128:(g + 1) * 128], start=True, stop=False)
                            ins = nc.tensor.matmul(self.pol.t[:, q * 128:(q + 1) * 128], L2.t[0:2, cbk * 128:(cbk + 1) * 128], RB.t[0:2, g * 128:(g + 1) * 128], start=False, stop=True)
                        return ins
                    S.op("pe", fn, reads=[gtile, wmT, L2, RB], writes=[self.pol])
                    S.op("dve", lambda: nc.vector.tensor_tensor(gu.t[:, :], gu.t[:, :], self.pol.t[:, :], ALU.mult), reads=[gu, self.pol], writes=[gu])
                    S.op("dve", lambda: nc.vector.tensor_tensor(ybuf.t[:, tc * 512:(tc + 1) * 512], gu.t[:, :], zs.t[:, :], ALU.mult), reads=[gu, zs], writes=[ybuf])
                S.dma("sp", self.ysc[cbk * 128:(cbk + 1) * 128, :], ybuf.t[:, :], reads=[ybuf], writes=[self.ysc_res[cbk]])
        self.out_proj(li, self.sgu_w_out[j], 2)

    def conv_layer(self, li, j):
        nc, S = self.nc, self.S
        w_in = self.conv_w_in[j]
        self.a_reset()
        PADW = 32
        gpad = [self.a_bf16(f"gpad{i}", PADW + T) for i in range(2)]
        diag = [self.a_bf16(f"diag{i}", 31 * 128) for i in range(2)]
        dwt = self.a_f32("dwt", 32 * 31)
        vec = self.a_f32("vec", 96)
        ssum = self.a_f32("csum", T)
        ssq = self.a_f32("csq", T)
        g2b = [self.a_bf16(f"g2b{i}", T) for i in range(2)]
        ybuf = self.a_bf16("cybuf", T)
        tmpf = self.a_f32("tmpf", T)
        S.dma("sp", dwt.t[:, :], self.conv_dwT.rearrange("p c k -> p (c k)"), writes=[dwt])
        S.dma("sp", vec.t[:, :], self.conv_vecT.rearrange("p w c -> p (w c)"), writes=[vec])
        for i in range(2):
            S.op("dve", lambda: nc.vector.memset(gpad[i].t[:, 0:PADW], 0.0), writes=[gpad[i]])
        pipe = Pipe(depth=2)

        def mkc1(cbk):
            gp = gpad[cbk % 2]
            dg = diag[cbk % 2]
            g2 = g2b[cbk % 2]

            def s0():
                self.ada_step(1)
                wa = self.load_w(w_in[:, cbk * 128:(cbk + 1) * 128])
                wb = self.load_w(w_in[:, E + cbk * 128:E + (cbk + 1) * 128])
                for k in range(31):
                    S.op("dve", lambda: nc.vector.tensor_scalar(dg.t[:, k * 128:(k + 1) * 128], self.ident, dwt.t[:, cbk * 31 + k:cbk * 31 + k + 1], None, ALU.mult),
                         reads=[self.cb, dwt], writes=[dg])
                for tc in range(TC):
                    ppa = self.proj_fm(wa, 0, self.hT, tc)
                    ppb = self.proj_fm(wb, 0, self.hT, tc)
                    sg = self.f32()
                    S.op("act", lambda: nc.scalar.activation(sg.t[:, :], ppb.t[:, :], AF.Sigmoid), reads=[ppb], writes=[sg])
                    S.op("dve", lambda: nc.vector.tensor_tensor(gp.t[:, PADW + tc * 512:PADW + (tc + 1) * 512], sg.t[:, :], ppa.t[:, :], ALU.mult), reads=[sg, ppa], writes=[gp])

            def s1():
                for tc in range(TC):
                    pc = self.psx[tc % 2]

                    def fn():
                        ins = None
                        for k in range(31):
                            o = PADW + tc * 512 + k - 30
                            ins = nc.tensor.matmul(pc.t[:, :], dg.t[:, k * 128:(k + 1) * 128], gp.t[:, o:o + 512], start=(k == 0), stop=(k == 30))
                        return ins
                    S.op("pe", fn, reads=[dg, gp], writes=[pc])
                    S.op("act", lambda: nc.scalar.activation(g2.t[:, tc * 512:(tc + 1) * 512], pc.t[:, :], AF.Identity, bias=vec.t[:, cbk:cbk + 1], scale=1.0), reads=[pc, vec], writes=[g2])
                    sq = self.b16()
                    S.op("act", lambda: nc.scalar.activation(sq.t[:, :], g2.t[:, tc * 512:(tc + 1) * 512], AF.Square), reads=[g2], writes=[sq])
                    S.op("pe", lambda: nc.tensor.matmul(self.pa.t[:, :], self.onesE.t[:, :], g2.t[:, tc * 512:(tc + 1) * 512], start=True, stop=True), reads=[g2, self.onesE], writes=[self.pa])
                    S.op("pe", lambda: nc.tensor.matmul(self.pb.t[:, :], self.onesE.t[:, :], sq.t[:, :], start=True, stop=True), reads=[sq, self.onesE], writes=[self.pb])
                    for pacc, acc in ((self.pa, ssum), (self.pb, ssq)):
                        if cbk == 0:
                            S.op("act", lambda: nc.scalar.activation(acc.t[:, tc * 512:(tc + 1) * 512], pacc.t[:, :], AF.Copy), reads=[pacc], writes=[acc])
                        else:
                            S.op("dve", lambda: nc.vector.tensor_tensor(acc.t[:, tc * 512:(tc + 1) * 512], acc.t[:, tc * 512:(tc + 1) * 512], pacc.t[:, :], ALU.add), reads=[pacc, acc], writes=[acc])
                S.dma("sp", self.g2sc[cbk * 128:(cbk + 1) * 128, :], g2.t[:, :], reads=[g2], writes=[self.g2_res[cbk]])
            return [s0, s1]
        for cbk in range(32):
            pipe.push(mkc1(cbk))
        pipe.flush()
        S.op("dve", lambda: nc.vector.tensor_tensor(tmpf.t[:, :], ssum.t[:, :], ssum.t[:, :], ALU.mult), reads=[ssum], writes=[tmpf])
        S.op("dve", lambda: nc.vector.tensor_tensor(ssq.t[:, :], ssq.t[:, :], tmpf.t[:, :], ALU.subtract), reads=[ssq, tmpf], writes=[ssq])
        S.op("act", lambda: nc.scalar.activation(ssq.t[:, :], ssq.t[:, :], AF.Ln, bias=self.epsb.t[:, 0:1], scale=1.0), reads=[ssq, self.epsb], writes=[ssq])
        S.op("act", lambda: nc.scalar.activation(ssq.t[:, :], ssq.t[:, :], AF.Exp, scale=-0.5), reads=[ssq], writes=[ssq])
        S.op("dve", lambda: nc.vector.scalar_tensor_tensor(ssum.t[:, :], ssum.t[:, :], -1.0, ssq.t[:, :], ALU.mult, ALU.mult), reads=[ssum, ssq], writes=[ssum])
        for cbk in range(32):
            if cbk < 16:
                self.ada_step(1)
            wz = self.load_w(w_in[:, 2 * E + cbk * 128:2 * E + (cbk + 1) * 128])
            g2 = g2b[cbk % 2]
            if cbk == 0:
                S.dma("sp", g2.t[:, :], self.g2sc[0:128, :], reads=[self.g2_res[0]], writes=[g2])
            if cbk + 1 < 32:
                S.dma("sp", g2b[(cbk + 1) % 2].t[:, :], self.g2sc[(cbk + 1) * 128:(cbk + 2) * 128, :], reads=[self.g2_res[cbk + 1]], writes=[g2b[(cbk + 1) % 2]])
            for tc in range(TC):
                sl = slice(tc * 512, (tc + 1) * 512)
                t1 = self.f32()
                S.op("dve", lambda: nc.vector.tensor_tensor(t1.t[:, :], g2.t[:, sl], ssq.t[:, sl], ALU.mult), reads=[g2, ssq], writes=[t1])
                S.op("dve", lambda: nc.vector.tensor_tensor(t1.t[:, :], t1.t[:, :], ssum.t[:, sl], ALU.add), reads=[t1, ssum], writes=[t1])
                S.op("act", lambda: nc.scalar.activation(t1.t[:, :], t1.t[:, :], AF.Silu, bias=vec.t[:, 64 + cbk:65 + cbk], scale=vec.t[:, 32 + cbk:33 + cbk]), reads=[t1, vec], writes=[t1])
                ppz = self.proj_fm(wz, 0, self.hT, tc)
                zs = self.f32()
                S.op("act", lambda: nc.scalar.activation(zs.t[:, :], ppz.t[:, :], AF.Silu), reads=[ppz], writes=[zs])
                S.op("dve", lambda: nc.vector.tensor_tensor(ybuf.t[:, sl], t1.t[:, :], zs.t[:, :], ALU.mult), reads=[t1, zs], writes=[ybuf])
            S.dma("sp", self.ysc[cbk * 128:(cbk + 1) * 128, :], ybuf.t[:, :], reads=[ybuf], writes=[self.ysc_res[cbk]])
        self.out_proj(li, self.conv_w_out[j], 2)

    def out_proj(self, li, w_out, nhalf):
        nc, S = self.nc, self.S
        last = (li == self.nlayers - 1)
        src = self.xT if li == 0 else self.xs
        dst = self.outT if last else self.xs
        S.barrier()
        self.a_reset()
        ysrc = [list(self.hT)]
        if nhalf == 2:
            ysrc.append([self.a_bf16(f"yh{kc}", T) for kc in range(KC)])
        for half in range(nhalf):
            for kc in range(KC):
                r = half * KC + kc
                S.dma("sp", ysrc[half][kc].t[:, :], self.ysc[r * 128:(r + 1) * 128, :], reads=[self.ysc_res[r]], writes=[ysrc[half][kc]])
        iters = [(cb, tc) for cb in range(16) for tc in range(TC)]
        xbs = {}
        PRE = 3

        def load(i):
            if i < len(iters):
                cb, tc = iters[i]
                xb = self.f32()
                S.dma("sp", xb.t[:], src[cb * 128:(cb + 1) * 128, tc * 512:(tc + 1) * 512], reads=[self.xs_res[cb][tc]], writes=[xb])
                xbs[i] = xb
        for i in range(PRE):
            load(i)
        wts = None
        for i, (cb, tc) in enumerate(iters):
            if tc == 0:
                self.ada_step(1)
                wts = [self.load_w(w_out[half * 2048:(half + 1) * 2048, cb * 128:(cb + 1) * 128]) for half in range(nhalf)]
            load(i + PRE)
            pp = self.nextpp()

            def fn():
                ins = None
                n = nhalf * KC
                k = 0
                for half in range(nhalf):
                    for kc in range(KC):
                        ins = nc.tensor.matmul(pp.t[:, :], wts[half].t[:, kc, :], ysrc[half][kc].t[:, tc * 512:(tc + 1) * 512],
                                               start=(k == 0), stop=(k == n - 1))
                        k += 1
                return ins
            S.op("pe", fn, reads=list(wts) + [t for h in ysrc for t in h], writes=[pp])
            xb = xbs.pop(i)
            S.op("dve", lambda: nc.vector.scalar_tensor_tensor(xb.t[:], pp.t[:], self.mod.t[:, 32 + cb:33 + cb], xb.t[:], ALU.mult, ALU.add),
                 reads=[pp, self.mod, xb], writes=[xb])
            S.dma("act", dst[cb * 128:(cb + 1) * 128, tc * 512:(tc + 1) * 512], xb.t[:], reads=[xb], writes=[self.xs_res[cb][tc]])

    def layers(self):
        self.ada_begin(0)
        for li in range(self.nlayers):
            self.S.barrier()
            self.ada_finish(li)
            self.norm(li)
            kind, j = li % 3, li // 3
            if kind == 0:
                self.attn_layer(li, j)
            elif kind == 1:
                self.sgu_layer(li, j)
            else:
                self.conv_layer(li, j)

    def build(self):
        real = self.S
        self.S = DrySched()
        self.dry = True
        self.layers()
        self.S = real
        self.dry = False
        self.f32i = self.bf16i = self.ppi = 0
        self.setup()
        self.S.barrier()
        self.layers()
        self.S.finish()
        return self.nc


class Pipe:
    def __init__(self, depth=4):
        self.q = []
        self.depth = depth

    def push(self, stages):
        self.q.insert(0, stages)
        for age, st in enumerate(self.q):
            if age < len(st) and st[age] is not None:
                st[age]()
        del self.q[self.depth:]

    def flush(self):
        for _ in range(self.depth):
            self.push([])


class DrySched:
    def op(self, *a, **k):
        return None

    def barrier(self):
        return None

    def dma(self, *a, **k):
        return None


def make_in_maps(inputs):
    f = lambda a: np.ascontiguousarray(np.asarray(a))
    x, c, positions = f(inputs["x"]), f(inputs["c"]), f(inputs["positions"])
    cmat, fcol = _consts()
    shared = {
        "ada_w": f(inputs["ada_w"]),
        "ada_bT": f(inputs["ada_b"]).reshape(4, 48, 128).transpose(0, 2, 1).copy(),
        "norm_gT": f(inputs["norm_g"]).reshape(4, KC, 128).transpose(0, 2, 1).copy(),
        "attn_w_in": f(inputs["attn_w_in"]),
        "attn_qg": f(inputs["attn_q_gain"]).reshape(2, 128, 1),
        "attn_kg": f(inputs["attn_k_gain"]).reshape(2, 128, 1),
        "attn_w_out": f(inputs["attn_w_out"]),
        "sgu_w_in": f(inputs["sgu_w_in"]),
        "sgu_ln_g": np.ascontiguousarray(np.broadcast_to(f(inputs["sgu_ln_g"]).reshape(1, E), (128, E))),
        "sgu_ln_b": f(inputs["sgu_ln_b"]).reshape(1, E),
        "sgu_wsT": f(inputs["sgu_ws"])[0].transpose(2, 0, 1).copy(),
        "sgu_bs2": np.concatenate([np.zeros((1, 2048), np.float32), f(inputs["sgu_bs"]).reshape(1, 16 * 128)], 0),
        "sgu_w_out": f(inputs["sgu_w_out"]),
        "conv_w_in": f(inputs["conv_w_in"]),
        "conv_w_out": f(inputs["conv_w_out"]),
        "conv_dwT": f(inputs["conv_dw_w"])[0].reshape(31, 32, 128).transpose(2, 1, 0).copy(),
        "conv_vecT": np.stack([f(inputs["conv_dw_b"])[0], f(inputs["conv_ln_g"])[0], f(inputs["conv_ln_b"])[0]], 0)
                        .reshape(3, 32, 128).transpose(2, 0, 1).copy(),
        "cmat": cmat, "fcol": fcol,
    }
    maps = [None] * NCORES
    for b in range(NB):
        m = dict(shared)
        m["xT"] = np.ascontiguousarray(x[b].T)
        m["cT"] = np.ascontiguousarray(c[b].reshape(KC, 128).T)
        m["pos"] = np.ascontiguousarray(np.broadcast_to(positions[b].astype(np.int32)[None, :], (128, T)))
        maps[REAL_CORES[b]] = m
    zero = {k: np.zeros_like(v) for k, v in maps[REAL_CORES[0]].items()}
    for i in range(NCORES):
        if maps[i] is None:
            maps[i] = zero
    return maps


_NC_CACHE = {}


def kernel(**inputs):
    nl = 4
    if nl not in _NC_CACHE:
        _NC_CACHE[nl] = Builder(nl).build()
    nc = _NC_CACHE[nl]
    maps = make_in_maps(inputs)
    res = run_bass_kernel_spmd(nc, maps, core_ids=list(range(NCORES)))
    out = np.stack([np.ascontiguousarray(res.results[REAL_CORES[b]]["outT"].T) for b in range(NB)], 0)
    return out.astype(np.float32)
```

```python
import numpy as np
import concourse.bass as bass
import concourse.mybir as mybir
from concourse.bass_utils import run_bass_kernel_spmd

F32 = mybir.dt.float32
BF16 = mybir.dt.bfloat16
AF = mybir.ActivationFunctionType
ALU = mybir.AluOpType


class Res:
    __slots__ = ("name", "t", "w", "r")

    def __init__(self, name, t=None):
        self.name = name
        self.t = t
        self.w = None
        self.r = {}


class Sched:
    COMPUTE = ("pe", "act", "dve", "pool")

    def __init__(self, nc, ndma=6):
        self.nc = nc
        self.eng = {"pe": nc.tensor, "act": nc.scalar, "dve": nc.vector,
                    "pool": nc.gpsimd, "sp": nc.sync}
        self.sems = {}
        for k in self.COMPUTE:
            self.sems[("c", k)] = nc.alloc_semaphore(name=f"c_{k}")
        self.ccnt = {k: 0 for k in self.COMPUTE}
        self.ndma = ndma
        self.dval = {}
        self.dnext = {}
        for q in ("sp", "pool", "act"):
            self.dnext[q] = 0
            for i in range(ndma):
                self.sems[("d", q, i)] = nc.alloc_semaphore(name=f"d_{q}_{i}")
                self.dval[("d", q, i)] = 0
        self.seen = {e: {} for e in self.eng}
        self.nwait = 0
        self.nops = 0
        self.trace = {e: [] for e in self.eng}
        self.pending = {e: [] for e in self.eng}

    def sb(self, name, shape, dtype):
        return Res(name, self.nc.alloc_sbuf_tensor(name, shape, dtype))

    def ps(self, name, shape, dtype):
        return Res(name, self.nc.alloc_psum_tensor(name, shape, dtype))

    def res(self, name):
        return Res(name)

    dram_res = res

    def _wait(self, e, ev):
        if ev is None:
            return
        key, val = ev
        if val <= 0:
            return
        if self.seen[e].get(key, 0) >= val:
            return
        if e == "pe" and key == ("c", "pe"):
            return
        self.eng[e].wait_ge(self.sems[key], val)
        self.seen[e][key] = val
        self.nwait += 1
        self.pending[e].append((key, val))

    def _deps(self, e, reads, writes):
        for r in reads:
            self._wait(e, r.w)
        for w in writes:
            self._wait(e, w.w)
            for key, val in w.r.items():
                self._wait(e, (key, val))

    def _mark(self, ev, reads, writes):
        key, val = ev
        for r in reads:
            if r.r.get(key, 0) < val:
                r.r[key] = val
        for w in writes:
            w.w = ev
            w.r = {}

    def op(self, e, fn, reads=(), writes=()):
        self._deps(e, reads, writes)
        inst = fn()
        self.ccnt[e] += 1
        key = ("c", e)
        inst.then_inc(self.sems[key], 1)
        ev = (key, self.ccnt[e])
        self._mark(ev, reads, writes)
        self.nops += 1
        self.trace[e].append((self.pending[e], (key, 1)))
        self.pending[e] = []
        return ev

    def _dma_like(self, q, fn, reads, writes):
        slot = self.dnext[q]
        self.dnext[q] = (slot + 1) % self.ndma
        key = ("d", q, slot)
        self._wait(q, (key, self.dval[key]))
        self._deps(q, reads, writes)
        inst = fn()
        inst.then_inc(self.sems[key], 16)
        self.dval[key] += 16
        ev = (key, self.dval[key])
        self._mark(ev, reads, writes)
        self.nops += 1
        self.trace[q].append((self.pending[q], (key, 16)))
        self.pending[q] = []
        return ev

    def dma(self, q, out, in_, reads=(), writes=()):
        return self._dma_like(q, lambda: self.eng[q].dma_start(out=out, in_=in_), reads, writes)

    def collective(self, fn, reads=(), writes=()):
        return self._dma_like("pool", fn, reads, writes)

    def barrier(self):
        for e in ("pe", "act", "dve", "sp"):
            for k in ("pe", "act", "dve"):
                self._wait(e, (("c", k), self.ccnt[k]))
            for q in ("sp", "act"):
                for i in range(self.ndma):
                    key = ("d", q, i)
                    self._wait(e, (key, self.dval[key]))

    def simulate(self):
        for e in self.eng:
            if self.pending[e]:
                self.trace[e].append((self.pending[e], None))
                self.pending[e] = []
        cnt = {k: 0 for k in self.sems}
        ptr = {e: 0 for e in self.eng}
        progress = True
        while progress:
            progress = False
            for e in self.eng:
                tr = self.trace[e]
                while ptr[e] < len(tr):
                    waits, inc = tr[ptr[e]]
                    if any(cnt[k] < v for k, v in waits):
                        break
                    if inc is not None:
                        cnt[inc[0]] += inc[1]
                    ptr[e] += 1
                    progress = True
        stuck = {e: (ptr[e], len(self.trace[e]), self.trace[e][ptr[e]][0]) for e in self.eng if ptr[e] < len(self.trace[e])}
        return stuck, cnt

    def finish(self):
        for key, val in self.dval.items():
            self._wait("sp", (key, val))
        for k in self.COMPUTE:
            self._wait("sp", (("c", k), self.ccnt[k]))


D = 2048
T = 2048
NB = 4
NCORES = 4
REAL_CORES = (0, 1, 2, 3)
KC = 16
TC = 4
E = 4096
NEG = -30000.0
EPS = 1e-6
SCALE = 128.0 ** -0.5
TWO_PI_HI = 6.28125
TWO_PI_LO = 2.0 * np.pi - 6.28125
DILS = (1, 4, 16)


def _consts():
    ident = np.eye(128, dtype=np.float32)
    prot = np.zeros((128, 128), np.float32)
    for p in range(16):
        prot[p + 16, p] = -1.0
        prot[p, p + 16] = 1.0
    j = np.arange(128)[:, None]
    i = np.arange(128)[None, :]
    maskb = np.concatenate([np.where(j <= i, 0.0, NEG), np.where(j >= i, 0.0, NEG)], 1).astype(np.float32)
    inv_freq = (500000.0 ** (-np.arange(0, 32, 2, dtype=np.float32) / 32.0)).astype(np.float32)
    fcol = np.zeros((128, 1), np.float32)
    fcol[:32, 0] = np.concatenate([inv_freq, inv_freq])
    tril_st = (j <= i).astype(np.float32)
    cmat = np.concatenate([ident, prot, maskb, tril_st], 1)
    return cmat, fcol


class Builder:
    def __init__(self, nlayers=4):
        self.nlayers = nlayers
        nc = self.nc = bass.Bass("TRN2", target_bir_lowering=False)
        S = self.S = Sched(nc)
        dt = nc.dram_tensor
        def inp(name, shape, dtype=F32):
            return dt(name, list(shape), dtype, kind="ExternalInput").ap()
        self.xT = inp("xT", [D, T])
        self.cT = inp("cT", [128, KC])
        self.pos = inp("pos", [128, T], mybir.dt.int32)
        self.ada_w = inp("ada_w", [4, D, 3 * D])
        self.ada_bT = inp("ada_bT", [4, 128, 48])
        self.norm_gT = inp("norm_gT", [4, 128, KC])
        self.attn_w_in = inp("attn_w_in", [2, D, 20480])
        self.attn_qg = inp("attn_qg", [2, 128, 1])
        self.attn_kg = inp("attn_kg", [2, 128, 1])
        self.attn_w_out = inp("attn_w_out", [2, D, D])
        self.sgu_w_in = inp("sgu_w_in", [1, D, 3 * E])
        self.sgu_ln_g = inp("sgu_ln_g", [128, E])
        self.sgu_ln_b = inp("sgu_ln_b", [1, E])
        self.sgu_wsT = inp("sgu_wsT", [128, 16, 128])
        self.sgu_bs2 = inp("sgu_bs2", [2, 16 * 128])
        self.sgu_w_out = inp("sgu_w_out", [1, E, D])
        self.conv_w_in = inp("conv_w_in", [1, D, 3 * E])
        self.conv_w_out = inp("conv_w_out", [1, E, D])
        self.conv_dwT = inp("conv_dwT", [128, 32, 31])
        self.conv_vecT = inp("conv_vecT", [128, 3, 32])
        self.cmat = inp("cmat", [128, 640])
        self.fcol_in = inp("fcol", [128, 1])
        self.outT = dt("outT", [D, T], F32, kind="ExternalOutput").ap()
        self.xs = dt("xs", [D, T], F32, kind="Internal").ap()
        self.ysc = dt("ysc", [E, T], BF16, kind="Internal").ap()
        self.gsc = dt("gsc", [T, E], BF16, kind="Internal").ap()
        self.g2sc = dt("g2sc", [E, T], BF16, kind="Internal").ap()
        self.xs_res = [[S.res(f"xs{c}_{t}") for t in range(TC)] for c in range(KC)]
        self.ysc_res = [S.res(f"ysc{c}") for c in range(32)]
        self.gsc_res = [S.res(f"gsc{c}") for c in range(16)]
        self.g2_res = [S.res(f"g2sc{c}") for c in range(32)]
        self.alloc()

    def alloc(self):
        S = self.S
        self.hT = [S.sb(f"hT{kc}", [128, T], BF16) for kc in range(KC)]
        self.NW = 8
        self.wt = [S.sb(f"wt{i}", [128, KC, 128], BF16) for i in range(self.NW)]
        self.wplan = []
        self.wissued = 0
        self.wused = 0
        self.dry = False
        self.pp = [S.ps(f"pp{i}", [128, 512], F32) for i in range(3)]
        self.pa = S.ps("pa", [128, 512], F32)
        self.pb = S.ps("pb", [128, 512], F32)
        self.psx = [S.ps(f"psx{i}", [128, 512], F32) for i in range(2)]
        self.pol = S.ps("pol", [128, 512], F32)
        self.ppi = 0
        self.cb = S.sb("cb", [128, 640], BF16)
        self.ident = self.cb.t[:, 0:128]
        self.prot = self.cb.t[:, 128:256]
        self.maskb = self.cb.t[:, 256:512]
        self.tril = self.cb.t[:, 512:640]
        self.onesD = S.sb("onesD", [128, 128], BF16)
        self.onesH = S.sb("onesH", [128, 128], BF16)
        self.ones1 = S.sb("ones1", [128, 128], BF16)
        self.onesE = S.sb("onesE", [128, 128], BF16)
        self.fcol = S.sb("fcolsb", [128, 1], F32)
        self.cosT = S.sb("cosT", [128, T], F32)
        self.sinT = S.sb("sinT", [128, T], F32)
        self.cact = S.sb("cact", [128, KC], BF16)
        self.mod = S.sb("mod", [128, 48], F32)
        self.modn = S.sb("modn", [128, 48], F32)
        self.avec = S.sb("avec", [128, KC], F32)
        self.adab = S.sb("adab", [128, 48], F32)
        self.ng = S.sb("ng", [128, KC], F32)
        self.gain = S.sb("gain", [128, 2], F32)
        self.epsb = S.sb("epsb", [128, 1], F32)
        self.rstdn = [S.sb(f"rstdn{i}", [128, 512], F32) for i in range(2)]
        self.f32s = [S.sb(f"f32s{i}", [128, 512], F32) for i in range(6)]
        self.f32i = 0
        self.bf16s = [S.sb(f"bf16s{i}", [128, 512], BF16) for i in range(4)]
        self.bf16i = 0
        self.ARENA_F32 = 17664
        self.arena = S.sb("arena", [128, self.ARENA_F32], F32)
        self.aoff = 0

    def a_reset(self):
        self.aoff = 0

    def a_f32(self, name, n):
        ap = self.arena.t[:, self.aoff:self.aoff + n]
        self.aoff += n
        assert self.aoff <= self.ARENA_F32, (name, self.aoff)
        return Res(name, ap)

    def a_bf16(self, name, n):
        assert n % 2 == 0
        ap = self.arena.t[:, self.aoff:self.aoff + n // 2].bitcast(BF16)
        self.aoff += n // 2
        assert self.aoff <= self.ARENA_F32, (name, self.aoff)
        return Res(name, ap)

    def a_i32(self, name, n):
        ap = self.arena.t[:, self.aoff:self.aoff + n].bitcast(mybir.dt.int32)
        self.aoff += n
        assert self.aoff <= self.ARENA_F32, (name, self.aoff)
        return Res(name, ap)

    def f32(self):
        r = self.f32s[self.f32i % len(self.f32s)]
        self.f32i += 1
        return r

    def b16(self):
        r = self.bf16s[self.bf16i % len(self.bf16s)]
        self.bf16i += 1
        return r

    def nextpp(self):
        r = self.pp[self.ppi % 3]
        self.ppi += 1
        return r

    def load_w(self, wap):
        if self.dry:
            self.wplan.append(wap)
            return self.wt[0]
        u = self.wused
        self.wused += 1
        while self.wissued < len(self.wplan) and self.wissued < u + self.NW - 1:
            i = self.wissued
            t = self.wt[i % self.NW]
            self.S.dma("pool", t.t[:], self.wplan[i].rearrange("(kc p) n -> p kc n", p=128), writes=[t])
            self.wissued += 1
        return self.wt[u % self.NW]

    def proj_fm(self, wt, c0, src, tc, n=512, pp=None):
        nc, S = self.nc, self.S
        if pp is None:
            pp = self.nextpp()

        def fn():
            ins = None
            for kc in range(KC):
                ins = nc.tensor.matmul(pp.t[:, 0:n], wt.t[:, kc, c0:c0 + 128],
                                       src[kc].t[:, tc * n:(tc + 1) * n],
                                       start=(kc == 0), stop=(kc == KC - 1))
            return ins
        S.op("pe", fn, reads=[wt] + list(src), writes=[pp])
        return pp

    def setup(self):
        nc, S = self.nc, self.S
        S.dma("pool", self.cb.t[:], self.cmat, writes=[self.cb])
        S.dma("sp", self.fcol.t[:], self.fcol_in, writes=[self.fcol])
        for t, v in ((self.onesD, 1.0 / D), (self.onesH, 1.0 / 128), (self.ones1, 1.0), (self.onesE, 1.0 / E)):
            S.op("dve", lambda t=t, v=v: nc.vector.memset(t.t[:], v), writes=[t])
        S.op("dve", lambda: nc.vector.memset(self.epsb.t[:], EPS), writes=[self.epsb])
        c32 = self.f32()
        S.dma("sp", c32.t[:, 0:KC], self.cT, writes=[c32])
        S.op("act", lambda: nc.scalar.activation(self.cact.t[:], c32.t[:, 0:KC], AF.Silu), reads=[c32], writes=[self.cact])
        self.a_reset()
        posi = self.a_i32("posi", T)
        S.dma("sp", posi.t[:], self.pos, writes=[posi])
        ang, ki, kf, r = self.a_f32("ang", T), self.a_i32("ki", T), self.a_f32("kf", T), self.a_f32("r", T)
        S.op("dve", lambda: nc.vector.tensor_copy(ang.t[:], posi.t[:]), reads=[posi], writes=[ang])
        S.op("dve", lambda: nc.vector.tensor_scalar(ang.t[:], ang.t[:], self.fcol.t[:, 0:1], None, ALU.mult), reads=[ang, self.fcol], writes=[ang])
        for tab, shift in ((self.sinT, 0.0), (self.cosT, np.pi / 2)):
            S.op("dve", lambda: nc.vector.tensor_scalar(r.t[:], ang.t[:], float(shift), None, ALU.add), reads=[ang], writes=[r])
            S.op("dve", lambda: nc.vector.tensor_scalar(ki.t[:], r.t[:], float(1.0 / (2 * np.pi)), None, ALU.mult), reads=[r], writes=[ki])
            S.op("dve", lambda: nc.vector.tensor_copy(kf.t[:], ki.t[:]), reads=[ki], writes=[kf])
            S.op("dve", lambda: nc.vector.scalar_tensor_tensor(r.t[:], kf.t[:], -TWO_PI_HI, r.t[:], ALU.mult, ALU.add), reads=[kf, r], writes=[r])
            S.op("dve", lambda: nc.vector.scalar_tensor_tensor(r.t[:], kf.t[:], -float(TWO_PI_LO), r.t[:], ALU.mult, ALU.add), reads=[kf, r], writes=[r])
            S.op("dve", lambda: nc.vector.tensor_scalar(kf.t[:], r.t[:], float(np.pi), None, ALU.is_gt), reads=[r], writes=[kf])
            S.op("dve", lambda: nc.vector.scalar_tensor_tensor(r.t[:], kf.t[:], -float(2 * np.pi), r.t[:], ALU.mult, ALU.add), reads=[kf, r], writes=[r])
            S.op("dve", lambda: nc.vector.tensor_scalar(kf.t[:], r.t[:], -float(np.pi), None, ALU.is_lt), reads=[r], writes=[kf])
            S.op("dve", lambda: nc.vector.scalar_tensor_tensor(r.t[:], kf.t[:], float(2 * np.pi), r.t[:], ALU.mult, ALU.add), reads=[kf, r], writes=[r])
            S.op("dve", lambda: nc.vector.tensor_scalar(r.t[:], r.t[:], -3.1415925, 3.1415925, ALU.max, ALU.min), reads=[r], writes=[r])
            S.op("act", lambda tab=tab: nc.scalar.activation(tab.t[:], r.t[:], AF.Sin), reads=[r], writes=[tab])

    def ada_begin(self, li):
        self.ada_li = li
        self.ada_j = 0

    def ada_step(self, n=1):
        nc, S = self.nc, self.S
        li = self.ada_li
        for _ in range(n):
            if li is None or li >= self.nlayers or self.ada_j >= 48:
                return
            j = self.ada_j
            self.ada_j += 1
            wt = self.load_w(self.ada_w[li, :, j * 128:(j + 1) * 128])
            pm = self.pa

            def fn():
                ins = None
                for kc in range(KC):
                    ins = nc.tensor.matmul(pm.t[:, 0:1], wt.t[:, kc, :], self.cact.t[:, kc:kc + 1], start=(kc == 0), stop=(kc == KC - 1))
                return ins
            S.op("pe", fn, reads=[wt, self.cact], writes=[pm])
            S.op("dve", lambda: nc.vector.tensor_copy(self.modn.t[:, j:j + 1], pm.t[:, 0:1]), reads=[pm], writes=[self.modn])

    def ada_finish(self, li):
        nc, S = self.nc, self.S
        assert self.ada_li == li
        self.ada_step(48)
        S.dma("sp", self.adab.t[:], self.ada_bT[li], writes=[self.adab])
        S.dma("sp", self.ng.t[:], self.norm_gT[li], writes=[self.ng])
        S.op("dve", lambda: nc.vector.tensor_tensor(self.mod.t[:], self.modn.t[:], self.adab.t[:], ALU.add),
             reads=[self.modn, self.adab], writes=[self.mod])
        S.op("dve", lambda: nc.vector.scalar_tensor_tensor(self.avec.t[:], self.mod.t[:, 16:32], 1.0, self.ng.t[:], ALU.add, ALU.mult),
             reads=[self.mod, self.ng], writes=[self.avec])
        self.ada_begin(li + 1)

    def norm(self, li):
        nc, S = self.nc, self.S
        src = self.xT if li == 0 else self.xs
        self.a_reset()
        xch = [[self.a_f32(f"xch{i}_{kc}", 512) for kc in range(KC)] for i in range(2)]
        for tc in range(TC):
            pm = self.pa
            xc = xch[tc % 2]
            for kc in range(KC):
                xb = xc[kc]
                S.dma("sp", xb.t[:], src[kc * 128:(kc + 1) * 128, tc * 512:(tc + 1) * 512],
                      reads=[self.xs_res[kc][tc]], writes=[xb])
                sq = self.b16()
                S.op("act", lambda xb=xb, sq=sq: nc.scalar.activation(sq.t[:], xb.t[:], AF.Square), reads=[xb], writes=[sq])
                S.op("pe", lambda sq=sq, kc=kc: nc.tensor.matmul(pm.t[:], self.onesD.t[:], sq.t[:], start=(kc == 0), stop=(kc == KC - 1)),
                     reads=[sq, self.onesD], writes=[pm])
            rstd = self.rstd_from(pm, out=self.rstdn[tc % 2])
            for kc in range(KC):
                xb = xc[kc]
                S.op("dve", lambda xb=xb, kc=kc: nc.vector.scalar_tensor_tensor(xb.t[:], xb.t[:], self.avec.t[:, kc:kc + 1], rstd.t[:], ALU.mult, ALU.mult),
                     reads=[xb, self.avec, rstd], writes=[xb])
                S.op("act", lambda xb=xb, kc=kc, tc=tc: nc.scalar.activation(self.hT[kc].t[:, tc * 512:(tc + 1) * 512], xb.t[:], AF.Identity,
                                                                       bias=self.mod.t[:, kc:kc + 1], scale=1.0),
                     reads=[xb, self.mod], writes=[self.hT[kc]])
        self.S.barrier()

    def rstd_from(self, pm, n=512, out=None):
        nc, S = self.nc, self.S
        r = out if out is not None else self.f32()
        S.op("act", lambda: nc.scalar.activation(r.t[:, 0:n], pm.t[:, 0:n], AF.Ln, bias=self.epsb.t[:, 0:1], scale=1.0), reads=[pm, self.epsb], writes=[r])
        S.op("act", lambda: nc.scalar.activation(r.t[:, 0:n], r.t[:, 0:n], AF.Exp, scale=-0.5), reads=[r], writes=[r])
        return r

    def perm_out(self, buf, dil, tc):
        if dil == 1:
            return buf.t[:, tc * 512:(tc + 1) * 512]
        L = T // dil
        n = 512 // dil
        return buf.t[:, :].rearrange("p (r m) -> p r m", r=dil)[:, :, tc * n:(tc + 1) * n]

    def perm_in(self, ap512, dil):
        if dil == 1:
            return ap512
        return ap512.rearrange("p (m r) -> p r m", r=dil)

    def nat_ap(self, buf, dil, b4):
        if dil == 1:
            return buf.t[:, b4 * 512:(b4 + 1) * 512]
        if dil == 4:
            return buf.t[:, :].rearrange("p (m r) -> p r m", r=4)[:, b4, :]
        return buf.t[:, :].rearrange("p (m r) -> p r m", r=16)[:, 4 * b4:4 * b4 + 4, :]

    def blk_view(self, ps, dil):
        if dil == 16:
            return ps.t[:, :].rearrange("p (r m) -> p r m", r=4)
        return ps.t[:, :]

    def nat_ap2(self, buf, dil, b0):
        if dil == 1:
            return buf.t[:, b0 * 128:b0 * 128 + 256]
        if dil == 4:
            r, m0 = b0 // 4, (b0 % 4) * 128
            return buf.t[:, :].rearrange("p (m r) -> p r m", r=4)[:, r, m0:m0 + 256]
        return buf.t[:, :].rearrange("p (m r) -> p r m", r=16)[:, b0:b0 + 2, :]

    def blk_view2(self, ap256, dil):
        if dil == 16:
            return ap256.rearrange("p (r m) -> p r m", r=2)
        return ap256

    def attn_layer(self, li, j):
        nc, S = self.nc, self.S
        w_in = self.attn_w_in[j]
        S.dma("sp", self.gain.t[:, 0:1], self.attn_qg[j], writes=[self.gain])
        S.dma("sp", self.gain.t[:, 1:2], self.attn_kg[j], writes=[self.gain])
        self.a_reset()
        qf = [self.a_bf16(f"qf{i}", T) for i in range(2)]
        kf = [self.a_bf16(f"kf{i}", T) for i in range(2)]
        vtm = [self.a_bf16(f"vtm{i}", T) for i in range(2)]
        zs = [self.a_bf16(f"zs{i}", T) for i in range(2)]
        vT = self.a_bf16("vT", T)
        ybuf = self.a_bf16("ybuf", T)
        oacc, lacc = self.a_f32("oacc", T), self.a_f32("lacc", T)
        pT = [self.a_bf16(f"pT{i}", 256) for i in range(6)]
        kgp = [self.a_bf16(f"kg{i}", 512) for i in range(3)]
        pipe = Pipe()
        units = [(h, g) for h in range(16) for g in range(3)]
        jobc = [0]

        def a_jobs(u):
            h, g = units[u]
            dil = DILS[g]
            par = u % 2
            jobs = []

            def mk(name, base, tc, shared):
                jid = jobc[0]
                jobc[0] += 1
                pp = self.pp[jid % 3]
                kg = kgp[jid % 3]
                c = base + (g * 2048 if name != "z" else 0) + h * 128

                def s0():
                    if tc == 0:
                        shared["wt"] = self.load_w(w_in[:, c:c + 128])
                    self.proj_fm(shared["wt"], 0, self.hT, tc, pp=pp)
                if name in ("k", "q"):
                    gcol = 1 if name == "k" else 0
                    dst = kf[par] if name == "k" else qf[par]

                    def s1():
                        sq = self.b16()
                        S.op("act", lambda: nc.scalar.activation(sq.t[:], pp.t[:], AF.Square), reads=[pp], writes=[sq])
                        S.op("pe", lambda: nc.tensor.matmul(self.pa.t[:], self.onesH.t[:], sq.t[:], start=True, stop=True),
                             reads=[sq, self.onesH], writes=[self.pa])
                        rstd = self.rstd_from(self.pa)
                        S.op("dve", lambda: nc.vector.scalar_tensor_tensor(kg.t[:], pp.t[:], self.gain.t[:, gcol:gcol + 1], rstd.t[:], ALU.mult, ALU.mult),
                             reads=[pp, self.gain, rstd], writes=[kg])

                    def s2():
                        S.op("pe", lambda: nc.tensor.matmul(self.pb.t[:], self.prot, kg.t[:], start=True, stop=True),
                             reads=[kg, self.cb], writes=[self.pb])
                        t1 = self.f32()
                        t2 = self.f32()
                        S.op("dve", lambda: nc.vector.tensor_tensor(t1.t[:], kg.t[:], self.cosT.t[:, tc * 512:(tc + 1) * 512], ALU.mult),
                             reads=[kg, self.cosT], writes=[t1])
                        S.op("dve", lambda: nc.vector.tensor_tensor(t2.t[:], self.pb.t[:], self.sinT.t[:, tc * 512:(tc + 1) * 512], ALU.mult),
                             reads=[self.pb, self.sinT], writes=[t2])
                        S.op("dve", lambda: nc.vector.tensor_tensor(self.perm_out(dst, dil, tc), self.perm_in(t1.t[:, :], dil), self.perm_in(t2.t[:, :], dil), ALU.add),
                             reads=[t1, t2], writes=[dst])
                    return [s0, s1, s2]
                if name == "v":
                    def s1():
                        S.op("act", lambda: nc.scalar.activation(self.perm_out(vT, dil, tc), self.perm_in(pp.t[:, :], dil), AF.Copy),
                             reads=[pp], writes=[vT])

                    def s2():
                        for b4 in range(4):
                            ps = self.pa if b4 % 2 == 0 else self.pb

                            def fn():
                                ins = None
                                for q in range(4):
                                    b = b4 * 4 + q
                                    ins = nc.tensor.matmul(ps.t[:, q * 128:(q + 1) * 128], vT.t[:, b * 128:(b + 1) * 128], self.ident, start=True, stop=True)
                                return ins
                            S.op("pe", fn, reads=[vT, self.cb], writes=[ps])
                            S.op("act", lambda: nc.scalar.activation(vtm[par].t[:, b4 * 512:(b4 + 1) * 512], ps.t[:, :], AF.Copy), reads=[ps], writes=[vtm[par]])
                    return [s0, s1, s2 if tc == 3 else None]
                def s1z():
                    S.op("act", lambda: nc.scalar.activation(zs[h % 2].t[:, tc * 512:(tc + 1) * 512], pp.t[:, :], AF.Silu), reads=[pp], writes=[zs[h % 2]])
                return [s0, s1z]

            for name, base in (("k", 6144), ("q", 0), ("v", 12288)) + ((("z", 18432),) if g == 2 else ()):
                shared = {}
                for tc in range(TC):
                    jobs.append(mk(name, base, tc, shared))
            return jobs

        def b_steps(u):
            h, g = units[u]
            dil = DILS[g]
            nb = 16 // dil
            par = u % 2
            kfin, qfin, vt = kf[par], qf[par], vtm[par]

            def score(b):
                jj = b % nb
                n = 256 if jj + 1 < nb else 128
                ps = self.psx[b % 2]
                p = pT[b % 6]

                def fn():
                    nc.tensor.matmul(ps.t[:, 0:n], self.ident, self.maskb[:, 0:n], start=True, stop=False)
                    return nc.tensor.matmul(ps.t[:, 0:n], kfin.t[:, b * 128:(b + 1) * 128], qfin.t[:, b * 128:b * 128 + n], start=False, stop=True)
                S.op("pe", fn, reads=[self.cb, kfin, qfin], writes=[ps])
                S.op("act", lambda: nc.scalar.activation(p.t[:, 0:n], ps.t[:, 0:n], AF.Exp, scale=SCALE), reads=[ps], writes=[p])

            def pv(b):
                jj = b % nb
                q2 = b % 2
                p = pT[b % 6]
                pprev = pT[(b - 1) % 6]
                for which in (0, 1):
                    col = which * 256 + q2 * 128

                    def fn2():
                        first = True
                        if jj > 0:
                            l = vt.t[:, (b - 1) * 128:b * 128] if which == 0 else self.ones1.t[:, :]
                            nc.tensor.matmul(self.pol.t[:, col:col + 128], l, pprev.t[:, 128:256], start=True, stop=False)
                            first = False
                        l = vt.t[:, b * 128:(b + 1) * 128] if which == 0 else self.ones1.t[:, :]
                        return nc.tensor.matmul(self.pol.t[:, col:col + 128], l, p.t[:, 0:128], start=first, stop=True)
                    S.op("pe", fn2, reads=[vt, self.ones1, p] + ([pprev] if jj > 0 else []), writes=[self.pol])
                if q2 == 1:
                    b0 = b - 1
                    for which, acc in ((0, oacc), (1, lacc)):
                        dst_ap = self.nat_ap2(acc, dil, b0)
                        src_ap = self.blk_view2(self.pol.t[:, which * 256:(which + 1) * 256], dil)
                        if g == 0:
                            S.op("act", lambda: nc.scalar.activation(dst_ap, src_ap, AF.Copy), reads=[self.pol], writes=[acc])
                        else:
                            S.op("dve", lambda: nc.vector.tensor_tensor(dst_ap, dst_ap, src_ap, ALU.add), reads=[self.pol, acc], writes=[acc])

            def mkstep(k):
                def st():
                    for b in (2 * k, 2 * k + 1):
                        if b < 16:
                            score(b)
                    for b in (2 * k - 2, 2 * k - 1):
                        if 0 <= b < 16:
                            pv(b)
                return st
            steps = [mkstep(k) for k in range(9)]
            if g == 2:
                def comb():
                    S.op("dve", lambda: nc.vector.reciprocal(lacc.t[:, :], lacc.t[:, :]), reads=[lacc], writes=[lacc])
                    S.op("dve", lambda: nc.vector.tensor_tensor(oacc.t[:, :], oacc.t[:, :], lacc.t[:, :], ALU.mult), reads=[oacc, lacc], writes=[oacc])
                    S.op("dve", lambda: nc.vector.tensor_tensor(ybuf.t[:, :], oacc.t[:, :], zs[h % 2].t[:, :], ALU.mult), reads=[oacc, zs[h % 2]], writes=[ybuf])
                    S.dma("sp", self.ysc[h * 128:(h + 1) * 128, :], ybuf.t[:, :], reads=[ybuf], writes=[self.ysc_res[h]])
                steps.append(comb)
            return steps

        NU = len(units)
        LAG = 3
        for w in range(NU + 1):
            aj = a_jobs(w) if w < NU else []
            bs = b_steps(w - 1) if w >= 1 else []
            n = max(len(aj), (LAG + len(bs)) if bs else 0)
            self.ada_step(1)
            for i in range(n):
                pipe.push(aj[i] if i < len(aj) else [])
                if bs and LAG <= i < LAG + len(bs):
                    bs[i - LAG]()
        pipe.flush()
        self.out_proj(li, self.attn_w_out[j], 1)

    def gelu_from(self, pp, out_ap, n=512):
        nc, S = self.nc, self.S
        a = self.f32()
        S.op("act", lambda: nc.scalar.activation(a.t[:, 0:n], pp.t[:, 0:n], AF.Square), reads=[pp], writes=[a])
        S.op("dve", lambda: nc.vector.tensor_scalar(a.t[:, 0:n], a.t[:, 0:n], 0.044715, 1.0, ALU.mult, ALU.add), reads=[a], writes=[a])
        S.op("dve", lambda: nc.vector.tensor_tensor(a.t[:, 0:n], a.t[:, 0:n], pp.t[:, 0:n], ALU.mult), reads=[a, pp], writes=[a])
        S.op("act", lambda: nc.scalar.activation(a.t[:, 0:n], a.t[:, 0:n], AF.Sigmoid, scale=1.5957691216057308), reads=[a], writes=[a])
        return a

    def sgu_layer(self, li, j):
        nc, S = self.nc, self.S
        w_in = self.sgu_w_in[j]
        self.a_reset()
        gtiles = [self.a_bf16(f"gtile{i}", 16 * 512) for i in range(2)]
        lngs = [self.a_f32(f"lng{i}", 512) for i in range(2)]
        wmT = self.a_bf16("wmT", 16 * 128)
        L2 = self.a_bf16("L2", E)
        RB = self.a_bf16("RB", 16 * 128)
        ybuf = self.a_bf16("ybuf", T)
        ssum = self.a_f32("ssum", 512)
        ssq = self.a_f32("ssq", 512)
        st = self.a_f32("st", 64)
        gstage = [self.a_bf16(f"gstage{i}", 512) for i in range(2)]
        junk = self.a_f32("junk", 512)
        for q in range(4):
            w32 = self.f32()
            S.dma("sp", w32.t[:, :], self.sgu_wsT[:, q * 4:(q + 1) * 4, :].rearrange("p g t -> p (g t)"), writes=[w32])
            for gg in range(4):
                g = q * 4 + gg
                S.op("dve", lambda: nc.vector.tensor_tensor(wmT.t[:, g * 128:(g + 1) * 128], w32.t[:, gg * 128:(gg + 1) * 128], self.tril, ALU.mult),
                     reads=[w32, self.cb], writes=[wmT])
        S.op("dve", lambda: nc.vector.memset(L2.t[0:2, :], 1.0), writes=[L2])
        for q in range(8):
            st32 = self.f32()
            S.dma("sp", st32.t[0:1, :], self.sgu_ln_b[:, q * 512:(q + 1) * 512], writes=[st32])
            S.op("dve", lambda: nc.vector.tensor_copy(L2.t[0:1, q * 512:(q + 1) * 512], st32.t[0:1, :]), reads=[st32], writes=[L2])
        for q in range(4):
            st32 = self.f32()
            S.dma("sp", st32.t[0:2, :], self.sgu_bs2[:, q * 512:(q + 1) * 512], writes=[st32])
            S.op("pe", lambda: nc.tensor.matmul(self.pa.t[0:1, :], self.ones1.t[:, 0:1], wmT.t[:, q * 512:(q + 1) * 512], start=True, stop=True),
                 reads=[wmT, self.ones1], writes=[self.pa])
            S.op("act", lambda: nc.scalar.activation(st32.t[0:1, :], self.pa.t[0:1, :], AF.Copy), reads=[self.pa], writes=[st32])
            S.op("dve", lambda: nc.vector.tensor_copy(RB.t[0:2, q * 512:(q + 1) * 512], st32.t[0:2, :]), reads=[st32], writes=[RB])
        pipe = Pipe()
        gvp = [self.a_bf16(f"gvp{i}", 512) for i in range(4)]
        jobc = [0]

        def mkb1(cbk, tc, shared):
            jid = jobc[0]
            jobc[0] += 1
            pp = self.pp[jid % 3]
            gv = gvp[jid % 4]
            ps = self.psx[jid % 2]
            gs = gstage[jid % 2]

            def s0():
                if tc == 0:
                    self.ada_step(1)
                    shared["wt"] = self.load_w(w_in[:, E + cbk * 128:E + (cbk + 1) * 128])
                self.proj_fm(shared["wt"], 0, self.hT, tc, pp=pp)

            def s1():
                a = self.gelu_from(pp, None)
                S.op("dve", lambda: nc.vector.tensor_tensor(gv.t[:, :], a.t[:, :], pp.t[:, :], ALU.mult), reads=[a, pp], writes=[gv])

            def s2():
                def fn():
                    ins = None
                    for q in range(4):
                        ins = nc.tensor.matmul(ps.t[:, q * 128:(q + 1) * 128], gv.t[:, q * 128:(q + 1) * 128], self.ident, start=True, stop=True)
                    return ins
                S.op("pe", fn, reads=[gv, self.cb], writes=[ps])
                S.op("act", lambda: nc.scalar.activation(gs.t[:, :], ps.t[:, :], AF.Copy), reads=[ps], writes=[gs])
                gs3 = gs.t.rearrange("p (q c) -> p q c", q=4)
                sum_ap = ssum.t.rearrange("p (n c) -> p n c", c=32)[:, tc * 4:(tc + 1) * 4, cbk]
                sq_ap = ssq.t.rearrange("p (n c) -> p n c", c=32)[:, tc * 4:(tc + 1) * 4, cbk]
                S.op("dve", lambda: nc.vector.tensor_reduce(sum_ap, gs3, mybir.AxisListType.X, ALU.add), reads=[gs], writes=[ssum])
                S.op("dve", lambda: nc.vector.tensor_tensor(junk.t[:, :], gs.t[:, :], gs.t[:, :], ALU.mult), reads=[gs], writes=[junk])
                S.op("dve", lambda: nc.vector.tensor_reduce(sq_ap, junk.t.rearrange("p (q c) -> p q c", q=4), mybir.AxisListType.X, ALU.add), reads=[junk], writes=[ssq])
                dst = self.gsc[tc * 512:(tc + 1) * 512, cbk * 128:(cbk + 1) * 128].rearrange("(q p) c -> p q c", p=128)
                S.dma("sp", dst, gs.t.rearrange("p (q c) -> p q c", q=4), reads=[gs], writes=[self.gsc_res[tc * 4 + q2] for q2 in range(4)])
            return [s0, s1, None, s2]
        for cbk in range(32):
            shared = {}
            for tc in range(TC):
                pipe.push(mkb1(cbk, tc, shared))
        pipe.flush()
        mean, ex2, rstd, nmr = (st.t[:, k * 16:(k + 1) * 16] for k in range(4))
        S.op("dve", lambda: nc.vector.tensor_reduce(mean, ssum.t.rearrange("p (n c) -> p n c", n=16), mybir.AxisListType.X, ALU.add), reads=[ssum], writes=[st])
        S.op("dve", lambda: nc.vector.tensor_reduce(ex2, ssq.t.rearrange("p (n c) -> p n c", n=16), mybir.AxisListType.X, ALU.add), reads=[ssq], writes=[st])
        S.op("dve", lambda: nc.vector.tensor_scalar(mean, mean, 1.0 / E, None, ALU.mult), reads=[st], writes=[st])
        S.op("dve", lambda: nc.vector.tensor_scalar(ex2, ex2, 1.0 / E, None, ALU.mult), reads=[st], writes=[st])
        S.op("dve", lambda: nc.vector.tensor_tensor(nmr, mean, mean, ALU.mult), reads=[st], writes=[st])
        S.op("dve", lambda: nc.vector.tensor_tensor(ex2, ex2, nmr, ALU.subtract), reads=[st], writes=[st])
        S.op("act", lambda: nc.scalar.activation(rstd, ex2, AF.Ln, bias=self.epsb.t[:, 0:1], scale=1.0), reads=[st, self.epsb], writes=[st])
        S.op("act", lambda: nc.scalar.activation(rstd, rstd, AF.Exp, scale=-0.5), reads=[st], writes=[st])
        S.op("dve", lambda: nc.vector.scalar_tensor_tensor(nmr, mean, -1.0, rstd, ALU.mult, ALU.mult), reads=[st], writes=[st])
        def prep(cg):
            gtile, lng = gtiles[cg % 2], lngs[cg % 2]
            gt3 = gtile.t.rearrange("p (n c) -> p n c", n=16)
            ops = []

            def ld():
                S.dma("sp", gt3, self.gsc[:, cg * 512:(cg + 1) * 512].rearrange("(n p) c -> p n c", p=128), reads=self.gsc_res, writes=[gtile])
                S.dma("sp", lng.t[:, :], self.sgu_ln_g[:, cg * 512:(cg + 1) * 512], writes=[lng])
            ops.append(ld)
            for n in range(16):
                def nrm(n=n):
                    S.op("dve", lambda: nc.vector.tensor_scalar(gt3[:, n, :], gt3[:, n, :], st.t[:, 32 + n:33 + n], st.t[:, 48 + n:49 + n], ALU.mult, ALU.add),
                         reads=[gtile, st], writes=[gtile])
                    S.op("dve", lambda: nc.vector.tensor_tensor(gt3[:, n, :], gt3[:, n, :], lng.t[:, :], ALU.mult), reads=[gtile, lng], writes=[gtile])
                ops.append(nrm)
            return ops
        for o in prep(0):
            o()
        pend = []
        for cg in range(8):
            gtile = gtiles[cg % 2]
            gt3 = gtile.t.rearrange("p (n c) -> p n c", n=16)
            for o in pend:
                o()
            pend = prep(cg + 1) if cg + 1 < 8 else []
            for cbl in range(4):
                cbk = cg * 4 + cbl
                g = cbk // 2
                if cbk % 2 == 0:
                    self.ada_step(1)
                wu = self.load_w(w_in[:, cbk * 128:(cbk + 1) * 128])
                wz = self.load_w(w_in[:, 2 * E + cbk * 128:2 * E + (cbk + 1) * 128])
                for tc in range(TC):
                    for _ in range(2):
                        if pend:
                            pend.pop(0)()
                    ppu = self.proj_fm(wu, 0, self.hT, tc)
                    a = self.gelu_from(ppu, None)
                    gu = self.f32()
                    S.op("dve", lambda: nc.vector.tensor_tensor(gu.t[:, :], a.t[:, :], ppu.t[:, :], ALU.mult), reads=[a, ppu], writes=[gu])
                    ppz = self.proj_fm(wz, 0, self.hT, tc)
                    zs = self.f32()
                    S.op("act", lambda: nc.scalar.activation(zs.t[:, :], ppz.t[:, :], AF.Silu), reads=[ppz], writes=[zs])

                    def fn():
                        ins = None
                        for q in range(4):
                            n = tc * 4 + q
                            nc.tensor.matmul(self.pol.t[:, q * 128:(q + 1) * 128], gt3[:, n, cbl * 128:(cbl + 1) * 128], wmT.t[:, g * 128:(g + 1) * 128], start=True, stop=False)
                            ins = nc.tensor.matmul(self.pol.t[:, q * 128:(q + 1) * 128], L2.t[0:2, cbk * 128:(cbk + 1) * 128], RB.t[0:2, g * 128:(g + 1) * 128], start=False, stop=True)
                        return ins
                    S.op("pe", fn, reads=[gtile, wmT, L2, RB], writes=[self.pol])
                    S.op("dve", lambda: nc.vector.tensor_tensor(gu.t[:, :], gu.t[:, :], self.pol.t[:, :], ALU.mult), reads=[gu, self.pol], writes=[gu])
                    S.op("dve", lambda: nc.vector.tensor_tensor(ybuf.t[:, tc * 512:(tc + 1) * 512], gu.t[:, :], zs.t[:, :], ALU.mult), reads=[gu, zs], writes=[ybuf])
                S.dma("sp", self.ysc[cbk * 128:(cbk + 1) * 128, :], ybuf.t[:, :], reads=[ybuf], writes=[self.ysc_res[cbk]])
        self.out_proj(li, self.sgu_w_out[j], 2)

    def conv_layer(self, li, j):
        nc, S = self.nc, self.S
        w_in = self.conv_w_in[j]
        self.a_reset()
        PADW = 32
        gpad = [self.a_bf16(f"gpad{i}", PADW + T) for i in range(2)]
        diag = [self.a_bf16(f"diag{i}", 31 * 128) for i in range(2)]
        dwt = self.a_f32("dwt", 32 * 31)
        vec = self.a_f32("vec", 96)
        ssum = self.a_f32("csum", T)
        ssq = self.a_f32("csq", T)
        g2b = [self.a_bf16(f"g2b{i}", T) for i in range(2)]
        ybuf = self.a_bf16("cybuf", T)
        tmpf = self.a_f32("tmpf", T)
        S.dma("sp", dwt.t[:, :], self.conv_dwT.rearrange("p c k -> p (c k)"), writes=[dwt])
        S.dma("sp", vec.t[:, :], self.conv_vecT.rearrange("p w c -> p (w c)"), writes=[vec])
        for i in range(2):
            S.op("dve", lambda: nc.vector.memset(gpad[i].t[:, 0:PADW], 0.0), writes=[gpad[i]])
        pipe = Pipe(depth=2)

        def mkc1(cbk):
            gp = gpad[cbk % 2]
            dg = diag[cbk % 2]
            g2 = g2b[cbk % 2]

            def s0():
                self.ada_step(1)
                wa = self.load_w(w_in[:, cbk * 128:(cbk + 1) * 128])
                wb = self.load_w(w_in[:, E + cbk * 128:E + (cbk + 1) * 128])
                for k in range(31):
                    S.op("dve", lambda: nc.vector.tensor_scalar(dg.t[:, k * 128:(k + 1) * 128], self.ident, dwt.t[:, cbk * 31 + k:cbk * 31 + k + 1], None, ALU.mult),
                         reads=[self.cb, dwt], writes=[dg])
                for tc in range(TC):
                    ppa = self.proj_fm(wa, 0, self.hT, tc)
                    ppb = self.proj_fm(wb, 0, self.hT, tc)
                    sg = self.f32()
                    S.op("act", lambda: nc.scalar.activation(sg.t[:, :], ppb.t[:, :], AF.Sigmoid), reads=[ppb], writes=[sg])
                    S.op("dve", lambda: nc.vector.tensor_tensor(gp.t[:, PADW + tc * 512:PADW + (tc + 1) * 512], sg.t[:, :], ppa.t[:, :], ALU.mult), reads=[sg, ppa], writes=[gp])

            def s1():
                for tc in range(TC):
                    pc = self.psx[tc % 2]

                    def fn():
                        ins = None
                        for k in range(31):
                            o = PADW + tc * 512 + k - 30
                            ins = nc.tensor.matmul(pc.t[:, :], dg.t[:, k * 128:(k + 1) * 128], gp.t[:, o:o + 512], start=(k == 0), stop=(k == 30))
                        return ins
                    S.op("pe", fn, reads=[dg, gp], writes=[pc])
                    S.op("act", lambda: nc.scalar.activation(g2.t[:, tc * 512:(tc + 1) * 512], pc.t[:, :], AF.Identity, bias=vec.t[:, cbk:cbk + 1], scale=1.0), reads=[pc, vec], writes=[g2])
                    sq = self.b16()
                    S.op("act", lambda: nc.scalar.activation(sq.t[:, :], g2.t[:, tc * 512:(tc + 1) * 512], AF.Square), reads=[g2], writes=[sq])
                    S.op("pe", lambda: nc.tensor.matmul(self.pa.t[:, :], self.onesE.t[:, :], g2.t[:, tc * 512:(tc + 1) * 512], start=True, stop=True), reads=[g2, self.onesE], writes=[self.pa])
                    S.op("pe", lambda: nc.tensor.matmul(self.pb.t[:, :], self.onesE.t[:, :], sq.t[:, :], start=True, stop=True), reads=[sq, self.onesE], writes=[self.pb])
                    for pacc, acc in ((self.pa, ssum), (self.pb, ssq)):
                        if cbk == 0:
                            S.op("act", lambda: nc.scalar.activation(acc.t[:, tc * 512:(tc + 1) * 512], pacc.t[:, :], AF.Copy), reads=[pacc], writes=[acc])
                        else:
                            S.op("dve", lambda: nc.vector.tensor_tensor(acc.t[:, tc * 512:(tc + 1) * 512], acc.t[:, tc * 512:(tc + 1) * 512], pacc.t[:, :], ALU.add), reads=[pacc, acc], writes=[acc])
                S.dma("sp", self.g2sc[cbk * 128:(cbk + 1) * 128, :], g2.t[:, :], reads=[g2], writes=[self.g2_res[cbk]])
            return [s0, s1]
        for cbk in range(32):
            pipe.push(mkc1(cbk))
        pipe.flush()
        S.op("dve", lambda: nc.vector.tensor_tensor(tmpf.t[:, :], ssum.t[:, :], ssum.t[:, :], ALU.mult), reads=[ssum], writes=[tmpf])
        S.op("dve", lambda: nc.vector.tensor_tensor(ssq.t[:, :], ssq.t[:, :], tmpf.t[:, :], ALU.subtract), reads=[ssq, tmpf], writes=[ssq])
        S.op("act", lambda: nc.scalar.activation(ssq.t[:, :], ssq.t[:, :], AF.Ln, bias=self.epsb.t[:, 0:1], scale=1.0), reads=[ssq, self.epsb], writes=[ssq])
        S.op("act", lambda: nc.scalar.activation(ssq.t[:, :], ssq.t[:, :], AF.Exp, scale=-0.5), reads=[ssq], writes=[ssq])
        S.op("dve", lambda: nc.vector.scalar_tensor_tensor(ssum.t[:, :], ssum.t[:, :], -1.0, ssq.t[:, :], ALU.mult, ALU.mult), reads=[ssum, ssq], writes=[ssum])
        for cbk in range(32):
            if cbk < 16:
                self.ada_step(1)
            wz = self.load_w(w_in[:, 2 * E + cbk * 128:2 * E + (cbk + 1) * 128])
            g2 = g2b[cbk % 2]
            if cbk == 0:
                S.dma("sp", g2.t[:, :], self.g2sc[0:128, :], reads=[self.g2_res[0]], writes=[g2])
            if cbk + 1 < 32:
                S.dma("sp", g2b[(cbk + 1) % 2].t[:, :], self.g2sc[(cbk + 1) * 128:(cbk + 2) * 128, :], reads=[self.g2_res[cbk + 1]], writes=[g2b[(cbk + 1) % 2]])
            for tc in range(TC):
                sl = slice(tc * 512, (tc + 1) * 512)
                t1 = self.f32()
                S.op("dve", lambda: nc.vector.tensor_tensor(t1.t[:, :], g2.t[:, sl], ssq.t[:, sl], ALU.mult), reads=[g2, ssq], writes=[t1])
                S.op("dve", lambda: nc.vector.tensor_tensor(t1.t[:, :], t1.t[:, :], ssum.t[:, sl], ALU.add), reads=[t1, ssum], writes=[t1])
                S.op("act", lambda: nc.scalar.activation(t1.t[:, :], t1.t[:, :], AF.Silu, bias=vec.t[:, 64 + cbk:65 + cbk], scale=vec.t[:, 32 + cbk:33 + cbk]), reads=[t1, vec], writes=[t1])
                ppz = self.proj_fm(wz, 0, self.hT, tc)
                zs = self.f32()
                S.op("act", lambda: nc.scalar.activation(zs.t[:, :], ppz.t[:, :], AF.Silu), reads=[ppz], writes=[zs])
                S.op("dve", lambda: nc.vector.tensor_tensor(ybuf.t[:, sl], t1.t[:, :], zs.t[:, :], ALU.mult), reads=[t1, zs], writes=[ybuf])
            S.dma("sp", self.ysc[cbk * 128:(cbk + 1) * 128, :], ybuf.t[:, :], reads=[ybuf], writes=[self.ysc_res[cbk]])
        self.out_proj(li, self.conv_w_out[j], 2)

    def out_proj(self, li, w_out, nhalf):
        nc, S = self.nc, self.S
        last = (li == self.nlayers - 1)
        src = self.xT if li == 0 else self.xs
        dst = self.outT if last else self.xs
        S.barrier()
        self.a_reset()
        ysrc = [list(self.hT)]
        if nhalf == 2:
            ysrc.append([self.a_bf16(f"yh{kc}", T) for kc in range(KC)])
        for half in range(nhalf):
            for kc in range(KC):
                r = half * KC + kc
                S.dma("sp", ysrc[half][kc].t[:, :], self.ysc[r * 128:(r + 1) * 128, :], reads=[self.ysc_res[r]], writes=[ysrc[half][kc]])
        iters = [(cb, tc) for cb in range(16) for tc in range(TC)]
        xbs = {}
        PRE = 3

        def load(i):
            if i < len(iters):
                cb, tc = iters[i]
                xb = self.f32()
                S.dma("sp", xb.t[:], src[cb * 128:(cb + 1) * 128, tc * 512:(tc + 1) * 512], reads=[self.xs_res[cb][tc]], writes=[xb])
                xbs[i] = xb
        for i in range(PRE):
            load(i)
        wts = None
        for i, (cb, tc) in enumerate(iters):
            if tc == 0:
                self.ada_step(1)
                wts = [self.load_w(w_out[half * 2048:(half + 1) * 2048, cb * 128:(cb + 1) * 128]) for half in range(nhalf)]
            load(i + PRE)
            pp = self.nextpp()

            def fn():
                ins = None
                n = nhalf * KC
                k = 0
                for half in range(nhalf):
                    for kc in range(KC):
                        ins = nc.tensor.matmul(pp.t[:, :], wts[half].t[:, kc, :], ysrc[half][kc].t[:, tc * 512:(tc + 1) * 512],
                                               start=(k == 0), stop=(k == n - 1))
                        k += 1
                return ins
            S.op("pe", fn, reads=list(wts) + [t for h in ysrc for t in h], writes=[pp])
            xb = xbs.pop(i)
            S.op("dve", lambda: nc.vector.scalar_tensor_tensor(xb.t[:], pp.t[:], self.mod.t[:, 32 + cb:33 + cb], xb.t[:], ALU.mult, ALU.add),
                 reads=[pp, self.mod, xb], writes=[xb])
            S.dma("act", dst[cb * 128:(cb + 1) * 128, tc * 512:(tc + 1) * 512], xb.t[:], reads=[xb], writes=[self.xs_res[cb][tc]])

    def layers(self):
        self.ada_begin(0)
        for li in range(self.nlayers):
            self.S.barrier()
            self.ada_finish(li)
            self.norm(li)
            kind, j = li % 3, li // 3
            if kind == 0:
                self.attn_layer(li, j)
            elif kind == 1:
                self.sgu_layer(li, j)
            else:
                self.conv_layer(li, j)

    def build(self):
        real = self.S
        self.S = DrySched()
        self.dry = True
        self.layers()
        self.S = real
        self.dry = False
        self.f32i = self.bf16i = self.ppi = 0
        self.setup()
        self.S.barrier()
        self.layers()
        self.S.finish()
        return self.nc


class Pipe:
    def __init__(self, depth=4):
        self.q = []
        self.depth = depth

    def push(self, stages):
        self.q.insert(0, stages)
        for age, st in enumerate(self.q):
            if age < len(st) and st[age] is not None:
                st[age]()
        del self.q[self.depth:]

    def flush(self):
        for _ in range(self.depth):
            self.push([])


class DrySched:
    def op(self, *a, **k):
        return None

    def barrier(self):
        return None

    def dma(self, *a, **k):
        return None


def make_in_maps(inputs):
    f = lambda a: np.ascontiguousarray(np.asarray(a))
    x, c, positions = f(inputs["x"]), f(inputs["c"]), f(inputs["positions"])
    cmat, fcol = _consts()
    shared = {
        "ada_w": f(inputs["ada_w"]),
        "ada_bT": f(inputs["ada_b"]).reshape(4, 48, 128).transpose(0, 2, 1).copy(),
        "norm_gT": f(inputs["norm_g"]).reshape(4, KC, 128).transpose(0, 2, 1).copy(),
        "attn_w_in": f(inputs["attn_w_in"]),
        "attn_qg": f(inputs["attn_q_gain"]).reshape(2, 128, 1),
        "attn_kg": f(inputs["attn_k_gain"]).reshape(2, 128, 1),
        "attn_w_out": f(inputs["attn_w_out"]),
        "sgu_w_in": f(inputs["sgu_w_in"]),
        "sgu_ln_g": np.ascontiguousarray(np.broadcast_to(f(inputs["sgu_ln_g"]).reshape(1, E), (128, E))),
        "sgu_ln_b": f(inputs["sgu_ln_b"]).reshape(1, E),
        "sgu_wsT": f(inputs["sgu_ws"])[0].transpose(2, 0, 1).copy(),
        "sgu_bs2": np.concatenate([np.zeros((1, 2048), np.float32), f(inputs["sgu_bs"]).reshape(1, 16 * 128)], 0),
        "sgu_w_out": f(inputs["sgu_w_out"]),
        "conv_w_in": f(inputs["conv_w_in"]),
        "conv_w_out": f(inputs["conv_w_out"]),
        "conv_dwT": f(inputs["conv_dw_w"])[0].reshape(31, 32, 128).transpose(2, 1, 0).copy(),
        "conv_vecT": np.stack([f(inputs["conv_dw_b"])[0], f(inputs["conv_ln_g"])[0], f(inputs["conv_ln_b"])[0]], 0)
                        .reshape(3, 32, 128).transpose(2, 0, 1).copy(),
        "cmat": cmat, "fcol": fcol,
    }
    maps = [None] * NCORES
    for b in range(NB):
        m = dict(shared)
        m["xT"] = np.ascontiguousarray(x[b].T)
        m["cT"] = np.ascontiguousarray(c[b].reshape(KC, 128).T)
        m["pos"] = np.ascontiguousarray(np.broadcast_to(positions[b].astype(np.int32)[None, :], (128, T)))
        maps[REAL_CORES[b]] = m
    zero = {k: np.zeros_like(v) for k, v in maps[REAL_CORES[0]].items()}
    for i in range(NCORES):
        if maps[i] is None:
            maps[i] = zero
    return maps


_NC_CACHE = {}


def kernel(**inputs):
    nl = 4
    if nl not in _NC_CACHE:
        _NC_CACHE[nl] = Builder(nl).build()
    nc = _NC_CACHE[nl]
    maps = make_in_maps(inputs)
    res = run_bass_kernel_spmd(nc, maps, core_ids=list(range(NCORES)))
    out = np.stack([np.ascontiguousarray(res.results[REAL_CORES[b]]["outT"].T) for b in range(NB)], 0)
    return out.astype(np.float32)
```

```python
import numpy as np
import concourse.bass as bass
import concourse.mybir as mybir
from concourse.bass_utils import run_bass_kernel_spmd

F32 = mybir.dt.float32
BF16 = mybir.dt.bfloat16
AF = mybir.ActivationFunctionType
ALU = mybir.AluOpType


class Res:
    __slots__ = ("name", "t", "w", "r")

    def __init__(self, name, t=None):
        self.name = name
        self.t = t
        self.w = None
        self.r = {}


class Sched:
    COMPUTE = ("pe", "act", "dve", "pool")

    def __init__(self, nc, ndma=6):
        self.nc = nc
        self.eng = {"pe": nc.tensor, "act": nc.scalar, "dve": nc.vector,
                    "pool": nc.gpsimd, "sp": nc.sync}
        self.sems = {}
        for k in self.COMPUTE:
            self.sems[("c", k)] = nc.alloc_semaphore(name=f"c_{k}")
        self.ccnt = {k: 0 for k in self.COMPUTE}
        self.ndma = ndma
        self.dval = {}
        self.dnext = {}
        for q in ("sp", "pool", "act"):
            self.dnext[q] = 0
            for i in range(ndma):
                self.sems[("d", q, i)] = nc.alloc_semaphore(name=f"d_{q}_{i}")
                self.dval[("d", q, i)] = 0
        self.seen = {e: {} for e in self.eng}
        self.nwait = 0
        self.nops = 0
        self.trace = {e: [] for e in self.eng}
        self.pending = {e: [] for e in self.eng}

    def sb(self, name, shape, dtype):
        return Res(name, self.nc.alloc_sbuf_tensor(name, shape, dtype))

    def ps(self, name, shape, dtype):
        return Res(name, self.nc.alloc_psum_tensor(name, shape, dtype))

    def res(self, name):
        return Res(name)

    dram_res = res

    def _wait(self, e, ev):
        if ev is None:
            return
        key, val = ev
        if val <= 0:
            return
        if self.seen[e].get(key, 0) >= val:
            return
        if e == "pe" and key == ("c", "pe"):
            return
        self.eng[e].wait_ge(self.sems[key], val)
        self.seen[e][key] = val
        self.nwait += 1
        self.pending[e].append((key, val))

    def _deps(self, e, reads, writes):
        for r in reads:
            self._wait(e, r.w)
        for w in writes:
            self._wait(e, w.w)
            for key, val in w.r.items():
                self._wait(e, (key, val))

    def _mark(self, ev, reads, writes):
        key, val = ev
        for r in reads:
            if r.r.get(key, 0) < val:
                r.r[key] = val
        for w in writes:
            w.w = ev
            w.r = {}

    def op(self, e, fn, reads=(), writes=()):
        self._deps(e, reads, writes)
        inst = fn()
        self.ccnt[e] += 1
        key = ("c", e)
        inst.then_inc(self.sems[key], 1)
        ev = (key, self.ccnt[e])
        self._mark(ev, reads, writes)
        self.nops += 1
        self.trace[e].append((self.pending[e], (key, 1)))
        self.pending[e] = []
        return ev

    def _dma_like(self, q, fn, reads, writes):
        slot = self.dnext[q]
        self.dnext[q] = (slot + 1) % self.ndma
        key = ("d", q, slot)
        self._wait(q, (key, self.dval[key]))
        self._deps(q, reads, writes)
        inst = fn()
        inst.then_inc(self.sems[key], 16)
        self.dval[key] += 16
        ev = (key, self.dval[key])
        self._mark(ev, reads, writes)
        self.nops += 1
        self.trace[q].append((self.pending[q], (key, 16)))
        self.pending[q] = []
        return ev

    def dma(self, q, out, in_, reads=(), writes=()):
        return self._dma_like(q, lambda: self.eng[q].dma_start(out=out, in_=in_), reads, writes)

    def collective(self, fn, reads=(), writes=()):
        return self._dma_like("pool", fn, reads, writes)

    def barrier(self):
        for e in ("pe", "act", "dve", "sp"):
            for k in ("pe", "act", "dve"):
                self._wait(e, (("c", k), self.ccnt[k]))
            for q in ("sp", "act"):
                for i in range(self.ndma):
                    key = ("d", q, i)
                    self._wait(e, (key, self.dval[key]))

    def simulate(self):
        for e in self.eng:
            if self.pending[e]:
                self.trace[e].append((self.pending[e], None))
                self.pending[e] = []
        cnt = {k: 0 for k in self.sems}
        ptr = {e: 0 for e in self.eng}
        progress = True
        while progress:
            progress = False
            for e in self.eng:
                tr = self.trace[e]
                while ptr[e] < len(tr):
                    waits, inc = tr[ptr[e]]
                    if any(cnt[k] < v for k, v in waits):
                        break
                    if inc is not None:
                        cnt[inc[0]] += inc[1]
                    ptr[e] += 1
                    progress = True
        stuck = {e: (ptr[e], len(self.trace[e]), self.trace[e][ptr[e]][0]) for e in self.eng if ptr[e] < len(self.trace[e])}
        return stuck, cnt

    def finish(self):
        for key, val in self.dval.items():
            self._wait("sp", (key, val))
        for k in self.COMPUTE:
            self._wait("sp", (("c", k), self.ccnt[k]))


D = 2048
T = 2048
NB = 4
NCORES = 8
REAL_CORES = (0, 1, 4, 5)
KC = 16
TC = 4
E = 4096
NEG = -30000.0
EPS = 1e-6
SCALE = 128.0 ** -0.5
TWO_PI_HI = 6.28125
TWO_PI_LO = 2.0 * np.pi - 6.28125
DILS = (1, 4, 16)


def _consts():
    ident = np.eye(128, dtype=np.float32)
    prot = np.zeros((128, 128), np.float32)
    for p in range(16):
        prot[p + 16, p] = -1.0
        prot[p, p + 16] = 1.0
    j = np.arange(128)[:, None]
    i = np.arange(128)[None, :]
    maskb = np.concatenate([np.where(j <= i, 0.0, NEG), np.where(j >= i, 0.0, NEG)], 1).astype(np.float32)
    inv_freq = (500000.0 ** (-np.arange(0, 32, 2, dtype=np.float32) / 32.0)).astype(np.float32)
    fcol = np.zeros((128, 1), np.float32)
    fcol[:32, 0] = np.concatenate([inv_freq, inv_freq])
    tril_st = (j <= i).astype(np.float32)
    mask01 = np.concatenate([(j <= i), (j >= i)], 1).astype(np.float32)
    cmat = np.concatenate([ident, prot, maskb, tril_st, mask01], 1)
    return cmat, fcol


class Builder:
    def __init__(self, nlayers=4):
        self.nlayers = nlayers
        nc = self.nc = bass.Bass("TRN2", target_bir_lowering=False)
        S = self.S = Sched(nc)
        dt = nc.dram_tensor
        def inp(name, shape, dtype=F32):
            return dt(name, list(shape), dtype, kind="ExternalInput").ap()
        self.xT = inp("xT", [D, T])
        self.cT = inp("cT", [128, KC])
        self.pos = inp("pos", [128, T], mybir.dt.int32)
        self.ada_w = inp("ada_w", [4, D, 3 * D])
        self.ada_bT = inp("ada_bT", [4, 128, 48])
        self.norm_gT = inp("norm_gT", [4, 128, KC])
        self.attn_w_in = inp("attn_w_in", [2, D, 20480])
        self.attn_qg = inp("attn_qg", [2, 128, 1])
        self.attn_kg = inp("attn_kg", [2, 128, 1])
        self.attn_w_out = inp("attn_w_out", [2, D, D])
        self.sgu_w_in = inp("sgu_w_in", [1, D, 3 * E])
        self.sgu_ln_g = inp("sgu_ln_g", [128, E])
        self.sgu_ln_b = inp("sgu_ln_b", [1, E])
        self.sgu_wsT = inp("sgu_wsT", [128, 16, 128])
        self.sgu_bs2 = inp("sgu_bs2", [2, 16 * 128])
        self.sgu_w_out = inp("sgu_w_out", [1, E, D])
        self.conv_w_in = inp("conv_w_in", [1, D, 3 * E])
        self.conv_w_out = inp("conv_w_out", [1, E, D])
        self.conv_dwT = inp("conv_dwT", [128, 32, 31])
        self.conv_vecT = inp("conv_vecT", [128, 3, 32])
        self.cmat = inp("cmat", [128, 896])
        self.fcol_in = inp("fcol", [128, 1])
        self.outT = dt("outT", [D, T], F32, kind="ExternalOutput").ap()
        self.xs = dt("xs", [D, T], F32, kind="Internal").ap()
        self.ysc = dt("ysc", [E, T], BF16, kind="Internal").ap()
        self.gsc = dt("gsc", [T, E], BF16, kind="Internal").ap()
        self.g2sc = dt("g2sc", [E, T], BF16, kind="Internal").ap()
        self.xs_res = [[S.res(f"xs{c}_{t}") for t in range(TC)] for c in range(KC)]
        self.ysc_res = [S.res(f"ysc{c}") for c in range(32)]
        self.gsc_res = [S.res(f"gsc{c}") for c in range(16)]
        self.g2_res = [S.res(f"g2sc{c}") for c in range(32)]
        self.alloc()

    def alloc(self):
        S = self.S
        self.hT = [S.sb(f"hT{kc}", [128, T], BF16) for kc in range(KC)]
        self.NW = 8
        self.wt = [S.sb(f"wt{i}", [128, KC, 128], BF16) for i in range(self.NW)]
        self.wplan = []
        self.wissued = 0
        self.wused = 0
        self.dry = False
        self.pp = [S.ps(f"pp{i}", [128, 512], F32) for i in range(3)]
        self.pa = S.ps("pa", [128, 512], F32)
        self.pb = S.ps("pb", [128, 512], F32)
        self.psx = [S.ps(f"psx{i}", [128, 512], F32) for i in range(2)]
        self.pol = S.ps("pol", [128, 512], F32)
        self.ppi = 0
        self.cb = S.sb("cb", [128, 896], BF16)
        self.ident = self.cb.t[:, 0:128]
        self.prot = self.cb.t[:, 128:256]
        self.maskb = self.cb.t[:, 256:512]
        self.tril = self.cb.t[:, 512:640]
        self.mask01 = self.cb.t[:, 640:896]
        self.onesD = S.sb("onesD", [128, 128], BF16)
        self.onesH = S.sb("onesH", [128, 128], BF16)
        self.ones1 = S.sb("ones1", [128, 128], BF16)
        self.onesE = S.sb("onesE", [128, 128], BF16)
        self.fcol = S.sb("fcolsb", [128, 1], F32)
        self.cosT = S.sb("cosT", [128, T], F32)
        self.sinT = S.sb("sinT", [128, T], F32)
        self.cact = S.sb("cact", [128, KC], BF16)
        self.mod = S.sb("mod", [128, 48], F32)
        self.modn = S.sb("modn", [128, 48], F32)
        self.avec = S.sb("avec", [128, KC], F32)
        self.adab = S.sb("adab", [128, 48], F32)
        self.ng = S.sb("ng", [128, KC], F32)
        self.gain = S.sb("gain", [128, 2], F32)
        self.epsb = S.sb("epsb", [128, 1], F32)
        self.negc = S.sb("negc", [128, 1], F32)
        self.rstdn = [S.sb(f"rstdn{i}", [128, 512], F32) for i in range(2)]
        self.f32s = [S.sb(f"f32s{i}", [128, 512], F32) for i in range(6)]
        self.f32i = 0
        self.bf16s = [S.sb(f"bf16s{i}", [128, 512], BF16) for i in range(4)]
        self.bf16i = 0
        self.ARENA_F32 = 17664
        self.arena = S.sb("arena", [128, self.ARENA_F32], F32)
        self.aoff = 0

    def a_reset(self):
        self.aoff = 0

    def a_f32(self, name, n):
        ap = self.arena.t[:, self.aoff:self.aoff + n]
        self.aoff += n
        assert self.aoff <= self.ARENA_F32, (name, self.aoff)
        return Res(name, ap)

    def a_bf16(self, name, n):
        assert n % 2 == 0
        ap = self.arena.t[:, self.aoff:self.aoff + n // 2].bitcast(BF16)
        self.aoff += n // 2
        assert self.aoff <= self.ARENA_F32, (name, self.aoff)
        return Res(name, ap)

    def a_i32(self, name, n):
        ap = self.arena.t[:, self.aoff:self.aoff + n].bitcast(mybir.dt.int32)
        self.aoff += n
        assert self.aoff <= self.ARENA_F32, (name, self.aoff)
        return Res(name, ap)

    def f32(self):
        r = self.f32s[self.f32i % len(self.f32s)]
        self.f32i += 1
        return r

    def b16(self):
        r = self.bf16s[self.bf16i % len(self.bf16s)]
        self.bf16i += 1
        return r

    def nextpp(self):
        r = self.pp[self.ppi % 3]
        self.ppi += 1
        return r

    def load_w(self, wap):
        if self.dry:
            self.wplan.append(wap)
            return self.wt[0]
        u = self.wused
        self.wused += 1
        while self.wissued < len(self.wplan) and self.wissued < u + self.NW - 1:
            i = self.wissued
            t = self.wt[i % self.NW]
            self.S.dma("pool", t.t[:], self.wplan[i].rearrange("(kc p) n -> p kc n", p=128), writes=[t])
            self.wissued += 1
        return self.wt[u % self.NW]

    def proj_fm(self, wt, c0, src, tc, n=512, pp=None):
        nc, S = self.nc, self.S
        if pp is None:
            pp = self.nextpp()

        def fn():
            ins = None
            for kc in range(KC):
                ins = nc.tensor.matmul(pp.t[:, 0:n], wt.t[:, kc, c0:c0 + 128],
                                       src[kc].t[:, tc * n:(tc + 1) * n],
                                       start=(kc == 0), stop=(kc == KC - 1))
            return ins
        S.op("pe", fn, reads=[wt] + list(src), writes=[pp])
        return pp

    def setup(self):
        nc, S = self.nc, self.S
        S.dma("pool", self.cb.t[:], self.cmat, writes=[self.cb])
        S.dma("sp", self.fcol.t[:], self.fcol_in, writes=[self.fcol])
        for t, v in ((self.onesD, 1.0 / D), (self.onesH, 1.0 / 128), (self.ones1, 1.0), (self.onesE, 1.0 / E)):
            S.op("dve", lambda t=t, v=v: nc.vector.memset(t.t[:], v), writes=[t])
        S.op("dve", lambda: nc.vector.memset(self.epsb.t[:], EPS), writes=[self.epsb])
        S.op("dve", lambda: nc.vector.memset(self.negc.t[:], -4.0), writes=[self.negc])
        c32 = self.f32()
        S.dma("sp", c32.t[:, 0:KC], self.cT, writes=[c32])
        S.op("act", lambda: nc.scalar.activation(self.cact.t[:], c32.t[:, 0:KC], AF.Silu), reads=[c32], writes=[self.cact])
        self.a_reset()
        posi = self.a_i32("posi", T)
        S.dma("sp", posi.t[:], self.pos, writes=[posi])
        ang, ki, kf, r = self.a_f32("ang", T), self.a_i32("ki", T), self.a_f32("kf", T), self.a_f32("r", T)
        S.op("dve", lambda: nc.vector.tensor_copy(ang.t[:], posi.t[:]), reads=[posi], writes=[ang])
        S.op("dve", lambda: nc.vector.tensor_scalar(ang.t[:], ang.t[:], self.fcol.t[:, 0:1], None, ALU.mult), reads=[ang, self.fcol], writes=[ang])
        for tab, shift in ((self.sinT, 0.0), (self.cosT, np.pi / 2)):
            S.op("dve", lambda: nc.vector.tensor_scalar(r.t[:], ang.t[:], float(shift), None, ALU.add), reads=[ang], writes=[r])
            S.op("dve", lambda: nc.vector.tensor_scalar(ki.t[:], r.t[:], float(1.0 / (2 * np.pi)), None, ALU.mult), reads=[r], writes=[ki])
            S.op("dve", lambda: nc.vector.tensor_copy(kf.t[:], ki.t[:]), reads=[ki], writes=[kf])
            S.op("dve", lambda: nc.vector.scalar_tensor_tensor(r.t[:], kf.t[:], -TWO_PI_HI, r.t[:], ALU.mult, ALU.add), reads=[kf, r], writes=[r])
            S.op("dve", lambda: nc.vector.scalar_tensor_tensor(r.t[:], kf.t[:], -float(TWO_PI_LO), r.t[:], ALU.mult, ALU.add), reads=[kf, r], writes=[r])
            S.op("dve", lambda: nc.vector.tensor_scalar(kf.t[:], r.t[:], float(np.pi), None, ALU.is_gt), reads=[r], writes=[kf])
            S.op("dve", lambda: nc.vector.scalar_tensor_tensor(r.t[:], kf.t[:], -float(2 * np.pi), r.t[:], ALU.mult, ALU.add), reads=[kf, r], writes=[r])
            S.op("dve", lambda: nc.vector.tensor_scalar(kf.t[:], r.t[:], -float(np.pi), None, ALU.is_lt), reads=[r], writes=[kf])
            S.op("dve", lambda: nc.vector.scalar_tensor_tensor(r.t[:], kf.t[:], float(2 * np.pi), r.t[:], ALU.mult, ALU.add), reads=[kf, r], writes=[r])
            S.op("dve", lambda: nc.vector.tensor_scalar(r.t[:], r.t[:], -3.1415925, 3.1415925, ALU.max, ALU.min), reads=[r], writes=[r])
            S.op("act", lambda tab=tab: nc.scalar.activation(tab.t[:], r.t[:], AF.Sin), reads=[r], writes=[tab])

    def ada_begin(self, li):
        self.ada_li = li
        self.ada_j = 0

    def ada_step(self, n=1):
        nc, S = self.nc, self.S
        li = self.ada_li
        for _ in range(n):
            if li is None or li >= self.nlayers or self.ada_j >= 48:
                return
            j = self.ada_j
            self.ada_j += 1
            wt = self.load_w(self.ada_w[li, :, j * 128:(j + 1) * 128])
            pm = self.pa

            def fn():
                ins = None
                for kc in range(KC):
                    ins = nc.tensor.matmul(pm.t[:, 0:1], wt.t[:, kc, :], self.cact.t[:, kc:kc + 1], start=(kc == 0), stop=(kc == KC - 1))
                return ins
            S.op("pe", fn, reads=[wt, self.cact], writes=[pm])
            S.op("dve", lambda: nc.vector.tensor_copy(self.modn.t[:, j:j + 1], pm.t[:, 0:1]), reads=[pm], writes=[self.modn])

    def ada_finish(self, li):
        nc, S = self.nc, self.S
        assert self.ada_li == li
        self.ada_step(48)
        S.dma("sp", self.adab.t[:], self.ada_bT[li], writes=[self.adab])
        S.dma("sp", self.ng.t[:], self.norm_gT[li], writes=[self.ng])
        S.op("dve", lambda: nc.vector.tensor_tensor(self.mod.t[:], self.modn.t[:], self.adab.t[:], ALU.add),
             reads=[self.modn, self.adab], writes=[self.mod])
        S.op("dve", lambda: nc.vector.scalar_tensor_tensor(self.avec.t[:], self.mod.t[:, 16:32], 1.0, self.ng.t[:], ALU.add, ALU.mult),
             reads=[self.mod, self.ng], writes=[self.avec])
        self.ada_begin(li + 1)

    def norm(self, li):
        nc, S = self.nc, self.S
        src = self.xT if li == 0 else self.xs
        self.a_reset()
        xch = [[self.a_f32(f"xch{i}_{kc}", 512) for kc in range(KC)] for i in range(2)]
        for tc in range(TC):
            pm = self.pa
            xc = xch[tc % 2]
            for kc in range(KC):
                xb = xc[kc]
                S.dma("sp", xb.t[:], src[kc * 128:(kc + 1) * 128, tc * 512:(tc + 1) * 512],
                      reads=[self.xs_res[kc][tc]], writes=[xb])
                sq = self.b16()
                S.op("act", lambda xb=xb, sq=sq: nc.scalar.activation(sq.t[:], xb.t[:], AF.Square), reads=[xb], writes=[sq])
                S.op("pe", lambda sq=sq, kc=kc: nc.tensor.matmul(pm.t[:], self.onesD.t[:], sq.t[:], start=(kc == 0), stop=(kc == KC - 1)),
                     reads=[sq, self.onesD], writes=[pm])
            rstd = self.rstd_from(pm, out=self.rstdn[tc % 2])
            for kc in range(KC):
                xb = xc[kc]
                S.op("dve", lambda xb=xb, kc=kc: nc.vector.scalar_tensor_tensor(xb.t[:], xb.t[:], self.avec.t[:, kc:kc + 1], rstd.t[:], ALU.mult, ALU.mult),
                     reads=[xb, self.avec, rstd], writes=[xb])
                S.op("act", lambda xb=xb, kc=kc, tc=tc: nc.scalar.activation(self.hT[kc].t[:, tc * 512:(tc + 1) * 512], xb.t[:], AF.Identity,
                                                                       bias=self.mod.t[:, kc:kc + 1], scale=1.0),
                     reads=[xb, self.mod], writes=[self.hT[kc]])
        self.S.barrier()

    def rstd_from(self, pm, n=512, out=None):
        nc, S = self.nc, self.S
        r = out if out is not None else self.f32()
        S.op("act", lambda: nc.scalar.activation(r.t[:, 0:n], pm.t[:, 0:n], AF.Ln, bias=self.epsb.t[:, 0:1], scale=1.0), reads=[pm, self.epsb], writes=[r])
        S.op("act", lambda: nc.scalar.activation(r.t[:, 0:n], r.t[:, 0:n], AF.Exp, scale=-0.5), reads=[r], writes=[r])
        return r

    def perm_out(self, buf, dil, tc):
        if dil == 1:
            return buf.t[:, tc * 512:(tc + 1) * 512]
        L = T // dil
        n = 512 // dil
        return buf.t[:, :].rearrange("p (r m) -> p r m", r=dil)[:, :, tc * n:(tc + 1) * n]

    def perm_in(self, ap512, dil):
        if dil == 1:
            return ap512
        return ap512.rearrange("p (m r) -> p r m", r=dil)

    def nat_ap(self, buf, dil, b4):
        if dil == 1:
            return buf.t[:, b4 * 512:(b4 + 1) * 512]
        if dil == 4:
            return buf.t[:, :].rearrange("p (m r) -> p r m", r=4)[:, b4, :]
        return buf.t[:, :].rearrange("p (m r) -> p r m", r=16)[:, 4 * b4:4 * b4 + 4, :]

    def blk_view(self, ps, dil):
        if dil == 16:
            return ps.t[:, :].rearrange("p (r m) -> p r m", r=4)
        return ps.t[:, :]

    def nat_ap2(self, buf, dil, b0):
        if dil == 1:
            return buf.t[:, b0 * 128:b0 * 128 + 256]
        if dil == 4:
            r, m0 = b0 // 4, (b0 % 4) * 128
            return buf.t[:, :].rearrange("p (m r) -> p r m", r=4)[:, r, m0:m0 + 256]
        return buf.t[:, :].rearrange("p (m r) -> p r m", r=16)[:, b0:b0 + 2, :]

    def blk_view2(self, ap256, dil):
        if dil == 16:
            return ap256.rearrange("p (r m) -> p r m", r=2)
        return ap256

    def attn_layer(self, li, j):
        nc, S = self.nc, self.S
        w_in = self.attn_w_in[j]
        S.dma("sp", self.gain.t[:, 0:1], self.attn_qg[j], writes=[self.gain])
        S.dma("sp", self.gain.t[:, 1:2], self.attn_kg[j], writes=[self.gain])
        self.a_reset()
        qf = [self.a_bf16(f"qf{i}", T) for i in range(2)]
        kf = [self.a_bf16(f"kf{i}", T) for i in range(2)]
        vtm = [self.a_bf16(f"vtm{i}", T) for i in range(2)]
        zs = [self.a_bf16(f"zs{i}", T) for i in range(2)]
        vT = self.a_bf16("vT", T)
        ybuf = self.a_bf16("ybuf", T)
        oacc, lacc = self.a_f32("oacc", T), self.a_f32("lacc", T)
        pT = [self.a_bf16(f"pT{i}", 256) for i in range(6)]
        kgp = [self.a_bf16(f"kg{i}", 512) for i in range(3)]
        pipe = Pipe()
        units = [(h, g) for h in range(16) for g in range(3)]
        jobc = [0]

        def a_jobs(u):
            h, g = units[u]
            dil = DILS[g]
            par = u % 2
            jobs = []

            def mk(name, base, tc, shared):
                jid = jobc[0]
                jobc[0] += 1
                pp = self.pp[jid % 3]
                kg = kgp[jid % 3]
                c = base + (g * 2048 if name != "z" else 0) + h * 128

                def s0():
                    if tc == 0:
                        shared["wt"] = self.load_w(w_in[:, c:c + 128])
                    self.proj_fm(shared["wt"], 0, self.hT, tc, pp=pp)
                if name in ("k", "q"):
                    gcol = 1 if name == "k" else 0
                    dst = kf[par] if name == "k" else qf[par]

                    def s1():
                        sq = self.b16()
                        S.op("act", lambda: nc.scalar.activation(sq.t[:], pp.t[:], AF.Square), reads=[pp], writes=[sq])
                        S.op("pe", lambda: nc.tensor.matmul(self.pa.t[:], self.onesH.t[:], sq.t[:], start=True, stop=True),
                             reads=[sq, self.onesH], writes=[self.pa])
                        rstd = self.rstd_from(self.pa)
                        S.op("dve", lambda: nc.vector.scalar_tensor_tensor(kg.t[:], pp.t[:], self.gain.t[:, gcol:gcol + 1], rstd.t[:], ALU.mult, ALU.mult),
                             reads=[pp, self.gain, rstd], writes=[kg])

                    def s2():
                        S.op("pe", lambda: nc.tensor.matmul(self.pb.t[:], self.prot, kg.t[:], start=True, stop=True),
                             reads=[kg, self.cb], writes=[self.pb])
                        t1 = self.f32()
                        t2 = self.f32()
                        S.op("dve", lambda: nc.vector.tensor_tensor(t1.t[:], kg.t[:], self.cosT.t[:, tc * 512:(tc + 1) * 512], ALU.mult),
                             reads=[kg, self.cosT], writes=[t1])
                        S.op("dve", lambda: nc.vector.tensor_tensor(t2.t[:], self.pb.t[:], self.sinT.t[:, tc * 512:(tc + 1) * 512], ALU.mult),
                             reads=[self.pb, self.sinT], writes=[t2])
                        S.op("dve", lambda: nc.vector.tensor_tensor(self.perm_out(dst, dil, tc), self.perm_in(t1.t[:, :], dil), self.perm_in(t2.t[:, :], dil), ALU.add),
                             reads=[t1, t2], writes=[dst])
                    return [s0, s1, s2]
                if name == "v":
                    def s1():
                        S.op("act", lambda: nc.scalar.activation(self.perm_out(vT, dil, tc), self.perm_in(pp.t[:, :], dil), AF.Copy),
                             reads=[pp], writes=[vT])

                    def s2():
                        for b4 in range(4):
                            ps = self.pa if b4 % 2 == 0 else self.pb

                            def fn():
                                ins = None
                                for q in range(4):
                                    b = b4 * 4 + q
                                    ins = nc.tensor.matmul(ps.t[:, q * 128:(q + 1) * 128], vT.t[:, b * 128:(b + 1) * 128], self.ident, start=True, stop=True)
                                return ins
                            S.op("pe", fn, reads=[vT, self.cb], writes=[ps])
                            S.op("act", lambda: nc.scalar.activation(vtm[par].t[:, b4 * 512:(b4 + 1) * 512], ps.t[:, :], AF.Copy), reads=[ps], writes=[vtm[par]])
                    return [s0, s1, s2 if tc == 3 else None]
                def s1z():
                    S.op("act", lambda: nc.scalar.activation(zs[h % 2].t[:, tc * 512:(tc + 1) * 512], pp.t[:, :], AF.Silu), reads=[pp], writes=[zs[h % 2]])
                return [s0, s1z]

            for name, base in (("k", 6144), ("q", 0), ("v", 12288)) + ((("z", 18432),) if g == 2 else ()):
                shared = {}
                for tc in range(TC):
                    jobs.append(mk(name, base, tc, shared))
            return jobs

        def b_steps(u):
            h, g = units[u]
            dil = DILS[g]
            nb = 16 // dil
            par = u % 2
            kfin, qfin, vt = kf[par], qf[par], vtm[par]

            def score(b):
                jj = b % nb
                n = 256 if jj + 1 < nb else 128
                ps = self.psx[b % 2]
                p = pT[b % 6]

                def fn():
                    return nc.tensor.matmul(ps.t[:, 0:n], kfin.t[:, b * 128:(b + 1) * 128], qfin.t[:, b * 128:b * 128 + n], start=True, stop=True)
                S.op("pe", fn, reads=[kfin, qfin], writes=[ps])
                S.op("act", lambda: nc.scalar.activation(p.t[:, 0:n], ps.t[:, 0:n], AF.Exp, bias=self.negc.t[:, 0:1], scale=SCALE), reads=[ps, self.negc], writes=[p])
                S.op("dve", lambda: nc.vector.tensor_tensor(p.t[:, 0:n], p.t[:, 0:n], self.mask01[:, 0:n], ALU.mult), reads=[p, self.cb], writes=[p])

            def pv(b):
                jj = b % nb
                q2 = b % 2
                p = pT[b % 6]
                pprev = pT[(b - 1) % 6]
                for which in (0, 1):
                    col = which * 256 + q2 * 128

                    def fn2():
                        first = True
                        if jj > 0:
                            l = vt.t[:, (b - 1) * 128:b * 128] if which == 0 else self.ones1.t[:, :]
                            nc.tensor.matmul(self.pol.t[:, col:col + 128], l, pprev.t[:, 128:256], start=True, stop=False)
                            first = False
                        l = vt.t[:, b * 128:(b + 1) * 128] if which == 0 else self.ones1.t[:, :]
                        return nc.tensor.matmul(self.pol.t[:, col:col + 128], l, p.t[:, 0:128], start=first, stop=True)
                    S.op("pe", fn2, reads=[vt, self.ones1, p] + ([pprev] if jj > 0 else []), writes=[self.pol])
                if q2 == 1:
                    b0 = b - 1
                    for which, acc in ((0, oacc), (1, lacc)):
                        dst_ap = self.nat_ap2(acc, dil, b0)
                        src_ap = self.blk_view2(self.pol.t[:, which * 256:(which + 1) * 256], dil)
                        if g == 0:
                            S.op("act", lambda: nc.scalar.activation(dst_ap, src_ap, AF.Copy), reads=[self.pol], writes=[acc])
                        else:
                            S.op("dve", lambda: nc.vector.tensor_tensor(dst_ap, dst_ap, src_ap, ALU.add), reads=[self.pol, acc], writes=[acc])

            def mkstep(k):
                def st():
                    for b in (2 * k, 2 * k + 1):
                        if b < 16:
                            score(b)
                    for b in (2 * k - 2, 2 * k - 1):
                        if 0 <= b < 16:
                            pv(b)
                return st
            steps = [mkstep(k) for k in range(9)]
            if g == 2:
                def comb():
                    S.op("dve", lambda: nc.vector.reciprocal(lacc.t[:, :], lacc.t[:, :]), reads=[lacc], writes=[lacc])
                    S.op("dve", lambda: nc.vector.tensor_tensor(oacc.t[:, :], oacc.t[:, :], lacc.t[:, :], ALU.mult), reads=[oacc, lacc], writes=[oacc])
                    S.op("dve", lambda: nc.vector.tensor_tensor(ybuf.t[:, :], oacc.t[:, :], zs[h % 2].t[:, :], ALU.mult), reads=[oacc, zs[h % 2]], writes=[ybuf])
                    S.dma("sp", self.ysc[h * 128:(h + 1) * 128, :], ybuf.t[:, :], reads=[ybuf], writes=[self.ysc_res[h]])
                steps.append(comb)
            return steps

        NU = len(units)
        LAG = 3
        for w in range(NU + 1):
            aj = a_jobs(w) if w < NU else []
            bs = b_steps(w - 1) if w >= 1 else []
            n = max(len(aj), (LAG + len(bs)) if bs else 0)
            self.ada_step(1)
            for i in range(n):
                pipe.push(aj[i] if i < len(aj) else [])
                if bs and LAG <= i < LAG + len(bs):
                    bs[i - LAG]()
        pipe.flush()
        self.out_proj(li, self.attn_w_out[j], 1)

    def gelu_from(self, pp, out_ap, n=512):
        nc, S = self.nc, self.S
        a = self.f32()
        S.op("act", lambda: nc.scalar.activation(a.t[:, 0:n], pp.t[:, 0:n], AF.Square), reads=[pp], writes=[a])
        S.op("dve", lambda: nc.vector.tensor_scalar(a.t[:, 0:n], a.t[:, 0:n], 0.044715, 1.0, ALU.mult, ALU.add), reads=[a], writes=[a])
        S.op("dve", lambda: nc.vector.tensor_tensor(a.t[:, 0:n], a.t[:, 0:n], pp.t[:, 0:n], ALU.mult), reads=[a, pp], writes=[a])
        S.op("act", lambda: nc.scalar.activation(a.t[:, 0:n], a.t[:, 0:n], AF.Sigmoid, scale=1.5957691216057308), reads=[a], writes=[a])
        return a

    def sgu_layer(self, li, j):
        nc, S = self.nc, self.S
        w_in = self.sgu_w_in[j]
        self.a_reset()
        gtiles = [self.a_bf16(f"gtile{i}", 16 * 512) for i in range(2)]
        lngs = [self.a_f32(f"lng{i}", 512) for i in range(2)]
        wmT = self.a_bf16("wmT", 16 * 128)
        L2 = self.a_bf16("L2", E)
        RB = self.a_bf16("RB", 16 * 128)
        ybuf = self.a_bf16("ybuf", T)
        ssum = self.a_f32("ssum", 512)
        ssq = self.a_f32("ssq", 512)
        st = self.a_f32("st", 64)
        gstage = [self.a_bf16(f"gstage{i}", 512) for i in range(2)]
        junk = self.a_f32("junk", 512)
        for q in range(4):
            w32 = self.f32()
            S.dma("sp", w32.t[:, :], self.sgu_wsT[:, q * 4:(q + 1) * 4, :].rearrange("p g t -> p (g t)"), writes=[w32])
            for gg in range(4):
                g = q * 4 + gg
                S.op("dve", lambda: nc.vector.tensor_tensor(wmT.t[:, g * 128:(g + 1) * 128], w32.t[:, gg * 128:(gg + 1) * 128], self.tril, ALU.mult),
                     reads=[w32, self.cb], writes=[wmT])
        S.op("dve", lambda: nc.vector.memset(L2.t[0:2, :], 1.0), writes=[L2])
        for q in range(8):
            st32 = self.f32()
            S.dma("sp", st32.t[0:1, :], self.sgu_ln_b[:, q * 512:(q + 1) * 512], writes=[st32])
            S.op("dve", lambda: nc.vector.tensor_copy(L2.t[0:1, q * 512:(q + 1) * 512], st32.t[0:1, :]), reads=[st32], writes=[L2])
        for q in range(4):
            st32 = self.f32()
            S.dma("sp", st32.t[0:2, :], self.sgu_bs2[:, q * 512:(q + 1) * 512], writes=[st32])
            S.op("pe", lambda: nc.tensor.matmul(self.pa.t[0:1, :], self.ones1.t[:, 0:1], wmT.t[:, q * 512:(q + 1) * 512], start=True, stop=True),
                 reads=[wmT, self.ones1], writes=[self.pa])
            S.op("act", lambda: nc.scalar.activation(st32.t[0:1, :], self.pa.t[0:1, :], AF.Copy), reads=[self.pa], writes=[st32])
            S.op("dve", lambda: nc.vector.tensor_copy(RB.t[0:2, q * 512:(q + 1) * 512], st32.t[0:2, :]), reads=[st32], writes=[RB])
        pipe = Pipe()
        gvp = [self.a_bf16(f"gvp{i}", 512) for i in range(4)]
        jobc = [0]

        def mkb1(cbk, tc, shared):
            jid = jobc[0]
            jobc[0] += 1
            pp = self.pp[jid % 3]
            gv = gvp[jid % 4]
            ps = self.psx[jid % 2]
            gs = gstage[jid % 2]

            def s0():
                if tc == 0:
                    self.ada_step(1)
                    shared["wt"] = self.load_w(w_in[:, E + cbk * 128:E + (cbk + 1) * 128])
                self.proj_fm(shared["wt"], 0, self.hT, tc, pp=pp)

            def s1():
                a = self.gelu_from(pp, None)
                S.op("dve", lambda: nc.vector.tensor_tensor(gv.t[:, :], a.t[:, :], pp.t[:, :], ALU.mult), reads=[a, pp], writes=[gv])

            def s2():
                def fn():
                    ins = None
                    for q in range(4):
                        ins = nc.tensor.matmul(ps.t[:, q * 128:(q + 1) * 128], gv.t[:, q * 128:(q + 1) * 128], self.ident, start=True, stop=True)
                    return ins
                S.op("pe", fn, reads=[gv, self.cb], writes=[ps])
                S.op("act", lambda: nc.scalar.activation(gs.t[:, :], ps.t[:, :], AF.Copy), reads=[ps], writes=[gs])
                gs3 = gs.t.rearrange("p (q c) -> p q c", q=4)
                sum_ap = ssum.t.rearrange("p (n c) -> p n c", c=32)[:, tc * 4:(tc + 1) * 4, cbk]
                sq_ap = ssq.t.rearrange("p (n c) -> p n c", c=32)[:, tc * 4:(tc + 1) * 4, cbk]
                S.op("dve", lambda: nc.vector.tensor_reduce(sum_ap, gs3, mybir.AxisListType.X, ALU.add), reads=[gs], writes=[ssum])
                S.op("act", lambda: nc.scalar.activation(junk.t[:, :], gs.t[:, :], AF.Square), reads=[gs], writes=[junk])
                S.op("dve", lambda: nc.vector.tensor_reduce(sq_ap, junk.t.rearrange("p (q c) -> p q c", q=4), mybir.AxisListType.X, ALU.add), reads=[junk], writes=[ssq])
                dst = self.gsc[tc * 512:(tc + 1) * 512, cbk * 128:(cbk + 1) * 128].rearrange("(q p) c -> p q c", p=128)
                S.dma("sp", dst, gs.t.rearrange("p (q c) -> p q c", q=4), reads=[gs], writes=[self.gsc_res[tc * 4 + q2] for q2 in range(4)])
            return [s0, s1, None, s2]
        for cbk in range(32):
            shared = {}
            for tc in range(TC):
                pipe.push(mkb1(cbk, tc, shared))
        pipe.flush()
        mean, ex2, rstd, nmr = (st.t[:, k * 16:(k + 1) * 16] for k in range(4))
        S.op("dve", lambda: nc.vector.tensor_reduce(mean, ssum.t.rearrange("p (n c) -> p n c", n=16), mybir.AxisListType.X, ALU.add), reads=[ssum], writes=[st])
        S.op("dve", lambda: nc.vector.tensor_reduce(ex2, ssq.t.rearrange("p (n c) -> p n c", n=16), mybir.AxisListType.X, ALU.add), reads=[ssq], writes=[st])
        S.op("dve", lambda: nc.vector.tensor_scalar(mean, mean, 1.0 / E, None, ALU.mult), reads=[st], writes=[st])
        S.op("dve", lambda: nc.vector.tensor_scalar(ex2, ex2, 1.0 / E, None, ALU.mult), reads=[st], writes=[st])
        S.op("dve", lambda: nc.vector.tensor_tensor(nmr, mean, mean, ALU.mult), reads=[st], writes=[st])
        S.op("dve", lambda: nc.vector.tensor_tensor(ex2, ex2, nmr, ALU.subtract), reads=[st], writes=[st])
        S.op("act", lambda: nc.scalar.activation(rstd, ex2, AF.Ln, bias=self.epsb.t[:, 0:1], scale=1.0), reads=[st, self.epsb], writes=[st])
        S.op("act", lambda: nc.scalar.activation(rstd, rstd, AF.Exp, scale=-0.5), reads=[st], writes=[st])
        S.op("dve", lambda: nc.vector.scalar_tensor_tensor(nmr, mean, -1.0, rstd, ALU.mult, ALU.mult), reads=[st], writes=[st])
        def prep(cg):
            gtile, lng = gtiles[cg % 2], lngs[cg % 2]
            gt3 = gtile.t.rearrange("p (n c) -> p n c", n=16)
            ops = []

            def ld():
                S.dma("sp", gt3, self.gsc[:, cg * 512:(cg + 1) * 512].rearrange("(n p) c -> p n c", p=128), reads=self.gsc_res, writes=[gtile])
                S.dma("sp", lng.t[:, :], self.sgu_ln_g[:, cg * 512:(cg + 1) * 512], writes=[lng])
            ops.append(ld)
            for n in range(16):
                def nrm(n=n):
                    S.op("dve", lambda: nc.vector.tensor_scalar(gt3[:, n, :], gt3[:, n, :], st.t[:, 32 + n:33 + n], st.t[:, 48 + n:49 + n], ALU.mult, ALU.add),
                         reads=[gtile, st], writes=[gtile])
                    S.op("dve", lambda: nc.vector.tensor_tensor(gt3[:, n, :], gt3[:, n, :], lng.t[:, :], ALU.mult), reads=[gtile, lng], writes=[gtile])
                ops.append(nrm)
            return ops
        for o in prep(0):
            o()
        pend = []
        for cg in range(8):
            gtile = gtiles[cg % 2]
            gt3 = gtile.t.rearrange("p (n c) -> p n c", n=16)
            for o in pend:
                o()
            pend = prep(cg + 1) if cg + 1 < 8 else []
            for cbl in range(4):
                cbk = cg * 4 + cbl
                g = cbk // 2
                if cbk % 2 == 0:
                    self.ada_step(1)
                wu = self.load_w(w_in[:, cbk * 128:(cbk + 1) * 128])
                wz = self.load_w(w_in[:, 2 * E + cbk * 128:2 * E + (cbk + 1) * 128])
                for tc in range(TC):
                    for _ in range(2):
                        if pend:
                            pend.pop(0)()
                    ppu = self.proj_fm(wu, 0, self.hT, tc)
                    a = self.gelu_from(ppu, None)
                    gu = self.f32()
                    S.op("dve", lambda: nc.vector.tensor_tensor(gu.t[:, :], a.t[:, :], ppu.t[:, :], ALU.mult), reads=[a, ppu], writes=[gu])
                    ppz = self.proj_fm(wz, 0, self.hT, tc)
                    zs = self.f32()
                    S.op("act", lambda: nc.scalar.activation(zs.t[:, :], ppz.t[:, :], AF.Silu), reads=[ppz], writes=[zs])

                    def fn():
                        ins = None
                        for q in range(4):
                            n = tc * 4 + q
                            nc.tensor.matmul(self.pol.t[:, q * 128:(q + 1) * 128], gt3[:, n, cbl * 128:(cbl + 1) * 128], wmT.t[:, g * 128:(g + 1) * 128], start=True, stop=False)
                            ins = nc.tensor.matmul(self.pol.t[:, q * 128:(q + 1) * 128], L2.t[0:2, cbk * 128:(cbk + 1) * 128], RB.t[0:2, g * 128:(g + 1) * 128], start=False, stop=True)
                        return ins
                    S.op("pe", fn, reads=[gtile, wmT, L2, RB], writes=[self.pol])
                    S.op("dve", lambda: nc.vector.tensor_tensor(gu.t[:, :], gu.t[:, :], self.pol.t[:, :], ALU.mult), reads=[gu, self.pol], writes=[gu])
                    S.op("dve", lambda: nc.vector.tensor_tensor(ybuf.t[:, tc * 512:(tc + 1) * 512], gu.t[:, :], zs.t[:, :], ALU.mult), reads=[gu, zs], writes=[ybuf])
                S.dma("sp", self.ysc[cbk * 128:(cbk + 1) * 128, :], ybuf.t[:, :], reads=[ybuf], writes=[self.ysc_res[cbk]])
        self.out_proj(li, self.sgu_w_out[j], 2)

    def conv_layer(self, li, j):
        nc, S = self.nc, self.S
        w_in = self.conv_w_in[j]
        self.a_reset()
        PADW = 32
        gpad = [self.a_bf16(f"gpad{i}", PADW + T) for i in range(2)]
        diag = [self.a_bf16(f"diag{i}", 31 * 128) for i in range(2)]
        dwt = self.a_f32("dwt", 32 * 31)
        vec = self.a_f32("vec", 96)
        ssum = self.a_f32("csum", T)
        ssq = self.a_f32("csq", T)
        g2b = [self.a_bf16(f"g2b{i}", T) for i in range(2)]
        ybuf = self.a_bf16("cybuf", T)
        tmpf = self.a_f32("tmpf", T)
        S.dma("sp", dwt.t[:, :], self.conv_dwT.rearrange("p c k -> p (c k)"), writes=[dwt])
        S.dma("sp", vec.t[:, :], self.conv_vecT.rearrange("p w c -> p (w c)"), writes=[vec])
        for i in range(2):
            S.op("dve", lambda: nc.vector.memset(gpad[i].t[:, 0:PADW], 0.0), writes=[gpad[i]])
        pipe = Pipe(depth=2)

        def mkc1(cbk):
            gp = gpad[cbk % 2]
            dg = diag[cbk % 2]
            g2 = g2b[cbk % 2]

            def s0():
                self.ada_step(1)
                wa = self.load_w(w_in[:, cbk * 128:(cbk + 1) * 128])
                wb = self.load_w(w_in[:, E + cbk * 128:E + (cbk + 1) * 128])
                for k in range(31):
                    S.op("dve", lambda: nc.vector.tensor_scalar(dg.t[:, k * 128:(k + 1) * 128], self.ident, dwt.t[:, cbk * 31 + k:cbk * 31 + k + 1], None, ALU.mult),
                         reads=[self.cb, dwt], writes=[dg])
                for tc in range(TC):
                    ppa = self.proj_fm(wa, 0, self.hT, tc)
                    ppb = self.proj_fm(wb, 0, self.hT, tc)
                    sg = self.f32()
                    S.op("act", lambda: nc.scalar.activation(sg.t[:, :], ppb.t[:, :], AF.Sigmoid), reads=[ppb], writes=[sg])
                    S.op("dve", lambda: nc.vector.tensor_tensor(gp.t[:, PADW + tc * 512:PADW + (tc + 1) * 512], sg.t[:, :], ppa.t[:, :], ALU.mult), reads=[sg, ppa], writes=[gp])

            def s1():
                for tc in range(TC):
                    pc = self.psx[tc % 2]

                    def fn():
                        ins = None
                        for k in range(31):
                            o = PADW + tc * 512 + k - 30
                            ins = nc.tensor.matmul(pc.t[:, :], dg.t[:, k * 128:(k + 1) * 128], gp.t[:, o:o + 512], start=(k == 0), stop=(k == 30))
                        return ins
                    S.op("pe", fn, reads=[dg, gp], writes=[pc])
                    S.op("act", lambda: nc.scalar.activation(g2.t[:, tc * 512:(tc + 1) * 512], pc.t[:, :], AF.Identity, bias=vec.t[:, cbk:cbk + 1], scale=1.0), reads=[pc, vec], writes=[g2])
                    sq = self.b16()
                    S.op("act", lambda: nc.scalar.activation(sq.t[:, :], g2.t[:, tc * 512:(tc + 1) * 512], AF.Square), reads=[g2], writes=[sq])
                    S.op("pe", lambda: nc.tensor.matmul(self.pa.t[:, :], self.onesE.t[:, :], g2.t[:, tc * 512:(tc + 1) * 512], start=True, stop=True), reads=[g2, self.onesE], writes=[self.pa])
                    S.op("pe", lambda: nc.tensor.matmul(self.pb.t[:, :], self.onesE.t[:, :], sq.t[:, :], start=True, stop=True), reads=[sq, self.onesE], writes=[self.pb])
                    for pacc, acc in ((self.pa, ssum), (self.pb, ssq)):
                        if cbk == 0:
                            S.op("act", lambda: nc.scalar.activation(acc.t[:, tc * 512:(tc + 1) * 512], pacc.t[:, :], AF.Copy), reads=[pacc], writes=[acc])
                        else:
                            S.op("dve", lambda: nc.vector.tensor_tensor(acc.t[:, tc * 512:(tc + 1) * 512], acc.t[:, tc * 512:(tc + 1) * 512], pacc.t[:, :], ALU.add), reads=[pacc, acc], writes=[acc])
                S.dma("sp", self.g2sc[cbk * 128:(cbk + 1) * 128, :], g2.t[:, :], reads=[g2], writes=[self.g2_res[cbk]])
            return [s0, s1]
        for cbk in range(32):
            pipe.push(mkc1(cbk))
        pipe.flush()
        S.op("dve", lambda: nc.vector.tensor_tensor(tmpf.t[:, :], ssum.t[:, :], ssum.t[:, :], ALU.mult), reads=[ssum], writes=[tmpf])
        S.op("dve", lambda: nc.vector.tensor_tensor(ssq.t[:, :], ssq.t[:, :], tmpf.t[:, :], ALU.subtract), reads=[ssq, tmpf], writes=[ssq])
        S.op("act", lambda: nc.scalar.activation(ssq.t[:, :], ssq.t[:, :], AF.Ln, bias=self.epsb.t[:, 0:1], scale=1.0), reads=[ssq, self.epsb], writes=[ssq])
        S.op("act", lambda: nc.scalar.activation(ssq.t[:, :], ssq.t[:, :], AF.Exp, scale=-0.5), reads=[ssq], writes=[ssq])
        S.op("dve", lambda: nc.vector.scalar_tensor_tensor(ssum.t[:, :], ssum.t[:, :], -1.0, ssq.t[:, :], ALU.mult, ALU.mult), reads=[ssum, ssq], writes=[ssum])
        for cbk in range(32):
            if cbk < 16:
                self.ada_step(1)
            wz = self.load_w(w_in[:, 2 * E + cbk * 128:2 * E + (cbk + 1) * 128])
            g2 = g2b[cbk % 2]
            if cbk == 0:
                S.dma("sp", g2.t[:, :], self.g2sc[0:128, :], reads=[self.g2_res[0]], writes=[g2])
            if cbk + 1 < 32:
                S.dma("sp", g2b[(cbk + 1) % 2].t[:, :], self.g2sc[(cbk + 1) * 128:(cbk + 2) * 128, :], reads=[self.g2_res[cbk + 1]], writes=[g2b[(cbk + 1) % 2]])
            for tc in range(TC):
                sl = slice(tc * 512, (tc + 1) * 512)
                t1 = self.f32()
                S.op("dve", lambda: nc.vector.tensor_tensor(t1.t[:, :], g2.t[:, sl], ssq.t[:, sl], ALU.mult), reads=[g2, ssq], writes=[t1])
                S.op("dve", lambda: nc.vector.tensor_tensor(t1.t[:, :], t1.t[:, :], ssum.t[:, sl], ALU.add), reads=[t1, ssum], writes=[t1])
                S.op("act", lambda: nc.scalar.activation(t1.t[:, :], t1.t[:, :], AF.Silu, bias=vec.t[:, 64 + cbk:65 + cbk], scale=vec.t[:, 32 + cbk:33 + cbk]), reads=[t1, vec], writes=[t1])
                ppz = self.proj_fm(wz, 0, self.hT, tc)
                zs = self.f32()
                S.op("act", lambda: nc.scalar.activation(zs.t[:, :], ppz.t[:, :], AF.Silu), reads=[ppz], writes=[zs])
                S.op("dve", lambda: nc.vector.tensor_tensor(ybuf.t[:, sl], t1.t[:, :], zs.t[:, :], ALU.mult), reads=[t1, zs], writes=[ybuf])
            S.dma("sp", self.ysc[cbk * 128:(cbk + 1) * 128, :], ybuf.t[:, :], reads=[ybuf], writes=[self.ysc_res[cbk]])
        self.out_proj(li, self.conv_w_out[j], 2)

    def out_proj(self, li, w_out, nhalf):
        nc, S = self.nc, self.S
        last = (li == self.nlayers - 1)
        src = self.xT if li == 0 else self.xs
        dst = self.outT if last else self.xs
        S.barrier()
        self.a_reset()
        ysrc = [list(self.hT)]
        if nhalf == 2:
            ysrc.append([self.a_bf16(f"yh{kc}", T) for kc in range(KC)])
        for half in range(nhalf):
            for kc in range(KC):
                r = half * KC + kc
                S.dma("sp", ysrc[half][kc].t[:, :], self.ysc[r * 128:(r + 1) * 128, :], reads=[self.ysc_res[r]], writes=[ysrc[half][kc]])
        iters = [(cb, tc) for cb in range(16) for tc in range(TC)]
        xbs = {}
        PRE = 3

        def load(i):
            if i < len(iters):
                cb, tc = iters[i]
                xb = self.f32()
                S.dma("sp", xb.t[:], src[cb * 128:(cb + 1) * 128, tc * 512:(tc + 1) * 512], reads=[self.xs_res[cb][tc]], writes=[xb])
                xbs[i] = xb
        for i in range(PRE):
            load(i)
        wts = None
        for i, (cb, tc) in enumerate(iters):
            if tc == 0:
                self.ada_step(1)
                wts = [self.load_w(w_out[half * 2048:(half + 1) * 2048, cb * 128:(cb + 1) * 128]) for half in range(nhalf)]
            load(i + PRE)
            pp = self.nextpp()

            def fn():
                ins = None
                n = nhalf * KC
                k = 0
                for half in range(nhalf):
                    for kc in range(KC):
                        ins = nc.tensor.matmul(pp.t[:, :], wts[half].t[:, kc, :], ysrc[half][kc].t[:, tc * 512:(tc + 1) * 512],
                                               start=(k == 0), stop=(k == n - 1))
                        k += 1
                return ins
            S.op("pe", fn, reads=list(wts) + [t for h in ysrc for t in h], writes=[pp])
            xb = xbs.pop(i)
            S.op("dve", lambda: nc.vector.scalar_tensor_tensor(xb.t[:], pp.t[:], self.mod.t[:, 32 + cb:33 + cb], xb.t[:], ALU.mult, ALU.add),
                 reads=[pp, self.mod, xb], writes=[xb])
            S.dma("act", dst[cb * 128:(cb + 1) * 128, tc * 512:(tc + 1) * 512], xb.t[:], reads=[xb], writes=[self.xs_res[cb][tc]])

    def layers(self):
        self.ada_begin(0)
        for li in range(self.nlayers):
            self.S.barrier()
            self.ada_finish(li)
            self.norm(li)
            kind, j = li % 3, li // 3
            if kind == 0:
                self.attn_layer(li, j)
            elif kind == 1:
                self.sgu_layer(li, j)
            else:
                self.conv_layer(li, j)

    def build(self):
        real = self.S
        self.S = DrySched()
        self.dry = True
        self.layers()
        self.S = real
        self.dry = False
        self.f32i = self.bf16i = self.ppi = 0
        self.setup()
        self.S.barrier()
        self.layers()
        self.S.finish()
        return self.nc


class Pipe:
    def __init__(self, depth=4):
        self.q = []
        self.depth = depth

    def push(self, stages):
        self.q.insert(0, stages)
        for age, st in enumerate(self.q):
            if age < len(st) and st[age] is not None:
                st[age]()
        del self.q[self.depth:]

    def flush(self):
        for _ in range(self.depth):
            self.push([])


class DrySched:
    def op(self, *a, **k):
        return None

    def barrier(self):
        return None

    def dma(self, *a, **k):
        return None


def make_in_maps(inputs):
    f = lambda a: np.ascontiguousarray(np.asarray(a))
    x, c, positions = f(inputs["x"]), f(inputs["c"]), f(inputs["positions"])
    cmat, fcol = _consts()
    shared = {
        "ada_w": f(inputs["ada_w"]),
        "ada_bT": f(inputs["ada_b"]).reshape(4, 48, 128).transpose(0, 2, 1).copy(),
        "norm_gT": f(inputs["norm_g"]).reshape(4, KC, 128).transpose(0, 2, 1).copy(),
        "attn_w_in": f(inputs["attn_w_in"]),
        "attn_qg": f(inputs["attn_q_gain"]).reshape(2, 128, 1),
        "attn_kg": f(inputs["attn_k_gain"]).reshape(2, 128, 1),
        "attn_w_out": f(inputs["attn_w_out"]),
        "sgu_w_in": f(inputs["sgu_w_in"]),
        "sgu_ln_g": np.ascontiguousarray(np.broadcast_to(f(inputs["sgu_ln_g"]).reshape(1, E), (128, E))),
        "sgu_ln_b": f(inputs["sgu_ln_b"]).reshape(1, E),
        "sgu_wsT": f(inputs["sgu_ws"])[0].transpose(2, 0, 1).copy(),
        "sgu_bs2": np.concatenate([np.zeros((1, 2048), np.float32), f(inputs["sgu_bs"]).reshape(1, 16 * 128)], 0),
        "sgu_w_out": f(inputs["sgu_w_out"]),
        "conv_w_in": f(inputs["conv_w_in"]),
        "conv_w_out": f(inputs["conv_w_out"]),
        "conv_dwT": f(inputs["conv_dw_w"])[0].reshape(31, 32, 128).transpose(2, 1, 0).copy(),
        "conv_vecT": np.stack([f(inputs["conv_dw_b"])[0], f(inputs["conv_ln_g"])[0], f(inputs["conv_ln_b"])[0]], 0)
                        .reshape(3, 32, 128).transpose(2, 0, 1).copy(),
        "cmat": cmat, "fcol": fcol,
    }
    maps = [None] * NCORES
    for b in range(NB):
        m = dict(shared)
        m["xT"] = np.ascontiguousarray(x[b].T)
        m["cT"] = np.ascontiguousarray(c[b].reshape(KC, 128).T)
        m["pos"] = np.ascontiguousarray(np.broadcast_to(positions[b].astype(np.int32)[None, :], (128, T)))
        maps[REAL_CORES[b]] = m
    zero = {k: np.zeros_like(v) for k, v in maps[REAL_CORES[0]].items()}
    for i in range(NCORES):
        if maps[i] is None:
            maps[i] = zero
    return maps


_NC_CACHE = {}


def kernel(**inputs):
    nl = 4
    if nl not in _NC_CACHE:
        _NC_CACHE[nl] = Builder(nl).build()
    nc = _NC_CACHE[nl]
    maps = make_in_maps(inputs)
    res = run_bass_kernel_spmd(nc, maps, core_ids=list(range(NCORES)))
    out = np.stack([np.ascontiguousarray(res.results[REAL_CORES[b]]["outT"].T) for b in range(NB)], 0)
    return out.astype(np.float32)
```
